# Optimizing a Trainium2 kernel written in Bass

```python
import math
import jax, jax.numpy as jnp
from jax import lax
import numpy as np

D_MODEL = 1024
BATCH = 8
SEQ = 2048
DEPTH = 1
DEC_BATCH = 128
DEC_SEQ = 4
PAST_LEN = 16384
PAGE_SIZE = 128

D_MIX = D_MODEL
W_A = D_MIX // 2
W_B = D_MIX - W_A
H_A = 8
HD_A = W_A // H_A
CONV_W = 4
LRU_C = 8.0
S5_GROUP = 16
G_B = W_B // S5_GROUP
N_S5 = 64
D_FF = 4 * D_MODEL
ALPHA = (2.0 * DEPTH) ** 0.25
BETA = (8.0 * DEPTH) ** -0.25
LN_EPS = 1e-5

kernel_name = "hymba_rglru_s5_deepnorm_adaln_step"


def layer_norm(x, g, b):
    xf = x.astype(jnp.float32)
    mu = jnp.mean(xf, axis=-1, keepdims=True)
    var = jnp.mean(jnp.square(xf - mu), axis=-1, keepdims=True)
    return (xf - mu) * lax.rsqrt(var + LN_EPS) * g + b


def causal_dwconv(x, buf, w, b):
    xx = jnp.concatenate([buf.astype(x.dtype), x], axis=1)
    y = lax.conv_general_dilated(xx, w[:, None, :].astype(xx.dtype), window_strides=(1,),
                                 padding='VALID', dimension_numbers=('NWC', 'WIO', 'NWC'),
                                 feature_group_count=x.shape[-1])
    return y + b, xx[:, -(CONV_W - 1):]


def linear_scan(a, b, h0):
    b = b.at[:, 0].add(a[:, 0] * h0)
    def comb(l, r):
        return (l[0] * r[0], r[0] * l[1] + r[1])
    _, h = lax.associative_scan(comb, (a, b), axis=1)
    return h


def complex_scan(ar, ai, br, bi, h0r, h0i):
    br = br.at[:, 0].add(ar[:, 0] * h0r - ai[:, 0] * h0i)
    bi = bi.at[:, 0].add(ar[:, 0] * h0i + ai[:, 0] * h0r)
    def comb(l, r):
        a1r, a1i, b1r, b1i = l
        a2r, a2i, b2r, b2i = r
        return (a2r * a1r - a2i * a1i, a2r * a1i + a2i * a1r,
                a2r * b1r - a2i * b1i + b2r, a2r * b1i + a2i * b1r + b2i)
    _, _, hr, hi = lax.associative_scan(comb, (ar, ai, br, bi), axis=1)
    return hr, hi


def rglru(x, h0, w_r, b_r, w_i, b_i, lam, reset_first):
    B, T, _ = x.shape
    xf = x.astype(jnp.float32)
    xh = xf.reshape(B, T, H_A, HD_A)
    r = jax.nn.sigmoid(jnp.einsum('bthi,hij->bthj', xh, w_r) + b_r).reshape(B, T, W_A)
    gi = jax.nn.sigmoid(jnp.einsum('bthi,hij->bthj', xh, w_i) + b_i).reshape(B, T, W_A)
    log_a = -LRU_C * r * jax.nn.softplus(-lam.astype(jnp.float32))
    a = jnp.exp(log_a)
    mult = jnp.sqrt(-jnp.expm1(2.0 * log_a))
    if reset_first:
        mult = mult.at[:, 0].set(1.0)
    return linear_scan(a, mult * gi * xf, h0.astype(jnp.float32))


def s5_ssm(u, s0r, s0i, a_re, a_im, log_dt, b_re, b_im, c_re, c_im, d_skip):
    B, T, _ = u.shape
    uf = u.astype(jnp.float32).reshape(B, T, G_B, S5_GROUP)
    dt = jnp.exp(log_dt.astype(jnp.float32))[:, None]
    ar = a_re.astype(jnp.float32)
    ai = a_im.astype(jnp.float32)
    mag = jnp.exp(dt * ar)
    abr = mag * jnp.cos(dt * ai)
    abi = mag * jnp.sin(dt * ai)
    den = ar * ar + ai * ai
    nr = abr - 1.0
    qr = (nr * ar + abi * ai) / den
    qi = (abi * ar - nr * ai) / den
    bbr = qr[..., None] * b_re - qi[..., None] * b_im
    bbi = qr[..., None] * b_im + qi[..., None] * b_re
    bur = jnp.einsum('btgk,gnk->btgn', uf, bbr)
    bui = jnp.einsum('btgk,gnk->btgn', uf, bbi)
    full = bur.shape
    hr, hi = complex_scan(jnp.broadcast_to(abr, full), jnp.broadcast_to(abi, full),
                          bur, bui, s0r.astype(jnp.float32), s0i.astype(jnp.float32))
    y = (jnp.einsum('btgn,gkn->btgk', hr, c_re) - jnp.einsum('btgn,gkn->btgk', hi, c_im)
         + d_skip * uf)
    return y.reshape(B, T, W_B), hr[:, -1], hi[:, -1]


def trunk_layer(x, c, conv_buf, h0, s0r, s0i, reset_first, p):
    mod = jnp.dot(jax.nn.silu(c), p['ada_w']) + p['ada_b']
    sh1, sc1, g1, sh2, sc2, g2 = [m[:, None, :] for m in jnp.split(mod, 6, axis=-1)]
    h = x * (1.0 + sc1) + sh1
    z = jnp.dot(h, p['in_proj'])
    gate_a = z[..., :W_A]
    xa = z[..., W_A:2 * W_A]
    xb = z[..., 2 * W_A:]
    xa_c, conv_new = causal_dwconv(xa, conv_buf, p['conv_w'], p['conv_b'])
    hseq = rglru(xa_c, h0, p['rg_wr'], p['rg_br'], p['rg_wi'], p['rg_bi'], p['rg_lam'], reset_first)
    y_a = jax.nn.gelu(gate_a) * hseq
    y_s, sr, si = s5_ssm(xb, s0r, s0i, p['s5_a_re'], p['s5_a_im'], p['s5_log_dt'],
                         p['s5_b_re'], p['s5_b_im'], p['s5_c_re'], p['s5_c_im'], p['s5_d'])
    gy = jax.nn.gelu(y_s)
    y_b = gy * jax.nn.sigmoid(jnp.dot(gy, p['glu_w']) + p['glu_b'])
    mix = jnp.dot(jnp.concatenate([y_a, y_b], axis=-1), p['out_proj'])
    x = layer_norm(ALPHA * x + g1 * mix, p['ln1_g'], p['ln1_b'])
    h = x * (1.0 + sc2) + sh2
    f = jnp.dot(jnp.square(jax.nn.relu(jnp.dot(h, p['mlp_w1']) + p['mlp_b1'])), p['mlp_w2']) + p['mlp_b2']
    x = layer_norm(ALPHA * x + g2 * f, p['ln2_g'], p['ln2_b'])
    return x, conv_new, hseq[:, -1], sr, si


def setup_inputs(seed: int = 0) -> dict:
    key = jax.random.key(seed)
    ks = iter(jax.random.split(key, 48))
    nrm = lambda shape, s: s * jax.random.normal(next(ks), shape, jnp.float32)
    L = DEPTH
    u = jax.random.uniform(next(ks), (L, W_A), jnp.float32, minval=0.9, maxval=0.999)
    base = u ** (1.0 / LRU_C)
    rg_lam = jnp.log(base) - jnp.log1p(-base)
    log_dt = jax.random.uniform(next(ks), (L, G_B), jnp.float32,
                                minval=math.log(0.001), maxval=math.log(0.1))
    a_im = math.pi * jnp.arange(N_S5, dtype=jnp.float32)
    return {
        "x_prompt": nrm((BATCH, SEQ, D_MODEL), 1.0),
        "x_sample": nrm((DEC_BATCH, DEC_SEQ, D_MODEL), 1.0),
        "state_conv": nrm((L, DEC_BATCH, CONV_W - 1, W_A), 1.0),
        "state_rglru_h": nrm((L, DEC_BATCH, W_A), 0.5),
        "state_s5_re": nrm((L, DEC_BATCH, G_B, N_S5), 0.3),
        "state_s5_im": nrm((L, DEC_BATCH, G_B, N_S5), 0.3),
        "c_prompt": nrm((BATCH, D_MODEL), 1.0),
        "c_sample": nrm((DEC_BATCH, D_MODEL), 1.0),
        "ada_w": nrm((L, D_MODEL, 6 * D_MODEL), 0.5 * D_MODEL ** -0.5),
        "ada_b": nrm((L, 6 * D_MODEL), 0.02),
        "in_proj": nrm((L, D_MODEL, 2 * W_A + W_B), D_MODEL ** -0.5),
        "conv_w": nrm((L, CONV_W, W_A), CONV_W ** -0.5),
        "conv_b": nrm((L, W_A), 0.02),
        "rg_wr": nrm((L, H_A, HD_A, HD_A), HD_A ** -0.5),
        "rg_br": nrm((L, H_A, HD_A), 0.02),
        "rg_wi": nrm((L, H_A, HD_A, HD_A), HD_A ** -0.5),
        "rg_bi": nrm((L, H_A, HD_A), 0.02),
        "rg_lam": rg_lam,
        "s5_a_re": -0.5 + nrm((L, G_B, N_S5), 0.01),
        "s5_a_im": a_im + nrm((L, G_B, N_S5), 0.01),
        "s5_log_dt": log_dt,
        "s5_b_re": nrm((L, G_B, N_S5, S5_GROUP), (2.0 * S5_GROUP) ** -0.5),
        "s5_b_im": nrm((L, G_B, N_S5, S5_GROUP), (2.0 * S5_GROUP) ** -0.5),
        "s5_c_re": nrm((L, G_B, S5_GROUP, N_S5), (2.0 * N_S5) ** -0.5),
        "s5_c_im": nrm((L, G_B, S5_GROUP, N_S5), (2.0 * N_S5) ** -0.5),
        "s5_d": nrm((L, G_B, S5_GROUP), 1.0),
        "glu_w": nrm((L, W_B, W_B), W_B ** -0.5),
        "glu_b": nrm((L, W_B), 0.02),
        "out_proj": nrm((L, D_MIX, D_MODEL), BETA * D_MIX ** -0.5),
        "ln1_g": 1.0 + nrm((L, D_MODEL), 0.02),
        "ln1_b": nrm((L, D_MODEL), 0.02),
        "mlp_w1": nrm((L, D_MODEL, D_FF), D_MODEL ** -0.5),
        "mlp_b1": nrm((L, D_FF), 0.02),
        "mlp_w2": nrm((L, D_FF, D_MODEL), BETA * D_FF ** -0.5),
        "mlp_b2": nrm((L, D_MODEL), 0.02),
        "ln2_g": 1.0 + nrm((L, D_MODEL), 0.02),
        "ln2_b": nrm((L, D_MODEL), 0.02),
    }


def reference(x_prompt, x_sample, state_conv, state_rglru_h, state_s5_re, state_s5_im,
              c_prompt, c_sample, ada_w, ada_b, in_proj, conv_w, conv_b, rg_wr, rg_br, rg_wi,
              rg_bi, rg_lam, s5_a_re, s5_a_im, s5_log_dt, s5_b_re, s5_b_im, s5_c_re, s5_c_im,
              s5_d, glu_w, glu_b, out_proj, ln1_g, ln1_b, mlp_w1, mlp_b1, mlp_w2, mlp_b2,
              ln2_g, ln2_b):
    xp = x_prompt
    xs = x_sample
    conv_p, h_p, sre_p, sim_p = [], [], [], []
    conv_s, h_s, sre_s, sim_s = [], [], [], []
    for l in range(DEPTH):
        p = dict(ada_w=ada_w[l], ada_b=ada_b[l], in_proj=in_proj[l], conv_w=conv_w[l],
                 conv_b=conv_b[l], rg_wr=rg_wr[l], rg_br=rg_br[l], rg_wi=rg_wi[l], rg_bi=rg_bi[l],
                 rg_lam=rg_lam[l], s5_a_re=s5_a_re[l], s5_a_im=s5_a_im[l], s5_log_dt=s5_log_dt[l],
                 s5_b_re=s5_b_re[l], s5_b_im=s5_b_im[l], s5_c_re=s5_c_re[l], s5_c_im=s5_c_im[l],
                 s5_d=s5_d[l], glu_w=glu_w[l], glu_b=glu_b[l], out_proj=out_proj[l],
                 ln1_g=ln1_g[l], ln1_b=ln1_b[l], mlp_w1=mlp_w1[l], mlp_b1=mlp_b1[l],
                 mlp_w2=mlp_w2[l], mlp_b2=mlp_b2[l], ln2_g=ln2_g[l], ln2_b=ln2_b[l])
        zc = jnp.zeros((xp.shape[0], CONV_W - 1, W_A), xp.dtype)
        zh = jnp.zeros((xp.shape[0], W_A), jnp.float32)
        zs = jnp.zeros((xp.shape[0], G_B, N_S5), jnp.float32)
        xp, cb, hl, sr, si = trunk_layer(xp, c_prompt, zc, zh, zs, zs, True, p)
        conv_p.append(cb); h_p.append(hl); sre_p.append(sr); sim_p.append(si)
        xs, cb, hl, sr, si = trunk_layer(xs, c_sample, state_conv[l], state_rglru_h[l],
                                         state_s5_re[l], state_s5_im[l], False, p)
        conv_s.append(cb); h_s.append(hl); sre_s.append(sr); sim_s.append(si)
    return (xp, xs, jnp.stack(conv_p), jnp.stack(h_p), jnp.stack(sre_p), jnp.stack(sim_p),
            jnp.stack(conv_s), jnp.stack(h_s), jnp.stack(sre_s), jnp.stack(sim_s))
```

```python
import numpy as np
from contextlib import ExitStack
import concourse.bass as bass
import concourse.mybir as mybir
from concourse.bass_utils import run_bass_kernel_spmd

F32 = mybir.dt.float32
BF16 = mybir.dt.bfloat16
AF = mybir.ActivationFunctionType
ALU = mybir.AluOpType

ALPHA = 2.0 ** 0.25
LN_EPS = 1e-5
NT = 512
NPT = 2048 // NT
ST = 1024


class Buf:
    __slots__ = ("name", "w", "r", "sem", "semv", "excl")

    def __init__(self, name):
        self.name = name
        self.excl = False
        self.w = None
        self.r = []
        self.sem = None
        self.semv = 0


class Sched:
    ENG = ("pe", "act", "dve", "pool", "sp")

    def __init__(self, nc, stack):
        self.nc = nc
        self.stack = stack
        self.q = {e: [] for e in self.ENG}
        self.cnt = {e: 0 for e in self.ENG}
        self.esem = {e: stack.enter_context(nc.semaphore("s_" + e)) for e in self.ENG}
        self.waited = {e: {} for e in self.ENG}
        self.dmabufs = []
        self.cap = None

    def capture(self, fn):
        self.cap = []
        fn()
        c = self.cap
        self.cap = None
        return c

    def replay(self, items):
        for it in items:
            if it[0] == "op":
                self.op(*it[1:])
            else:
                self.dma(it[1], it[2], it[3], reads=it[4], writes=it[5], slow=it[6])

    def interleave(self, A, B, frac=1.0):
        def rw(items):
            r, w = set(), set()
            for it in items:
                rd, wr = (it[3], it[4]) if it[0] == "op" else (it[4], it[5])
                r.update(id(x) for x in rd)
                w.update(id(x) for x in wr)
            return r, w
        rA, wA = rw(A)
        rB, wB = rw(B)
        names = {}
        for it in list(A) + list(B):
            rd, wr = (it[3], it[4]) if it[0] == "op" else (it[4], it[5])
            for x in list(rd) + list(wr):
                names[id(x)] = x.name
        bad = (wA & (rB | wB)) | (wB & rA)
        assert not bad, "interleaved streams share written buffers: %s" % sorted(names[i] for i in bad)
        out = []
        nA, nB = len(A), len(B)
        nAe = max(1, int(nA * frac))
        j = 0
        for i, a in enumerate(A):
            out.append(a)
            while j < nB and (j + 1) * nAe <= (i + 1) * nB:
                out.append(B[j])
                j += 1
        out.extend(B[j:])
        self.replay(out)

    def _deps(self, eng, reads, writes):
        deps = {}

        def add(d):
            key, val = d
            if deps.get(key, 0) < val:
                deps[key] = val

        for b in reads:
            if b.w is not None:
                add(b.w)
            if b.excl:
                for d in b.r:
                    if d[0] != eng:
                        add(d)
        for b in writes:
            if b.w is not None:
                add(b.w)
            for d in b.r:
                add(d)
        if eng == "pe":
            deps.pop("pe", None)
        out = []
        for key, val in deps.items():
            if self.waited[eng].get(key, 0) >= val:
                continue
            self.waited[eng][key] = val
            out.append((key, val))
        return out

    def op(self, eng, fn, reads=(), writes=()):
        if self.cap is not None:
            self.cap.append(("op", eng, fn, tuple(reads), tuple(writes)))
            return
        waits = self._deps(eng, reads, writes)
        self.cnt[eng] += 1
        c = self.cnt[eng]
        self.q[eng].append((waits, fn, None))
        for b in writes:
            b.w = (eng, c)
            b.r = []
        for b in reads:
            b.r.append((eng, c))

    def dma(self, q, out_ap, in_ap, reads=(), writes=(), slow=False):
        if self.cap is not None:
            self.cap.append(("dma", q, out_ap, in_ap, tuple(reads), tuple(writes), slow))
            return
        waits = self._deps(q, reads, writes)
        tb = writes[0] if writes else reads[0]
        if tb.sem is None:
            tb.sem = self.stack.enter_context(self.nc.semaphore("d_" + tb.name))
            self.dmabufs.append(tb)
        tb.semv += 16

        def fn(eng, out_ap=out_ap, in_ap=in_ap, slow=slow):
            if slow:
                return eng.dma_start(out=out_ap, in_=in_ap, allow_slow_non_contiguous=True)
            return eng.dma_start(out=out_ap, in_=in_ap)

        self.q[q].append((waits, fn, tb))
        for b in writes:
            b.w = (tb, tb.semv)
            b.r = []
        for b in reads:
            b.r.append((tb, tb.semv))

    def barrier(self):
        snap = {e: self.cnt[e] for e in self.ENG}
        dsnap = [(b, b.semv) for b in self.dmabufs]
        for e in self.ENG:
            waits = []
            for f in self.ENG:
                if f != e and snap[f] > self.waited[e].get(f, 0):
                    waits.append((f, snap[f]))
                    self.waited[e][f] = snap[f]
            for b, v in dsnap:
                if v > self.waited[e].get(b, 0):
                    waits.append((b, v))
                    self.waited[e][b] = v
            self.q[e].append((waits, None, None))

    def emit(self, block):
        sched = self

        def run(engname, eng):
            for waits, fn, tb in sched.q[engname]:
                for key, val in waits:
                    sem = sched.esem[key] if isinstance(key, str) else key.sem
                    eng.wait_ge(sem, val)
                if fn is None:
                    continue
                ins = fn(eng)
                if tb is None:
                    ins.then_inc(sched.esem[engname], 1)
                else:
                    ins.then_inc(tb.sem, 16)
            if engname == "sp":
                for b in sched.dmabufs:
                    eng.wait_ge(b.sem, b.semv)
                for e in ("pe", "act", "dve", "pool"):
                    if sched.cnt[e]:
                        eng.wait_ge(sched.esem[e], sched.cnt[e])

        @block.tensor
        def _(e):
            run("pe", e)

        @block.scalar
        def _(e):
            run("act", e)

        @block.vector
        def _(e):
            run("dve", e)

        @block.gpsimd
        def _(e):
            run("pool", e)

        @block.sync
        def _(e):
            run("sp", e)


IN_SPECS = [
    ("xp", [2048, 1024]), ("xs", [64, 1024]), ("cvec", [17, 1024]),
    ("conv0", [16, 3, 512]), ("h0", [16, 512]), ("s5r0", [16, 2048]), ("s5i0", [16, 2048]),
    ("ada_w", [1024, 6144]), ("ada_b", [48, 128]), ("in_proj", [1024, 1536]),
    ("conv_w", [16, 128]), ("conv_b", [4, 128]), ("rg_wr", [8, 64, 64]), ("rg_br", [4, 128]),
    ("rg_wi", [8, 64, 64]), ("rg_bi", [4, 128]), ("rg_lam", [4, 128]),
    ("s5_a_re", [32, 64]), ("s5_a_im", [32, 64]), ("s5_log_dt", [1, 32]),
    ("s5_b_re", [32, 64, 16]), ("s5_b_im", [32, 64, 16]), ("s5_c_re", [512, 64]), ("s5_c_im", [512, 64]),
    ("s5_d", [32, 16]), ("glu_w", [512, 512]), ("glu_b", [4, 128]), ("out_proj", [1024, 1024]),
    ("ln1_g", [8, 128]), ("ln1_b", [8, 128]), ("mlp_w1", [1024, 4096]), ("mlp_b1", [32, 128]),
    ("mlp_w2", [4096, 1024]), ("mlp_b2", [8, 128]), ("ln2_g", [8, 128]), ("ln2_b", [8, 128]),
    ("ident", [128, 128]), ("trii", [128, 128]), ("maskt", [128, 128]), ("diagsel", [128, 128]),
]
OUT_SPECS = [
    ("yp", [2048, 1024]), ("ys", [64, 1024]), ("convp", [12, 128]), ("hp", [4, 128]),
    ("s5rp", [32, 64]), ("s5ip", [32, 64]), ("convs", [48, 512]), ("hs", [16, 512]),
    ("s5rs", [16, 2048]), ("s5is", [16, 2048]),
]

PCOLS = {}
_c = 0
for _n, _k in [("ada_b", 48), ("ln1_g", 8), ("ln1_b", 8), ("ln2_g", 8), ("ln2_b", 8), ("mlp_b2", 8),
               ("mlp_b1", 32), ("conv_w", 16), ("conv_b", 4), ("rg_br", 4), ("rg_bi", 4), ("rg_lam", 4),
               ("glu_b", 4)]:
    PCOLS[_n] = (_c, _k)
    _c += _k
NPCOL = 160

CFG = {"sample": True, "nsuper": 2}


def build_nc():
    nc = bass.Bass("TRN2", target_bir_lowering=False)
    D = {}
    for n, s in IN_SPECS:
        D[n] = nc.dram_tensor(n, list(s), F32, kind="ExternalInput").ap()
    for n, s in OUT_SPECS:
        D[n] = nc.dram_tensor(n, list(s), F32, kind="ExternalOutput").ap()

    with ExitStack() as st:
        S = Sched(nc, st)

        class TT:
            def __init__(self, name, shape, dt=F32, psum=False):
                if psum:
                    self.t = st.enter_context(nc.psum_tensor(name, shape, dt))
                else:
                    self.t = st.enter_context(nc.sbuf_tensor(name, shape, dt))
                self.b = Buf(name)

            def __getitem__(self, k):
                return self.t[k]

        class VW:
            def __init__(self, name, ap):
                self.b = Buf(name)
                self.ap_ = ap

            def __getitem__(self, k):
                return self.ap_[k]

        def carve(arena, name, off, shape, dt=F32, parts=128):
            n = 1
            for d in shape[1:]:
                n *= d
            nf = n if dt == F32 else (n + 1) // 2
            ap = arena.t[0:parts, off:off + nf]
            if dt != F32:
                ap = ap.bitcast(dt)[:, 0:n]
            if len(shape) > 2:
                names = " ".join("d%d" % i for i in range(len(shape) - 1))
                kw = {"d%d" % i: shape[i + 1] for i in range(len(shape) - 1)}
                ap = ap.rearrange("p (%s) -> p %s" % (names, names), **kw)
            return VW(name, ap)

        PB = [TT("pb%d" % i, [128, 512], F32, psum=True) for i in range(8)]
        for i in range(8):
            PB[i].b.excl = True
        PBb = [PB[i].t[:].bitcast(BF16) for i in range(8)]
        bank_ctr = [0, 0]
        cur_pool = [None]

        def nb():
            if cur_pool[0] is None:
                i = bank_ctr[0] % 8
                bank_ctr[0] += 1
                return i
            if cur_pool[0] == 0:
                i = bank_ctr[0] % 5
                bank_ctr[0] += 1
                return i
            i = 5 + bank_ctr[1] % 3
            bank_ctr[1] += 1
            return i

        ident = TT("identt", [128, 128]); identb = TT("identb", [128, 128], BF16)
        triIb = TT("triIb", [128, 128], BF16); maskT = TT("maskTt", [128, 128]); diagsel = TT("diagselt", [128, 128])
        onesb = TT("onesb", [128, 128], BF16)
        PT = TT("PT", [128, NPCOL])
        modT = TT("modT", [128, 48, 17])
        GG2 = TT("GG2", [128, 8, 17]); BB2 = TT("BB2", [128, 8, 17])
        AG1 = TT("AG1", [128, 8]); AB1 = TT("AB1", [128, 8, 17])
        gluW = TT("gluW", [128, 4, 512], BF16)
        rgW = TT("rgW", [128, 4, 2, 128], BF16)
        cp1 = TT("cp1", [128, 4]); cp2 = TT("cp2", [128, 4])
        hstate = TT("hstate", [128, 4]); xhist = TT("xhist", [128, 4, 3])
        carryS = TT("carryS", [128, 32], BF16)
        P4 = [TT("P4_%d" % i, [128, 32]) for i in range(4)]
        P8 = [TT("P8_%d" % i, [128, 32]) for i in range(4)]
        hT = TT("hT", [128, 8, ST], BF16); axT = TT("axT", [128, 8, ST]); gyT = TT("gyT", [128, 4, ST], BF16)
        f1a = TT("f1a", [128, 2, NT], BF16)
        NWB = 4
        wblk = [TT("wblk%d" % i, [128, 8, 256], BF16) for i in range(NWB)]
        scT = TT("scT", [128, 8, 17], BF16)
        h0T = TT("h0T", [128, 4, 16])

        FA = TT("FA", [128, 20432])
        BA = TT("BA", [128, 6144])

        def P(name, j=0):
            c0, k = PCOLS[name]
            return PT[:, c0 + j:c0 + j + 1]

        t1 = carve(FA, "t1", 0, [128, 4, NT]); t2 = carve(FA, "t2", 2048, [128, 4, NT])
        t3 = carve(FA, "t3", 4096, [128, 4, NT]); t4 = carve(FA, "t4", 6144, [128, 4, NT])
        r1 = carve(FA, "r1", 8192, [128, 8, NT]); gT = carve(FA, "gT", 12288, [128, 4, NT])
        xaT = carve(FA, "xaT", 14336, [128, 4, 3 + NT]); xaS = carve(FA, "xaS", 16400, [128, 4, 16, 7])
        mean = carve(FA, "mean", 16848, [128, NT]); rstd = carve(FA, "rstd", 17360, [128, NT])
        yout = carve(FA, "yout", 18896, [128, 1024])
        tmpx = carve(FA, "tmpx", 16400, [128, 64]); tmpx.b = xaS.b
        deferred_alias = []

        class _One:
            def __init__(self, b):
                self.b = b
        for v_ in (t1, t2, t3, t4, gT, xaT):
            v_.bs = getattr(v_, "bs", None) or [Buf(v_.b.name + "_%d" % i) for i in range(4)]
            v_.c = [_One(b) for b in v_.bs]
        r1a = carve(FA, "r1a", 4096, [128, 8, NT]); r1a.bs = t3.bs + t4.bs; r1a.c = t3.c + t4.c
        r1.bs = [Buf("r1_%d" % i) for i in range(8)]; r1.c = [_One(b) for b in r1.bs]
        rb2 = carve(FA, "rb2", 0, [128, 8, NT], BF16); rb2.bs = t1.bs; rb2.c = [t1.c[i // 2] for i in range(8)]
        rsq2 = carve(FA, "rsq2", 2048, [128, 8, NT], BF16); rsq2.bs = t2.bs; rsq2.c = [t2.c[i // 2] for i in range(8)]
        mean2 = carve(FA, "mean2", 12288, [128, NT]); mean2.bs = gT.bs
        deferred_alias.append(lambda: (setattr(ycat, "bs", gT.bs), setattr(ycat, "c", [gT.c[i // 2] for i in range(8)])))
        rstd2 = carve(FA, "rstd2", 12800, [128, NT]); rstd2.bs = gT.bs
        xin = TT("xin", [128, 1024])
        csb = carve(FA, "csb", 10240, [17, 1024], parts=17)
        msb = carve(FA, "msb", 8704, [17, 256], parts=17)
        prow = [carve(FA, "prow%d" % i, 8960 + 128 * i, [80, 128], parts=80) for i in range(2)]
        sm = carve(FA, "sm", 19920, [48, 512], parts=48)
        scr = nc.dram_tensor("scr_tab", [4, 128, 2048], BF16).ap()
        scrB = [Buf("scr%d" % i) for i in range(4)]
        TSt = [carve(BA, "TSt%d" % i, 1024 * i, [128, 32, 64], BF16) for i in range(4)]
        TFr, TFi, TIr, TIi = [carve(FA, "TAB%d" % i, 7184 + 1024 * i, [128, 32, 64], BF16) for i in range(4)]
        Wmov = carve(FA, "Wmov", 13328, [128, 32, 256], BF16)
        WA = carve(FA, "WA", 17424, [128, 32, 128], BF16)
        scrW = nc.dram_tensor("scr_w", [128, 6144], F32).ap()
        scrWB = Buf("scrW")
        f1b = carve(BA, "f1b", 0, [128, 16, NT], BF16); rb = carve(BA, "rb", 0, [128, 8, NT], BF16)
        rsq = carve(BA, "rsq", 2048, [128, 8, NT], BF16); rb.b = f1b.b; rsq.b = f1b.b
        h2T = carve(BA, "h2T", 4096, [128, 8, NT], BF16)
        ycat = carve(FA, "ycat", 12288, [128, 8, NT], BF16)
        xcb = carve(FA, "xcb", 17872, [128, 4, NT], BF16)
        xcb.bs = [Buf("xcb_%d" % i) for i in range(4)]
        xcb.c = [_One(b) for b in xcb.bs]
        f1b.bs = [Buf("f1b_%d" % i) for i in range(16)]; f1b.c = [_One(b) for b in f1b.bs]
        rb.bs = f1b.bs[0:8]; rb.c = f1b.c[0:8]
        rsq.bs = f1b.bs[8:16]; rsq.c = f1b.c[8:16]
        h2T.bs = [Buf("h2T_%d" % i) for i in range(8)]; h2T.c = [_One(b) for b in h2T.bs]
        for fn_ in deferred_alias:
            fn_()
        M1 = carve(BA, "M1", 0, [128, 32, 2, 64], BF16); M2 = carve(BA, "M2", 2048, [128, 32, 2, 64], BF16)
        Zc = carve(BA, "Zc", 4096, [128, 32, 8, 16], BF16); U = carve(FA, "U", 11280, [128, 32, 128], BF16)
        Hcm = carve(FA, "Hcm", 0, [128, 32, 2, 64], BF16); Ysb = carve(FA, "Ysb", 2048, [128, 4, 8, 8, 16], BF16)
        HTx = carve(FA, "HTx", 4096, [128, 32, 129], BF16)
        tmpA = carve(FA, "tmpA", 6160, [128, 4, 2, 64]); tmpB = carve(FA, "tmpB", 6672, [128, 4, 2, 64])
        s0 = carve(FA, "s0", 7184, [16, 32, 2, 64], parts=16)
        SL = [carve(BA, "SL%d" % i, 0 + 1024 * i, [16, 32, 64], BF16, parts=16) for i in range(4)]
        Fs_r = carve(FA, "Fs_r", 0, [128, 32, 128]); Fs_i = carve(FA, "Fs_i", 4096, [128, 32, 128])
        ctA = carve(FA, "ctA", 8192, [128, 32, 64])
        MBN = carve(BA, "MBN", 0, [128, 32, 8, 16], BF16); ENAT = carve(BA, "ENAT", 2048, [128, 32, 8, 16], BF16)
        cT1 = carve(FA, "cT1", 3072, [128, 32, 16]); cT2 = carve(FA, "cT2", 3584, [128, 32, 16])
        pu1 = carve(FA, "pu1", 4096, [128, 32, 16]); pu2 = carve(FA, "pu2", 4608, [128, 32, 16])
        pu3 = carve(FA, "pu3", 5120, [128, 32, 16]); pu4 = carve(FA, "pu4", 5632, [128, 32, 16])
        tmpT = carve(FA, "tmpT", 6144, [128, 4, 128])
        PWr = carve(FA, "PWr", 6656, [128, 9, 32]); PWi = carve(FA, "PWi", 6944, [128, 9, 32])
        NWr = carve(FA, "NWr", 7232, [128, 9, 32]); NWi = carve(FA, "NWi", 7520, [128, 9, 32])
        tb16 = carve(FA, "tb16", 6160, [128, 32, 16])
        sc1 = carve(FA, "sc1", 6160, [16, 4, 2, 64], parts=16); sc1.b = tb16.b
        sc2 = carve(FA, "sc2", 6672, [16, 4, 2, 64], parts=16)

        wctr = [0, 0]

        wblkB = [TT("wblkB%d" % i, [128, 8, 256], BF16) for i in range(2)]

        def next_wblk():
            if cur_pool[0] == 1:
                w = wblkB[wctr[1] % 2]
                wctr[1] += 1
                return w
            w = wblk[wctr[0] % NWB]
            wctr[0] += 1
            return w

        def load_wblock(src2d, w):
            S.dma("pool", w[:], src2d.rearrange("(k p) c -> p k c", p=128), writes=[w.b])

        def _bufs(lst):
            out = []
            for x in lst:
                if hasattr(x, "bs"):
                    out.extend(x.bs)
                else:
                    out.append(x.b)
            return out

        A_ = lambda eng, fn, r, w: S.op(eng, fn, reads=_bufs(r), writes=_bufs(w))

        S.dma("sp", ident[:], D["ident"], writes=[ident.b])
        S.dma("sp", maskT[:], D["maskt"], writes=[maskT.b])
        S.dma("sp", diagsel[:], D["diagsel"], writes=[diagsel.b])
        S.dma("pool", triIb[:], D["trii"], writes=[triIb.b])
        A_("dve", lambda e: e.tensor_copy(out=identb[:], in_=ident[:]), [ident], [identb])
        A_("dve", lambda e: e.memset(onesb[:], 1.0 / 1024.0), [], [onesb])
        for half in range(2):
            lst = []
            for n in PCOLS:
                c0, k = PCOLS[n]
                if (c0 < 80) == (half == 0):
                    lst.append((n, c0 - 80 * half, k))
            if half == 1:
                A_("dve", lambda e: e.memset(prow[1][:], 0.0), [], [prow[1]])
            for n, cc, k in lst:
                S.dma("sp", prow[half][cc:cc + k, :], D[n], writes=[prow[half].b])
            A_("pe", lambda e, half=half: e.transpose(PB[0][:, 0:80], prow[half][:], ident[0:80, 0:80]), [prow[half], ident], [PB[0]])
            A_("dve", lambda e, half=half: e.tensor_copy(out=PT[:, 80 * half:80 * half + 80], in_=PB[0][:, 0:80]), [PB[0]], [PT])
        S.dma("sp", csb[:], D["cvec"], writes=[csb.b])
        for k in range(8):
            A_("pe", lambda e, k=k: e.transpose(PB[1][:, k * 17:(k + 1) * 17], csb[0:17, k * 128:(k + 1) * 128], ident[0:17, 0:17]), [csb, ident], [PB[1]])
        A_("act", lambda e: e.activation(out=scT[:].rearrange("p k r -> p (k r)"), in_=PB[1][:, 0:136], func=AF.Silu), [PB[1]], [scT])
        arS, aiS, dtS, v1, v2, v3, v4, abr, abi, qr, qi, w5, w6 = [carve(FA, n, 8192 + 32 * i_, [128, 32]) for i_, n in enumerate("arS aiS dtS v1 v2 v3 v4 abr abi qr qi w5 w6".split())]
        bSr = carve(FA, "bSr", 0, [128, 32, 16]); bSi = carve(FA, "bSi", 512, [128, 32, 16])
        bbr = carve(FA, "bbr", 1024, [128, 32, 16]); bbi = carve(FA, "bbi", 1536, [128, 32, 16])
        cN = carve(FA, "cN", 2048, [128, 4, 2, 64]); cN2 = carve(FA, "cN2", 2560, [128, 4, 2, 64])
        dK = TT("dK", [128, 32]); hp_ = TT("hp_", [128, 1])
        for hf in range(2):
            ps_ = slice(hf * 64, hf * 64 + 64)
            for gq in range(4):
                S.dma("sp", arS[ps_, gq * 8:(gq + 1) * 8], D["s5_a_re"][gq * 8:(gq + 1) * 8].rearrange("g n -> n g"), writes=[arS.b], slow=True)
                S.dma("sp", aiS[ps_, gq * 8:(gq + 1) * 8], D["s5_a_im"][gq * 8:(gq + 1) * 8].rearrange("g n -> n g"), writes=[aiS.b], slow=True)
            for gq in range(4):
                S.dma("sp", bSr[ps_, gq * 8:(gq + 1) * 8, :], D["s5_b_re"][gq * 8:(gq + 1) * 8].rearrange("g n k -> n g k"), writes=[bSr.b])
                S.dma("sp", bSi[ps_, gq * 8:(gq + 1) * 8, :], D["s5_b_im"][gq * 8:(gq + 1) * 8].rearrange("g n k -> n g k"), writes=[bSi.b])
        for k in range(8):
            S.dma("sp", dK[k * 16:(k + 1) * 16, :], D["s5_d"].rearrange("g i -> i g"), writes=[dK.b], slow=True)
        S.dma("sp", dtS[:], D["s5_log_dt"].broadcast_to([128, 32]), writes=[dtS.b])
        S.dma("sp", cN[:, :, 0, :], D["s5_c_re"].rearrange("(t q) n -> q t n", q=128), writes=[cN.b])
        S.dma("sp", cN[:, :, 1, :], D["s5_c_im"].rearrange("(t q) n -> q t n", q=128), writes=[cN.b])
        S.dma("sp", cN2[:, :, 0, :], D["s5_c_im"].rearrange("(t q) n -> q t n", q=128), writes=[cN2.b])
        S.dma("sp", cN2[:, :, 1, :], D["s5_c_re"].rearrange("(t q) n -> q t n", q=128), writes=[cN2.b])
        TT_ = lambda o, a, b, op: (lambda e: e.tensor_tensor(out=o, in0=a, in1=b, op=op))
        A_("act", lambda e: e.activation(out=dtS[:], in_=dtS[:], func=AF.Exp), [dtS], [dtS])
        A_("dve", TT_(v1[:], dtS[:], arS[:], ALU.mult), [dtS, arS], [v1])
        A_("dve", TT_(v2[:], dtS[:], aiS[:], ALU.mult), [dtS, aiS], [v2])
        A_("act", lambda e: e.activation(out=v1[:], in_=v1[:], func=AF.Exp), [v1], [v1])
        A_("dve", lambda e: e.memset(hp_[:], float(np.pi / 2)), [], [hp_])
        A_("act", lambda e: e.activation(out=v3[:], in_=v2[:], func=AF.Sin, scale=1.0 / 16.0), [v2], [v3])
        A_("act", lambda e: e.activation(out=v4[:], in_=v2[:], func=AF.Sin, scale=-1.0 / 16.0, bias=hp_[:, 0:1]), [v2, hp_], [v4])
        for _ in range(4):
            A_("dve", TT_(w5[:], v3[:], v4[:], ALU.mult), [v3, v4], [w5])
            A_("dve", TT_(w6[:], v3[:], v3[:], ALU.mult), [v3], [w6])
            A_("dve", TT_(v4[:], v4[:], v4[:], ALU.mult), [v4], [v4])
            A_("dve", TT_(v4[:], v4[:], w6[:], ALU.subtract), [v4, w6], [v4])
            A_("dve", lambda e: e.tensor_scalar(out=v3[:], in0=w5[:], scalar1=2.0, scalar2=None, op0=ALU.mult), [w5], [v3])
        A_("dve", TT_(abr[:], v1[:], v4[:], ALU.mult), [v1, v4], [abr])
        A_("dve", TT_(abi[:], v1[:], v3[:], ALU.mult), [v1, v3], [abi])
        A_("dve", TT_(v1[:], arS[:], arS[:], ALU.mult), [arS], [v1])
        A_("dve", TT_(v2[:], aiS[:], aiS[:], ALU.mult), [aiS], [v2])
        A_("dve", TT_(v1[:], v1[:], v2[:], ALU.add), [v1, v2], [v1])
        A_("dve", lambda e: e.reciprocal(out=v1[:], in_=v1[:]), [v1], [v1])
        A_("dve", lambda e: e.tensor_scalar(out=v2[:], in0=abr[:], scalar1=-1.0, scalar2=None, op0=ALU.add), [abr], [v2])
        A_("dve", TT_(v3[:], v2[:], arS[:], ALU.mult), [v2, arS], [v3])
        A_("dve", TT_(v4[:], abi[:], aiS[:], ALU.mult), [abi, aiS], [v4])
        A_("dve", TT_(v3[:], v3[:], v4[:], ALU.add), [v3, v4], [v3])
        A_("dve", TT_(qr[:], v3[:], v1[:], ALU.mult), [v3, v1], [qr])
        A_("dve", TT_(v3[:], abi[:], arS[:], ALU.mult), [abi, arS], [v3])
        A_("dve", TT_(v4[:], v2[:], aiS[:], ALU.mult), [v2, aiS], [v4])
        A_("dve", TT_(v3[:], v3[:], v4[:], ALU.subtract), [v3, v4], [v3])
        A_("dve", TT_(qi[:], v3[:], v1[:], ALU.mult), [v3, v1], [qi])

        def cmul(or_, oi_, ar, ai, br, bi, tmp, rd, wr_r, wr_i, tmpbuf, eng="dve"):
            S.op(eng, lambda e: e.tensor_tensor(out=or_(), in0=ar(), in1=br(), op=ALU.mult), reads=rd, writes=[wr_r])
            S.op(eng, lambda e: e.tensor_tensor(out=tmp(), in0=ai(), in1=bi(), op=ALU.mult), reads=rd, writes=[tmpbuf])
            S.op(eng, lambda e: e.tensor_tensor(out=or_(), in0=or_(), in1=tmp(), op=ALU.subtract), reads=[wr_r, tmpbuf], writes=[wr_r])
            S.op(eng, lambda e: e.tensor_tensor(out=oi_(), in0=ar(), in1=bi(), op=ALU.mult), reads=rd, writes=[wr_i])
            S.op(eng, lambda e: e.tensor_tensor(out=tmp(), in0=ai(), in1=br(), op=ALU.mult), reads=rd + [wr_r], writes=[tmpbuf])
            S.op(eng, lambda e: e.tensor_tensor(out=oi_(), in0=oi_(), in1=tmp(), op=ALU.add), reads=[wr_i, tmpbuf], writes=[wr_i])

        b3 = lambda ap: ap.unsqueeze(2).broadcast_to([128, 32, 16])
        cmul(lambda: bbr[:], lambda: bbi[:], lambda: b3(qr[:]), lambda: b3(qi[:]), lambda: bSr[:], lambda: bSi[:], lambda: pu1[:],
             [qr.b, qi.b, bSr.b, bSi.b], bbr.b, bbi.b, pu1.b)
        A_("dve", lambda e: e.tensor_tensor(out=v1[:], in0=abr[:], in1=abr[:], op=ALU.mult), [abr], [v1])
        A_("dve", lambda e: e.tensor_tensor(out=v2[:], in0=abi[:], in1=abi[:], op=ALU.mult), [abi], [v2])
        A_("dve", lambda e: e.tensor_tensor(out=v1[:], in0=v1[:], in1=v2[:], op=ALU.add), [v1, v2], [v1])
        A_("dve", lambda e: e.reciprocal(out=v1[:], in_=v1[:]), [v1], [v1])
        tmpw = lambda m: pu1[:].rearrange("p g k -> p (g k)")[:, 0:m * 32].rearrange("p (a b) -> p a b", b=32)
        A_("dve", lambda e: e.memset(PWr[:, 0, :], 1.0), [], [PWr])
        A_("dve", lambda e: e.memset(PWi[:, 0, :], 0.0), [], [PWi])
        A_("dve", lambda e: e.tensor_copy(out=PWr[:, 1, :], in_=abr[:]), [abr], [PWr])
        A_("dve", lambda e: e.tensor_copy(out=PWi[:, 1, :], in_=abi[:]), [abi], [PWi])
        for m in (1, 2, 4):
            bcm = lambda ap, m=m: ap.unsqueeze(1).broadcast_to([128, m, 32])
            cmul(lambda m=m: PWr[:, 1 + m:1 + 2 * m, :], lambda m=m: PWi[:, 1 + m:1 + 2 * m, :],
                 lambda m=m: PWr[:, 1:1 + m, :], lambda m=m: PWi[:, 1:1 + m, :],
                 lambda m=m, bcm=bcm: bcm(PWr[:, m, :]), lambda m=m, bcm=bcm: bcm(PWi[:, m, :]),
                 lambda m=m: tmpw(m), [PWr.b, PWi.b], PWr.b, PWi.b, pu1.b)
        A_("dve", lambda e: e.memset(NWr[:, 8, :], 1.0), [], [NWr])
        A_("dve", lambda e: e.memset(NWi[:, 8, :], 0.0), [], [NWi])
        A_("dve", lambda e: e.tensor_tensor(out=NWr[:, 7, :], in0=abr[:], in1=v1[:], op=ALU.mult), [abr, v1], [NWr])
        A_("dve", lambda e: e.scalar_tensor_tensor(out=NWi[:, 7, :], in0=abi[:], scalar=-1.0, in1=v1[:], op0=ALU.mult, op1=ALU.mult), [abi, v1], [NWi])
        for m in (1, 2, 4):
            bcm = lambda ap, m=m: ap.unsqueeze(1).broadcast_to([128, m, 32])
            cmul(lambda m=m: NWr[:, 8 - 2 * m:8 - m, :], lambda m=m: NWi[:, 8 - 2 * m:8 - m, :],
                 lambda m=m: NWr[:, 8 - m:8, :], lambda m=m: NWi[:, 8 - m:8, :],
                 lambda m=m, bcm=bcm: bcm(NWr[:, 8 - m, :]), lambda m=m, bcm=bcm: bcm(NWi[:, 8 - m, :]),
                 lambda m=m: tmpw(m), [NWr.b, NWi.b], NWr.b, NWi.b, pu1.b)
        for (src, dstc) in ((cN, cT1), (cN2, cT2)):
            bk = nb()
            for t in range(4):
                A_("pe", lambda e, src=src, t=t, bk=bk: e.transpose(PB[bk][:, t * 128:(t + 1) * 128], src[:, t, :, :].rearrange("p r n -> p (r n)"), ident[:]), [src, ident], [PB[bk]])
            A_("dve", lambda e, dstc=dstc, bk=bk: e.tensor_copy(out=dstc[:].rearrange("p g k -> p (g k)"), in_=PB[bk][:, :]), [PB[bk]], [dstc])
        for nbk in range(24):
            w = next_wblk()
            load_wblock(D["ada_w"][:, nbk * 256:(nbk + 1) * 256], w)
            bk = nb()
            for k in range(8):
                A_("pe", lambda e, k=k, w=w, bk=bk: e.matmul(PB[bk][0:17, 0:256], lhsT=scT[:, k, :], rhs=w[:, k, :], start=(k == 0), stop=(k == 7)), [scT, w], [PB[bk]])
            A_("act", lambda e, bk=bk: e.copy(out=msb[:], in_=PB[bk][0:17, 0:256]), [PB[bk]], [msb])
            bk2 = nb()
            for q in range(2):
                A_("pe", lambda e, q=q, bk2=bk2: e.transpose(PB[bk2][:, q * 17:(q + 1) * 17], msb[0:17, q * 128:(q + 1) * 128], ident[0:17, 0:17]), [msb, ident], [PB[bk2]])
            A_("act", lambda e, nbk=nbk, bk2=bk2: e.activation(out=modT[:, nbk * 2:(nbk + 1) * 2, :].rearrange("p k r -> p (k r)"), in_=PB[bk2][:, 0:34], func=AF.Copy), [PB[bk2]], [modT])
        vA = carve(FA, "vA", 4096, [128, 8, 8, 16]); vB = carve(FA, "vB", 5120, [128, 8, 8, 16]); vC = carve(FA, "vC", 9216, [128, 8, 8, 16])
        vA.b = pu1.b
        WA4 = WA[:].rearrange("p g (j k) -> p g j k", k=16)
        for gp in range(4):
            gs = slice(gp * 8, gp * 8 + 8)
            bk_ = lambda X, lo, gs=gs: X[:, lo:lo + 8, gs].rearrange("p k g -> p g k").unsqueeze(3).broadcast_to([128, 8, 8, 16])
            bb_ = lambda X, gs=gs: X[:, gs, :].unsqueeze(2).broadcast_to([128, 8, 8, 16])
            for (dst, Xr, Xi) in ((MBN, PWr, PWi), (ENAT, NWr, NWi)):
                A_("dve", lambda e, Xr=Xr, bk_=bk_, bb_=bb_: e.tensor_tensor(out=vA[:], in0=bb_(bbr), in1=bk_(Xr, 0), op=ALU.mult), [bbr, Xr], [vA])
                A_("dve", lambda e, Xi=Xi, bk_=bk_, bb_=bb_: e.tensor_tensor(out=vB[:], in0=bb_(bbi), in1=bk_(Xi, 0), op=ALU.mult), [bbi, Xi], [vB])
                A_("dve", lambda e: e.tensor_tensor(out=vA[:], in0=vA[:], in1=vB[:], op=ALU.subtract), [vA, vB], [vA])
                A_("dve", lambda e, Xr=Xr, bk_=bk_, bb_=bb_: e.tensor_tensor(out=vC[:], in0=bb_(bbi), in1=bk_(Xr, 0), op=ALU.mult), [bbi, Xr], [vC])
                A_("dve", lambda e, Xi=Xi, bk_=bk_, bb_=bb_: e.tensor_tensor(out=vB[:], in0=bb_(bbr), in1=bk_(Xi, 0), op=ALU.mult), [bbr, Xi], [vB])
                A_("dve", lambda e: e.tensor_tensor(out=vC[:], in0=vC[:], in1=vB[:], op=ALU.add), [vC, vB], [vC])
                A_("dve", lambda e, dst=dst, gs=gs: e.tensor_copy(out=dst[0:64, gs], in_=vA[0:64]), [vA], [dst])
                A_("dve", lambda e, dst=dst, gs=gs: e.tensor_copy(out=dst[64:128, gs], in_=vC[64:128]), [vC], [dst])
            A_("dve", lambda e, bk_=bk_, bb_=bb_: e.tensor_tensor(out=vA[:], in0=bb_(cT1), in1=bk_(PWr, 1), op=ALU.mult), [cT1, PWr], [vA])
            A_("dve", lambda e, bk_=bk_, bb_=bb_: e.tensor_tensor(out=vB[:], in0=bb_(cT2), in1=bk_(PWi, 1), op=ALU.mult), [cT2, PWi], [vB])
            A_("dve", lambda e, gs=gs: e.tensor_tensor(out=WA4[0:64, gs], in0=vA[0:64], in1=vB[0:64], op=ALU.subtract), [vA, vB], [WA])
            A_("dve", lambda e, gs=gs: e.scalar_tensor_tensor(out=WA4[64:128, gs], in0=vA[64:128], scalar=-1.0, in1=vB[64:128], op0=ALU.mult, op1=ALU.subtract), [vA, vB], [WA])
        for g0 in range(0, 32, 4):
            bk = nb()
            for gl in range(4):
                g = g0 + gl
                A_("pe", lambda e, g=g, gl=gl, bk=bk: e.matmul(PB[bk][:, gl * 128:(gl + 1) * 128], lhsT=ENAT[:, g, :, :].rearrange("p k i -> p (k i)"), rhs=WA[:, g, :], start=True, stop=True), [ENAT, WA], [PB[bk]])
            A_("dve", lambda e, bk=bk: e.tensor_tensor(out=tmpT[:], in0=PB[bk][:, :].rearrange("p (g c) -> p g c", g=4), in1=maskT[:].unsqueeze(1).broadcast_to([128, 4, 128]), op=ALU.mult), [PB[bk], maskT], [tmpT])
            for gl in range(4):
                g = g0 + gl
                A_("dve", lambda e, g=g, gl=gl: e.scalar_tensor_tensor(out=Wmov[:, g, 0:128], in0=diagsel[:], scalar=dK[:, g:g + 1], in1=tmpT[:, gl, :], op0=ALU.mult, op1=ALU.add), [diagsel, dK, tmpT], [Wmov])
        for g0 in range(0, 32, 8):
            bk = nb()
            for gl in range(8):
                g = g0 + gl
                A_("pe", lambda e, g=g, gl=gl, bk=bk: e.transpose(PBb[bk][:, gl * 128:(gl + 1) * 128], MBN[:, g, :, :].rearrange("p k i -> p (k i)"), identb[:]), [MBN, identb], [PB[bk]])
            A_("act", lambda e, g0=g0, bk=bk: e.copy(out=Wmov[:, g0:g0 + 8, 128:256], in_=PBb[bk][:, :].rearrange("p (g c) -> p g c", c=128)), [PB[bk]], [Wmov])
        S.dma("sp", scrW, FA[:, 13328:19472], reads=[Wmov.b, WA.b], writes=[scrWB])
        for i_, (X, e_) in enumerate(((PWr, 4), (PWi, 4), (NWr, 4), (NWi, 4))):
            A_("dve", lambda e, X=X, e_=e_, i_=i_: e.tensor_copy(out=P4[i_][:], in_=X[:, e_, :]), [X], [P4[i_]])
        for i_, (X, e_) in enumerate(((PWr, 8), (PWi, 8), (NWr, 0), (NWi, 0))):
            A_("dve", lambda e, X=X, e_=e_, i_=i_: e.tensor_copy(out=P8[i_][:], in_=X[:, e_, :]), [X], [P8[i_]])
        c0, _ = PCOLS["ada_b"]
        A_("dve", lambda e: e.tensor_tensor(out=modT[:], in0=modT[:], in1=PT[:, c0:c0 + 48].unsqueeze(2).broadcast_to([128, 48, 17]), op=ALU.add), [modT, PT], [modT])
        for j in (1, 4):
            A_("dve", lambda e, j=j: e.tensor_scalar(out=modT[:, j * 8:(j + 1) * 8, :], in0=modT[:, j * 8:(j + 1) * 8, :], scalar1=1.0, scalar2=None, op0=ALU.add), [modT], [modT])
        SH1, A1, G1, SH2, A2, G2 = [lambda ft, j=j: modT[:, j * 8 + ft, :] for j in range(6)]
        g1c, _ = PCOLS["ln1_g"]; b1c, _ = PCOLS["ln1_b"]; mb2c, _ = PCOLS["mlp_b2"]
        bc8 = lambda c: PT[:, c:c + 8].unsqueeze(2).broadcast_to([128, 8, 17])
        A_("dve", lambda e: e.tensor_tensor(out=GG2[:], in0=modT[:, 32:40, :], in1=bc8(g1c), op=ALU.mult), [modT, PT], [GG2])
        A_("dve", lambda e: e.tensor_tensor(out=BB2[:], in0=modT[:, 32:40, :], in1=bc8(b1c), op=ALU.mult), [modT, PT], [BB2])
        A_("dve", lambda e: e.tensor_tensor(out=BB2[:], in0=BB2[:], in1=modT[:, 24:32, :], op=ALU.add), [BB2, modT], [BB2])
        A_("dve", lambda e: e.tensor_scalar(out=AG1[:], in0=PT[:, g1c:g1c + 8], scalar1=ALPHA, scalar2=None, op0=ALU.mult), [PT], [AG1])
        A_("dve", lambda e: e.tensor_tensor(out=AB1[:], in0=modT[:, 40:48, :], in1=bc8(mb2c), op=ALU.mult), [modT, PT], [AB1])
        A_("dve", lambda e: e.scalar_tensor_tensor(out=AB1[:], in0=bc8(b1c), scalar=ALPHA, in1=AB1[:], op0=ALU.mult, op1=ALU.add), [AB1, PT], [AB1])
        S.dma("pool", gluW[:], D["glu_w"].rearrange("(k p) c -> p k c", p=128), writes=[gluW.b])
        A_("dve", lambda e: e.memset(rgW[:], 0.0), [], [rgW])
        for ct in range(4):
            for wi, nm in enumerate(("rg_wr", "rg_wi")):
                S.dma("pool", rgW[0:64, ct, wi, 0:64], D[nm][2 * ct], writes=[rgW.b])
                S.dma("pool", rgW[64:128, ct, wi, 64:128], D[nm][2 * ct + 1], writes=[rgW.b])
        lc, _ = PCOLS["rg_lam"]
        A_("act", lambda e: e.activation(out=cp1[:], in_=PT[:, lc:lc + 4], func=AF.Exp, scale=-1.0), [PT], [cp1])
        A_("act", lambda e: e.activation(out=cp1[:], in_=cp1[:], func=AF.Ln, bias=1.0, scale=1.0), [cp1], [cp1])
        A_("dve", lambda e: e.tensor_scalar(out=cp2[:], in0=cp1[:], scalar1=-16.0, scalar2=None, op0=ALU.mult), [cp1], [cp2])
        A_("dve", lambda e: e.tensor_scalar(out=cp1[:], in0=cp1[:], scalar1=-8.0, scalar2=None, op0=ALU.mult), [cp1], [cp1])
        A_("dve", lambda e: e.memset(hstate[:], 0.0), [], [hstate])
        A_("dve", lambda e: e.memset(xhist[:], 0.0), [], [xhist])
        A_("dve", lambda e: e.memset(carryS[:], 0.0), [], [carryS])

        cwc, _ = PCOLS["conv_w"]; cbc, _ = PCOLS["conv_b"]
        brc, _ = PCOLS["rg_br"]; bic, _ = PCOLS["rg_bi"]; gbc, _ = PCOLS["glu_b"]; mb1c, _ = PCOLS["mlp_b1"]
        l2g, _ = PCOLS["ln2_g"]; l2b, _ = PCOLS["ln2_b"]

        def bc(ap17):
            return ap17[:, 1:17].unsqueeze(2).broadcast_to([128, 16, 4])

        def v3d(ap):
            return ap.rearrange("p (s t) -> p s t", t=4)

        def load_x(tok0, ntok, is_s):
            xsrc = D["xs"] if is_s else D["xp"]
            nsub = (ntok + 127) // 128
            for sb in range(nsub):
                sw = min(128, ntok - sb * 128)
                S.dma("sp", xin[0:sw, :], xsrc[tok0 + sb * 128: tok0 + sb * 128 + sw, :], writes=[xin.b])
                for ft in range(8):
                    bk = nb()
                    A_("pe", lambda e, ft=ft, bk=bk, sw=sw: e.transpose(PB[bk][:, 0:sw], xin[0:sw, ft * 128:(ft + 1) * 128], ident[0:sw, 0:sw]), [xin, ident], [PB[bk]])
                    cs = slice(sb * 128, sb * 128 + sw)
                    if is_s:
                        A_("act", lambda e, ft=ft, cs=cs, bk=bk, sw=sw: e.mul(out=axT[:, ft, cs], in_=PB[bk][:, 0:sw], mul=ALPHA), [PB[bk]], [axT])
                    else:
                        A_("dve", lambda e, ft=ft, cs=cs, bk=bk, sw=sw: e.tensor_scalar(out=axT[:, ft, cs], in0=PB[bk][:, 0:sw], scalar1=ALPHA, scalar2=None, op0=ALU.mult), [PB[bk]], [axT])
                    if not is_s:
                        A_("act", lambda e, ft=ft, cs=cs, bk=bk, sw=sw: e.activation(out=hT[:, ft, cs], in_=PB[bk][:, 0:sw], func=AF.Identity, scale=A1(ft)[:, 0:1], bias=SH1(ft)[:, 0:1]), [PB[bk], modT], [hT])
                    else:
                        A_("dve", lambda e, ft=ft, bk=bk: e.tensor_tensor(out=v3d(tmpx[:, 0:64]), in0=v3d(PB[bk][:, 0:64]), in1=bc(A1(ft)), op=ALU.mult), [PB[bk], modT], [tmpx])
                        A_("dve", lambda e, ft=ft: e.tensor_tensor(out=v3d(hT[:, ft, 0:64]), in0=v3d(tmpx[:, 0:64]), in1=bc(SH1(ft)), op=ALU.add), [tmpx, modT], [hT])

        if CFG["nsuper"] > 0:
            load_x(0, ST, False)
        S.barrier()
        A_("dve", lambda e: e.tensor_copy(out=Fs_r[0:64, :, 0], in_=P8[0][0:64]), [P8[0]], [Fs_r])
        A_("dve", lambda e: e.tensor_copy(out=Fs_i[0:64, :, 0], in_=P8[1][0:64]), [P8[1]], [Fs_i])
        A_("dve", lambda e: e.tensor_copy(out=Fs_r[64:128, :, 0], in_=P8[2][64:128]), [P8[2]], [Fs_r])
        A_("dve", lambda e: e.tensor_copy(out=Fs_i[64:128, :, 0], in_=P8[3][64:128]), [P8[3]], [Fs_i])
        m = 1
        while m < 128:
            bcm = lambda ap, m=m: ap.unsqueeze(2).broadcast_to([128, 32, m])
            cmul(lambda m=m: Fs_r[:, :, m:2 * m], lambda m=m: Fs_i[:, :, m:2 * m],
                 lambda m=m: Fs_r[:, :, 0:m], lambda m=m: Fs_i[:, :, 0:m],
                 lambda m=m, bcm=bcm: bcm(Fs_r[:, :, m - 1]), lambda m=m, bcm=bcm: bcm(Fs_i[:, :, m - 1]),
                 lambda m=m: ctA[:, :, 0:m], [Fs_r.b, Fs_i.b], Fs_r.b, Fs_i.b, ctA.b)
            m *= 2
        for (src, dst, p0) in ((Fs_r, TSt[0], 0), (Fs_i, TSt[1], 0), (Fs_r, TSt[2], 64), (Fs_i, TSt[3], 64)):
            for g0 in range(0, 32, 8):
                bk = nb()
                for gl in range(8):
                    g = g0 + gl
                    A_("pe", lambda e, src=src, g=g, gl=gl, bk=bk, p0=p0: e.transpose(PB[bk][:, gl * 64:(gl + 1) * 64], src[p0:p0 + 64, g, :], ident[p0:p0 + 64, p0:p0 + 64]), [src, ident], [PB[bk]])
                if (g0 // 8) % 2 == 0:
                    A_("act", lambda e, dst=dst, g0=g0, bk=bk: e.copy(out=dst[:, g0:g0 + 8, :], in_=PB[bk][:, :].rearrange("p (g n) -> p g n", n=64)), [PB[bk]], [dst])
                else:
                    A_("dve", lambda e, dst=dst, g0=g0, bk=bk: e.tensor_copy(out=dst[:, g0:g0 + 8, :], in_=PB[bk][:, :].rearrange("p (g n) -> p g n", n=64)), [PB[bk]], [dst])
        for i_ in range(4):
            S.dma("sp", scr[i_], TSt[i_][:].rearrange("p g n -> p (g n)"), reads=[TSt[i_].b], writes=[scrB[i_]])
        S.barrier()

        class BB:
            def __init__(self, name):
                self.b = Buf(name)

        ZcH = [BB("ZcH%d" % i) for i in range(2)]
        UQ = [BB("UQ%d" % i) for i in range(4)]
        M1Q = [BB("M1Q%d" % i) for i in range(8)]
        M2Q = [BB("M2Q%d" % i) for i in range(8)]
        HcQ = [BB("HcQ%d" % i) for i in range(8)]
        HTQ = [BB("HTQ%d" % i) for i in range(4)]
        YsC = [BB("YsC%d" % i) for i in range(4)]

        def s5_phase(st_i, is_s):
            ncn = 16 if is_s else 128
            S.dma("sp", FA[:, 13328:19472], scrW, reads=[scrWB], writes=[Wmov.b, WA.b])
            if is_s:
                A_("dve", lambda e: e.memset(Zc[0:16, :, 4:8, :], 0.0), [], ZcH)
                hs = hT.t[:, :, 0:64].rearrange("p k (q t) -> p k t q", t=4)
                ns = 4
            else:
                hs = hT.t[:].rearrange("p k (c s) -> p k s c", s=8)
                ns = 8
            for hf in range(2):
                wb_ = next_wblk()
                load_wblock(D["in_proj"][:, 1024 + hf * 256:1024 + (hf + 1) * 256], wb_)
                for s in range(ns):
                    bk = nb()
                    for K in range(8):
                        A_("pe", lambda e, s=s, K=K, bk=bk, wb_=wb_: e.matmul(PB[bk][0:ncn, 0:256], lhsT=hs[:, K, s, :], rhs=wb_[:, K, :], start=(K == 0), stop=(K == 7)), [hT, wb_], [PB[bk]])
                    slot = (3 - s) if is_s else (7 - s)
                    gs = slice(hf * 16, hf * 16 + 16)
                    if s % 2 == 0:
                        A_("act", lambda e, bk=bk, slot=slot, gs=gs: e.copy(out=Zc[0:ncn, gs, slot, :], in_=PB[bk][0:ncn, 0:256].rearrange("p (g k) -> p g k", k=16)), [PB[bk]], [ZcH[hf]])
                    else:
                        A_("dve", lambda e, bk=bk, slot=slot, gs=gs: e.tensor_copy(out=Zc[0:ncn, gs, slot, :], in_=PB[bk][0:ncn, 0:256].rearrange("p (g k) -> p g k", k=16)), [PB[bk]], [ZcH[hf]])
            for g0 in range(0, 32, 8):
                bk = nb()
                for gl in range(8):
                    g = g0 + gl
                    A_("pe", lambda e, g=g, gl=gl, bk=bk: e.transpose(PBb[bk][:, gl * 128:gl * 128 + ncn], Zc[0:ncn, g, :, :].rearrange("p k i -> p (k i)"), identb[0:ncn, 0:ncn]), [ZcH[g // 16], identb], [PB[bk]])
                A_("act", lambda e, g0=g0, bk=bk: e.copy(out=U[:, g0:g0 + 8, 0:ncn], in_=PBb[bk][:, :].rearrange("p (g c) -> p g c", c=128)[:, :, 0:ncn]), [PB[bk]], [UQ[g0 // 8]])
            if not is_s:
                for i_, tv in enumerate((TFr, TFi, TIr, TIi)):
                    S.dma("sp", tv[:].rearrange("p g n -> p (g n)"), scr[i_], reads=[scrB[i_]], writes=[tv.b])
                for g0 in range(0, 32, 4):
                    bk = nb()
                    for gl in range(4):
                        g = g0 + gl
                        A_("pe", lambda e, g=g, gl=gl, bk=bk: e.matmul(PB[bk][:, gl * 128:(gl + 1) * 128], lhsT=U[:, g, :], rhs=Wmov[:, g, 128:256], start=True, stop=True), [UQ[g // 8], Wmov], [PB[bk]])
                    Pv = lambda bk=bk: PB[bk][:, :].rearrange("p (g r n) -> p g r n", g=4, r=2)
                    sl = slice(g0, g0 + 4)
                    A_("dve", lambda e, Pv=Pv, sl=sl: e.tensor_tensor(out=M1[:, sl], in0=Pv(), in1=TIr[:, sl, :].unsqueeze(2).broadcast_to([128, 4, 2, 64]), op=ALU.mult), [PB[bk], TIr], [M1Q[g0 // 4]])
                    A_("dve", lambda e, Pv=Pv, sl=sl: e.scalar_tensor_tensor(out=M2[:, sl, 0, :], in0=Pv()[:, :, 1, :], scalar=-1.0, in1=TIi[:, sl, :], op0=ALU.mult, op1=ALU.mult), [PB[bk], TIi], [M2Q[g0 // 4]])
                    A_("dve", lambda e, Pv=Pv, sl=sl: e.tensor_tensor(out=M2[:, sl, 1, :], in0=Pv()[:, :, 0, :], in1=TIi[:, sl, :], op=ALU.mult), [PB[bk], TIi], [M2Q[g0 // 4]])
                for q in range(8):
                    bk = nb()
                    sl = slice(4 * q, 4 * q + 4)
                    fl = lambda X, sl=sl: X[:, sl].rearrange("p g r n -> p (g r n)")
                    A_("pe", lambda e, bk=bk, fl=fl: e.matmul(PB[bk][:, :], lhsT=triIb[:], rhs=fl(M1), start=True, stop=False), [triIb, M1Q[q]], [PB[bk]])
                    A_("pe", lambda e, bk=bk, fl=fl: e.matmul(PB[bk][:, :], lhsT=triIb[:], rhs=fl(M2), start=False, stop=(st_i == 0)), [triIb, M2Q[q]], [PB[bk]])
                    if st_i > 0:
                        for gl in range(4):
                            g = 4 * q + gl
                            A_("pe", lambda e, bk=bk, g=g, gl=gl: e.matmul(PB[bk][:, gl * 128:(gl + 1) * 128], lhsT=carryS[:, g:g + 1].broadcast_to([128, 128]), rhs=identb[:], start=False, stop=(gl == 3)), [carryS, identb], [PB[bk]])
                    Gv = lambda bk=bk: PB[bk][:, :].rearrange("p (g r n) -> p g r n", g=4, r=2)
                    A_("dve", lambda e, Gv=Gv, sl=sl: e.tensor_tensor(out=tmpA[:], in0=Gv(), in1=TFr[:, sl, :].unsqueeze(2).broadcast_to([128, 4, 2, 64]), op=ALU.mult), [PB[bk], TFr], [tmpA])
                    A_("dve", lambda e, Gv=Gv, sl=sl: e.scalar_tensor_tensor(out=tmpB[:, :, 0, :], in0=Gv()[:, :, 1, :], scalar=-1.0, in1=TFi[:, sl, :], op0=ALU.mult, op1=ALU.mult), [PB[bk], TFi], [tmpB])
                    A_("dve", lambda e, Gv=Gv, sl=sl: e.tensor_tensor(out=tmpB[:, :, 1, :], in0=Gv()[:, :, 0, :], in1=TFi[:, sl, :], op=ALU.mult), [PB[bk], TFi], [tmpB])
                    A_("dve", lambda e, sl=sl: e.tensor_tensor(out=Hcm[:, sl], in0=tmpA[:], in1=tmpB[:], op=ALU.add), [tmpA, tmpB], [HcQ[q]])
                A_("dve", lambda e: e.tensor_copy(out=HTx[:, :, 0], in_=carryS[:]), [carryS], HTQ)
                for g0 in range(0, 32, 8):
                    bk = nb()
                    for gl in range(8):
                        g = g0 + gl
                        A_("pe", lambda e, g=g, gl=gl, bk=bk: e.transpose(PBb[bk][:, gl * 128:(gl + 1) * 128], Hcm[:, g, :, :].rearrange("p r n -> p (r n)"), identb[:]), [HcQ[g // 4], identb], [PB[bk]])
                    A_("act", lambda e, g0=g0, bk=bk: e.copy(out=HTx[:, g0:g0 + 8, 1:129], in_=PBb[bk][:, :].rearrange("p (g c) -> p g c", c=128)), [PB[bk]], [HTQ[g0 // 8]])
                A_("dve", lambda e: e.tensor_copy(out=carryS[:], in_=HTx[:, :, 128]), HTQ, [carryS])
                hprev = lambda g: HTx[:, g, 0:128]
            else:
                S.dma("sp", s0[:, :, 0, :], D["s5r0"].rearrange("s (g n) -> s g n", n=64), writes=[s0.b])
                S.dma("sp", s0[:, :, 1, :], D["s5i0"].rearrange("s (g n) -> s g n", n=64), writes=[s0.b])
                for ti_, X in enumerate(P4):
                    A_("dve", lambda e, X=X: e.tensor_copy(out=tb16[:], in_=X[:].unsqueeze(2).broadcast_to([128, 32, 16])), [X], [tb16])
                    for g0 in range(0, 32, 8):
                        bk = nb()
                        for gl in range(8):
                            g = g0 + gl
                            A_("pe", lambda e, g=g, gl=gl, bk=bk: e.transpose(PB[bk][0:16, gl * 64:(gl + 1) * 64], tb16[0:64, g, :], ident[0:64, 0:64]), [tb16, ident], [PB[bk]])
                        A_("act", lambda e, ti_=ti_, g0=g0, bk=bk: e.copy(out=SL[ti_][:, g0:g0 + 8, :], in_=PB[bk][0:16, :].rearrange("p (g n) -> p g n", n=64)), [PB[bk]], [SL[ti_]])
                for g0 in range(0, 32, 4):
                    bk = nb()
                    sl = slice(g0, g0 + 4)
                    for gl in range(4):
                        g = g0 + gl
                        A_("pe", lambda e, g=g, gl=gl, bk=bk: e.matmul(PB[bk][0:16, gl * 128:(gl + 1) * 128], lhsT=U[:, g, 0:16], rhs=Wmov[:, g, 128:256], start=True, stop=True), [UQ[g // 8], Wmov], [PB[bk]])
                    Pv = lambda bk=bk: PB[bk][0:16, :].rearrange("p (g r n) -> p g r n", g=4, r=2)
                    mul_ = lambda o, a, b: (lambda e: e.tensor_tensor(out=o(), in0=a(), in1=b(), op=ALU.mult))
                    h0r = lambda sl=sl: s0[:, sl, 0, :]; h0i = lambda sl=sl: s0[:, sl, 1, :]
                    for (dr, di, Lr, Li, addP) in ((lambda sl=sl: Hcm[0:16, sl, 0, :], lambda sl=sl: Hcm[0:16, sl, 1, :], SL[2], SL[3], False),
                                                   (lambda sl=sl: s0[:, sl, 0, :], lambda sl=sl: s0[:, sl, 1, :], SL[0], SL[1], True)):
                        lr = lambda Lr=Lr, sl=sl: Lr[:, sl, :]; li = lambda Li=Li, sl=sl: Li[:, sl, :]
                        A_("dve", mul_(lambda: sc1[:, :, 0, :], h0r, lr), [s0, Lr], [sc1])
                        A_("dve", mul_(lambda: sc2[:, :, 0, :], h0i, li), [s0, Li], [sc2])
                        A_("dve", mul_(lambda: sc1[:, :, 1, :], h0i, lr), [s0, Lr], [sc1])
                        A_("dve", mul_(lambda: sc2[:, :, 1, :], h0r, li), [s0, Li], [sc2])
                        if not addP:
                            A_("dve", lambda e, dr=dr: e.tensor_tensor(out=dr(), in0=sc1[:, :, 0, :], in1=sc2[:, :, 0, :], op=ALU.subtract), [sc1, sc2], [HcQ[g0 // 4]])
                            A_("dve", lambda e, di=di: e.tensor_tensor(out=di(), in0=sc1[:, :, 1, :], in1=sc2[:, :, 1, :], op=ALU.add), [sc1, sc2], [HcQ[g0 // 4]])
                        else:
                            A_("dve", lambda e: e.tensor_tensor(out=sc1[:, :, 0, :], in0=sc1[:, :, 0, :], in1=sc2[:, :, 0, :], op=ALU.subtract), [sc1, sc2], [sc1])
                            A_("dve", lambda e: e.tensor_tensor(out=sc1[:, :, 1, :], in0=sc1[:, :, 1, :], in1=sc2[:, :, 1, :], op=ALU.add), [sc1, sc2], [sc1])
                            A_("dve", lambda e, sl=sl, Pv=Pv: e.tensor_tensor(out=s0[:, sl], in0=sc1[:], in1=Pv(), op=ALU.add), [sc1, PB[bk]], [s0])
                S.dma("sp", D["s5rs"].rearrange("s (g n) -> s g n", n=64), s0[:, :, 0, :], reads=[s0.b])
                S.dma("sp", D["s5is"].rearrange("s (g n) -> s g n", n=64), s0[:, :, 1, :], reads=[s0.b])
                for g0 in range(0, 32, 8):
                    bk = nb()
                    for gl in range(8):
                        g = g0 + gl
                        A_("pe", lambda e, g=g, gl=gl, bk=bk: e.transpose(PBb[bk][:, gl * 128:gl * 128 + 16], Hcm[0:16, g, :, :].rearrange("p r n -> p (r n)"), identb[0:16, 0:16]), [HcQ[g // 4], identb], [PB[bk]])
                    A_("act", lambda e, g0=g0, bk=bk: e.copy(out=HTx[:, g0:g0 + 8, 0:16], in_=PBb[bk][:, :].rearrange("p (g c) -> p g c", c=128)[:, :, 0:16]), [PB[bk]], [HTQ[g0 // 8]])
                hprev = lambda g: HTx[:, g, 0:16]
            for g0 in range(0, 32, 4):
                bk = nb()
                for gl in range(4):
                    g = g0 + gl
                    A_("pe", lambda e, g=g, gl=gl, bk=bk: e.matmul(PB[bk][0:ncn, gl * 128:(gl + 1) * 128], lhsT=U[:, g, 0:ncn], rhs=Wmov[:, g, 0:128], start=True, stop=False), [UQ[g // 8], Wmov], [PB[bk]])
                    A_("pe", lambda e, g=g, gl=gl, bk=bk: e.matmul(PB[bk][0:ncn, gl * 128:(gl + 1) * 128], lhsT=hprev(g), rhs=WA[:, g, :], start=False, stop=True), [HTQ[g // 8], WA], [PB[bk]])
                ct, gl0 = g0 // 8, g0 % 8
                A_("act", lambda e, bk=bk, ct=ct, gl0=gl0: e.activation(out=Ysb[0:ncn, ct, :, gl0:gl0 + 4, :], in_=PB[bk][0:ncn, :].rearrange("p (g j k) -> p j g k", g=4, j=8), func=AF.Gelu_apprx_tanh), [PB[bk]], [YsC[ct]])
            for ct in range(4):
                bk = nb()
                js = range(4, 8) if is_s else range(8)
                for j in js:
                    A_("pe", lambda e, ct=ct, j=j, bk=bk: e.transpose(PBb[bk][:, j * 128:j * 128 + ncn], Ysb[0:ncn, ct, j, :, :].rearrange("p g k -> p (g k)"), identb[0:ncn, 0:ncn]), [YsC[ct], identb], [PB[bk]])
                if not is_s:
                    A_("act", lambda e, ct=ct, bk=bk: e.copy(out=gyT[:, ct, :].rearrange("p (c j) -> p j c", j=8), in_=PBb[bk][:, :].rearrange("p (j c) -> p j c", c=128)), [PB[bk]], [gyT])
                else:
                    A_("act", lambda e, ct=ct, bk=bk: e.copy(out=gyT[:, ct, 0:64].rearrange("p (s t) -> p t s", t=4), in_=PBb[bk][:, 512:1024].rearrange("p (j c) -> p j c", c=128)[:, :, 0:16]), [PB[bk]], [gyT])

        def run_tile(ti, is_s, off, stages=("F1", "F2", "M", "E"), role="S"):
            if role == "B":
                RM, LNE = r1a, (rb2, rsq2, mean2, rstd2)
            elif role == "A":
                RM, LNE = r1, (rb2, rsq2, mean, rstd)
            else:
                RM, LNE = r1, (rb, rsq, mean, rstd)
            n = 64 if is_s else NT
            tok0 = 0 if is_s else ti * NT
            ydst = D["ys"] if is_s else D["yp"]
            nsub = 1 if is_s else NT // 128
            sw = 64 if is_s else 128
            hTt = lambda k: hT[:, k, off:off + n]
            axt = lambda k: axT[:, k, off:off + n]
            def layer_norm_stats(r1, rb, rsq, mean, rstd):
                for ft in range(8):
                    A_("act", lambda e, ft=ft: e.copy(out=rb[:, ft, 0:n], in_=r1[:, ft, 0:n]), [r1.c[ft]], [rb.c[ft]])
                    A_("act", lambda e, ft=ft: e.activation(out=rsq[:, ft, 0:n], in_=r1[:, ft, 0:n], func=AF.Square), [r1.c[ft]], [rsq.c[ft]])
                b1_, b2_ = nb(), nb()
                for ft in range(8):
                    A_("pe", lambda e, ft=ft: e.matmul(PB[b1_][:, 0:n], lhsT=onesb[:], rhs=rb[:, ft, 0:n], start=(ft == 0), stop=(ft == 7)), [onesb, rb.c[ft]], [PB[b1_]])
                for ft in range(8):
                    A_("pe", lambda e, ft=ft: e.matmul(PB[b2_][:, 0:n], lhsT=onesb[:], rhs=rsq[:, ft, 0:n], start=(ft == 0), stop=(ft == 7)), [onesb, rsq.c[ft]], [PB[b2_]])
                A_("act", lambda e: e.copy(out=mean[:, 0:n], in_=PB[b1_][:, 0:n]), [PB[b1_]], [mean])
                A_("dve", lambda e: e.tensor_tensor(out=rstd[:, 0:n], in0=mean[:, 0:n], in1=mean[:, 0:n], op=ALU.mult), [mean], [rstd])
                A_("dve", lambda e: e.tensor_tensor(out=rstd[:, 0:n], in0=PB[b2_][:, 0:n], in1=rstd[:, 0:n], op=ALU.subtract), [PB[b2_], rstd], [rstd])
                A_("dve", lambda e: e.tensor_scalar(out=rstd[:, 0:n], in0=rstd[:, 0:n], scalar1=0.0, scalar2=LN_EPS, op0=ALU.max, op1=ALU.add), [rstd], [rstd])
                A_("act", lambda e: e.activation(out=rstd[:, 0:n], in_=rstd[:, 0:n], func=AF.Sqrt), [rstd], [rstd])
                A_("dve", lambda e: e.reciprocal(out=rstd[:, 0:n], in_=rstd[:, 0:n]), [rstd], [rstd])
                for ft in range(8):
                    A_("dve", lambda e, ft=ft: e.tensor_tensor(out=r1[:, ft, 0:n], in0=r1[:, ft, 0:n], in1=mean[:, 0:n], op=ALU.subtract), [r1.c[ft], mean], [r1.c[ft]])
                    A_("dve", lambda e, ft=ft: e.tensor_tensor(out=r1[:, ft, 0:n], in0=r1[:, ft, 0:n], in1=rstd[:, 0:n], op=ALU.mult), [r1.c[ft], rstd], [r1.c[ft]])

            if "F1" in stages:
                for mt in range(8):
                    if mt % 2 == 0:
                        win = next_wblk()
                        load_wblock(D["in_proj"][:, (mt // 2) * 256:(mt // 2 + 1) * 256], win)
                    bk = nb()
                    for k in range(8):
                        A_("pe", lambda e, mt=mt, k=k, bk=bk, win=win: e.matmul(PB[bk][:, 0:n], lhsT=win[:, k, (mt % 2) * 128:(mt % 2 + 1) * 128], rhs=hTt(k), start=(k == 0), stop=(k == 7)), [win, hT], [PB[bk]])
                    ct = mt % 4
                    if mt < 4:
                        A_("act", lambda e, ct=ct, bk=bk: e.copy(out=gT[:, ct, 0:n], in_=PB[bk][:, 0:n]), [PB[bk]], [gT.c[ct]])
                    elif not is_s:
                        A_("act", lambda e, ct=ct, bk=bk: e.copy(out=xaT[:, ct, 3:3 + n], in_=PB[bk][:, 0:n]), [PB[bk]], [xaT.c[ct]])
                    else:
                        A_("act", lambda e, ct=ct, bk=bk: e.copy(out=xaS[:, ct, :, 3:7], in_=v3d(PB[bk][:, 0:64])), [PB[bk]], [xaS])
                if not is_s:
                    A_("dve", lambda e: e.tensor_copy(out=xaT[:, :, 0:3], in_=xhist[:]), [xhist], [xaT])
                for ct in range(4):
                    if not is_s:
                        xx = lambda k, ct=ct: xaT[:, ct, k:k + n]
                        o3 = lambda tt, ct=ct: tt[:, ct, 0:n]
                        xab = xaT.c[ct]
                    else:
                        xx = lambda k, ct=ct: xaS[:, ct, :, k:k + 4]
                        o3 = lambda tt, ct=ct: v3d(tt[:, ct, 0:64])
                        xab = xaS
                    cw = lambda k, ct=ct: PT[:, cwc + k * 4 + ct: cwc + k * 4 + ct + 1]
                    A_("dve", lambda e, ct=ct, xx=xx, o3=o3, cw=cw: e.tensor_scalar(out=o3(t1), in0=xx(0), scalar1=cw(0), scalar2=PT[:, cbc + ct:cbc + ct + 1], op0=ALU.mult, op1=ALU.add), [xab, PT], [t1.c[ct]])
                    for k in (1, 2, 3):
                        A_("dve", lambda e, k=k, xx=xx, o3=o3, cw=cw: e.scalar_tensor_tensor(out=o3(t1), in0=xx(k), scalar=cw(k), in1=o3(t1), op0=ALU.mult, op1=ALU.add), [xab, PT, t1.c[ct]], [t1.c[ct]])
                    A_("act", lambda e, ct=ct: e.copy(out=xcb[:, ct, 0:n], in_=t1[:, ct, 0:n]), [t1.c[ct]], [xcb.c[ct]])
                for ct in range(4):
                    for wi in range(2):
                        bk = nb()
                        A_("pe", lambda e, ct=ct, wi=wi, bk=bk: e.matmul(PB[bk][:, 0:n], lhsT=rgW[:, ct, wi, :], rhs=xcb[:, ct, 0:n], start=True, stop=True), [rgW, xcb.c[ct]], [PB[bk]])
                        dst = t2 if wi == 0 else t3
                        dstb = dst.c[ct]
                        bcol = (brc if wi == 0 else bic) + ct
                        A_("act", lambda e, ct=ct, bk=bk, dst=dst, bcol=bcol: e.activation(out=dst[:, ct, 0:n], in_=PB[bk][:, 0:n], func=AF.Sigmoid, bias=PT[:, bcol:bcol + 1], scale=1.0), [PB[bk], PT], [dstb])
                for ct in range(4):
                    A_("act", lambda e, ct=ct: e.activation(out=t4[:, ct, 0:n], in_=t2[:, ct, 0:n], func=AF.Exp, scale=cp1[:, ct:ct + 1]), [t2.c[ct], cp1], [t4.c[ct]])
                    A_("act", lambda e, ct=ct: e.activation(out=t2[:, ct, 0:n], in_=t2[:, ct, 0:n], func=AF.Exp, scale=cp2[:, ct:ct + 1]), [t2.c[ct], cp2], [t2.c[ct]])
                for ct in range(4):
                    A_("act", lambda e, ct=ct: e.activation(out=t2[:, ct, 0:n], in_=t2[:, ct, 0:n], func=AF.Sqrt, scale=-1.0, bias=1.0), [t2.c[ct]], [t2.c[ct]])
                if (not is_s) and ti == 0:
                    A_("dve", lambda e: e.memset(t2[:, :, 0:1], 1.0), [t2], [t2])
                for ct in range(4):
                    A_("dve", lambda e, ct=ct: e.tensor_tensor(out=t3[:, ct, 0:n], in0=t3[:, ct, 0:n], in1=t1[:, ct, 0:n], op=ALU.mult), [t3.c[ct], t1.c[ct]], [t3.c[ct]])
                    A_("dve", lambda e, ct=ct: e.tensor_tensor(out=t3[:, ct, 0:n], in0=t3[:, ct, 0:n], in1=t2[:, ct, 0:n], op=ALU.mult), [t3.c[ct], t2.c[ct]], [t3.c[ct]])
                if is_s:
                    for ct in range(4):
                        a0 = lambda ct=ct: t4[:, ct, 0:64].rearrange("p (s t) -> p s t", t=4)[:, :, 0]
                        b0 = lambda ct=ct: t3[:, ct, 0:64].rearrange("p (s t) -> p s t", t=4)[:, :, 0]
                        A_("dve", lambda e, ct=ct, a0=a0: e.tensor_tensor(out=a0(), in0=a0(), in1=h0T[:, ct, :], op=ALU.mult), [t4.c[ct], h0T], [t4.c[ct]])
                        A_("dve", lambda e, ct=ct, a0=a0, b0=b0: e.tensor_tensor(out=b0(), in0=b0(), in1=a0(), op=ALU.add), [t4.c[ct], t3.c[ct]], [t3.c[ct]])
                        A_("dve", lambda e, ct=ct, a0=a0: e.memset(a0(), 0.0), [t4.c[ct]], [t4.c[ct]])
                for ct in range(4):
                    init = 0.0 if is_s else hstate[:, ct:ct + 1]
                    A_("dve", lambda e, ct=ct, init=init: e.tensor_tensor_scan(out=t1[:, ct, 0:n], data0=t4[:, ct, 0:n], data1=t3[:, ct, 0:n], initial=init, op0=ALU.mult, op1=ALU.add), [t4.c[ct], t3.c[ct], hstate], [t1.c[ct]])
                    if not is_s:
                        A_("dve", lambda e, ct=ct: e.tensor_copy(out=hstate[:, ct:ct + 1], in_=t1[:, ct, n - 1:n]), [t1.c[ct]], [hstate])
                        A_("dve", lambda e, ct=ct: e.tensor_copy(out=xhist[:, ct, :], in_=xaT[:, ct, n:n + 3]), [xaT.c[ct]], [xhist])
                    A_("act", lambda e, ct=ct: e.activation(out=t2[:, ct, 0:n], in_=gT[:, ct, 0:n], func=AF.Gelu_apprx_tanh), [gT.c[ct]], [t2.c[ct]])
            if "F2" in stages or "F2a" in stages:
                for ct in range(4):
                    A_("dve", lambda e, ct=ct: e.tensor_tensor(out=ycat[:, ct, 0:n], in0=t2[:, ct, 0:n], in1=t1[:, ct, 0:n], op=ALU.mult), [t2.c[ct], t1.c[ct]], [ycat.c[ct]])
                if is_s:
                    bk = nb()
                    for ct in range(4):
                        A_("pe", lambda e, ct=ct, bk=bk: e.transpose(PB[bk][0:16, ct * 128:(ct + 1) * 128], t1[:, ct, 0:64].rearrange("p (s t) -> p s t", t=4)[:, :, 3], ident[:]), [t1, ident], [PB[bk]])
                    A_("dve", lambda e, bk=bk: e.tensor_copy(out=sm[0:16, :], in_=PB[bk][0:16, :]), [PB[bk]], [sm])
                    S.dma("sp", D["hs"], sm[0:16, :], reads=[sm.b])
                    bk = nb()
                    for ct in range(4):
                        A_("act", lambda e, ct=ct: e.copy(out=t2[:, ct, 0:48].rearrange("p (s k) -> p s k", k=3), in_=xaS[:, ct, :, 4:7]), [xaS], [t2])
                        A_("pe", lambda e, ct=ct, bk=bk: e.transpose(PB[bk][0:48, ct * 128:(ct + 1) * 128], t2[:, ct, 0:48], ident[:]), [t2, ident], [PB[bk]])
                    A_("dve", lambda e, bk=bk: e.tensor_copy(out=sm[:, :], in_=PB[bk][0:48, :]), [PB[bk]], [sm])
                    S.dma("sp", D["convs"], sm[:, :], reads=[sm.b])
                elif ti == NPT - 1:
                    bk = nb()
                    A_("pe", lambda e, bk=bk: e.transpose(PB[bk][0:4, 0:128], hstate[:, 0:4], ident[:]), [hstate, ident], [PB[bk]])
                    A_("dve", lambda e, bk=bk: e.tensor_copy(out=sm[0:4, 0:128], in_=PB[bk][0:4, 0:128]), [PB[bk]], [sm])
                    S.dma("sp", D["hp"], sm[0:4, 0:128], reads=[sm.b])
                    A_("act", lambda e: e.copy(out=t2[:, 0, 0:12].rearrange("p (k c) -> p k c", c=4), in_=xhist[:].rearrange("p c k -> p k c")), [xhist], [t2])
                    bk = nb()
                    A_("pe", lambda e, bk=bk: e.transpose(PB[bk][0:12, 0:128], t2[:, 0, 0:12], ident[:]), [t2, ident], [PB[bk]])
                    A_("dve", lambda e, bk=bk: e.tensor_copy(out=sm[32:44, 0:128], in_=PB[bk][0:12, 0:128]), [PB[bk]], [sm])
                    S.dma("sp", D["convp"], sm[32:44, 0:128], reads=[sm.b])
                for mt in range(4):
                    bk = nb()
                    for k in range(4):
                        A_("pe", lambda e, mt=mt, k=k, bk=bk: e.matmul(PB[bk][:, 0:n], lhsT=gluW[:, k, mt * 128:(mt + 1) * 128], rhs=gyT[:, k, off:off + n], start=(k == 0), stop=(k == 3)), [gluW, gyT], [PB[bk]])
                    A_("act", lambda e, mt=mt, bk=bk: e.activation(out=t3[:, mt, 0:n], in_=PB[bk][:, 0:n], func=AF.Sigmoid, bias=PT[:, gbc + mt:gbc + mt + 1], scale=1.0), [PB[bk], PT], [t3.c[mt]])
                    A_("dve", lambda e, mt=mt: e.tensor_tensor(out=ycat[:, 4 + mt, 0:n], in0=t3[:, mt, 0:n], in1=gyT[:, mt, off:off + n], op=ALU.mult), [t3.c[mt], gyT], [ycat.c[4 + mt]])
                for cbk in range(4):
                    w = next_wblk()
                    load_wblock(D["out_proj"][:, cbk * 256:(cbk + 1) * 256], w)
                    for q in range(2):
                        ft = cbk * 2 + q
                        bk = nb()
                        for k in range(8):
                            A_("pe", lambda e, w=w, q=q, k=k, bk=bk: e.matmul(PB[bk][:, 0:n], lhsT=w[:, k, q * 128:(q + 1) * 128], rhs=ycat[:, k, 0:n], start=(k == 0), stop=(k == 7)), [w, ycat.c[k]], [PB[bk]])
                        if not is_s:
                            A_("dve", lambda e, ft=ft, bk=bk: e.scalar_tensor_tensor(out=r1a[:, ft, 0:n], in0=PB[bk][:, 0:n], scalar=G1(ft)[:, 0:1], in1=axt(ft), op0=ALU.mult, op1=ALU.add), [PB[bk], modT, axT], [r1a.c[ft]])
                        else:
                            A_("dve", lambda e, ft=ft, bk=bk: e.tensor_tensor(out=v3d(r1a[:, ft, 0:64]), in0=v3d(PB[bk][:, 0:64]), in1=bc(G1(ft)), op=ALU.mult), [PB[bk], modT], [r1a.c[ft]])
                            A_("dve", lambda e, ft=ft: e.tensor_tensor(out=r1a[:, ft, 0:64], in0=r1a[:, ft, 0:64], in1=axt(ft), op=ALU.add), [r1a.c[ft], axT], [r1a.c[ft]])

                layer_norm_stats(r1a, rb2, rsq2, mean2, rstd2)
            if "F2" in stages or "F2b" in stages:
                for ft in range(8):
                    if not is_s:
                        A_("act", lambda e, ft=ft: e.activation(out=h2T[:, ft, 0:n], in_=r1a[:, ft, 0:n], func=AF.Identity, scale=GG2[:, ft, 0:1], bias=BB2[:, ft, 0:1]), [r1a.c[ft], GG2, BB2], [h2T.c[ft]])
                        A_("act", lambda e, ft=ft: e.activation(out=axt(ft), in_=r1a[:, ft, 0:n], func=AF.Identity, scale=AG1[:, ft:ft + 1], bias=AB1[:, ft, 0:1]), [r1a.c[ft], AG1, AB1], [axT])
                    else:
                        A_("dve", lambda e, ft=ft: e.tensor_tensor(out=v3d(t1[:, 0, 0:64]), in0=v3d(r1a[:, ft, 0:64]), in1=bc(GG2[:, ft, :]), op=ALU.mult), [r1a.c[ft], GG2], [t1])
                        A_("dve", lambda e, ft=ft: e.tensor_tensor(out=v3d(h2T[:, ft, 0:64]), in0=v3d(t1[:, 0, 0:64]), in1=bc(BB2[:, ft, :]), op=ALU.add), [t1, BB2], [h2T.c[ft]])
                        A_("dve", lambda e, ft=ft: e.tensor_scalar(out=t1[:, 1, 0:64], in0=r1a[:, ft, 0:64], scalar1=AG1[:, ft:ft + 1], scalar2=None, op0=ALU.mult), [r1a.c[ft], AG1], [t1])
                        A_("dve", lambda e, ft=ft: e.tensor_tensor(out=v3d(axt(ft)), in0=v3d(t1[:, 1, 0:64]), in1=bc(AB1[:, ft, :]), op=ALU.add), [t1, AB1], [axT])
            if "M" in stages:
                for hk in range(2):
                    for blk in range(8):
                        w1 = next_wblk()
                        load_wblock(D["mlp_w1"][:, (hk * 8 + blk) * 256:(hk * 8 + blk + 1) * 256], w1)
                        for q in range(2):
                            bk = nb()
                            for k in range(8):
                                A_("pe", lambda e, w1=w1, q=q, k=k, bk=bk: e.matmul(PB[bk][:, 0:n], lhsT=w1[:, k, q * 128:(q + 1) * 128], rhs=h2T[:, k, 0:n], start=(k == 0), stop=(k == 7)), [w1, h2T.c[k]], [PB[bk]])
                            fcol = mb1c + (hk * 8 + blk) * 2 + q
                            A_("act", lambda e, q=q, bk=bk, fcol=fcol: e.activation(out=f1a[:, q, 0:n], in_=PB[bk][:, 0:n], func=AF.Relu, bias=PT[:, fcol:fcol + 1], scale=1.0), [PB[bk], PT], [f1a])
                            A_("dve", lambda e, q=q, blk=blk: e.tensor_tensor(out=f1b[:, blk * 2 + q, 0:n], in0=f1a[:, q, 0:n], in1=f1a[:, q, 0:n], op=ALU.mult), [f1a], [f1b.c[blk * 2 + q]])
                    for m in range(8):
                        bk = nb()
                        w2 = next_wblk()
                        w2v = w2[:].rearrange("p a b -> p (a b)").rearrange("p (k c) -> p k c", c=128)
                        S.dma("pool", w2v, D["mlp_w2"][hk * 2048:(hk + 1) * 2048, m * 128:(m + 1) * 128].rearrange("(k p) c -> p k c", p=128), writes=[w2.b])
                        for k in range(16):
                            A_("pe", lambda e, w2v=w2v, k=k, bk=bk, w2=w2: e.matmul(PB[bk][:, 0:n], lhsT=w2v[:, k, :], rhs=f1b[:, k, 0:n], start=(k == 0), stop=(k == 15)), [w2, f1b.c[k]], [PB[bk]])
                        if hk == 0:
                            A_("act", lambda e, m=m, bk=bk: e.copy(out=RM[:, m, 0:n], in_=PB[bk][:, 0:n]), [PB[bk]], [RM.c[m]])
                        else:
                            A_("dve", lambda e, m=m, bk=bk: e.tensor_tensor(out=RM[:, m, 0:n], in0=PB[bk][:, 0:n], in1=RM[:, m, 0:n], op=ALU.add), [PB[bk], RM.c[m]], [RM.c[m]])
                            if not is_s:
                                A_("dve", lambda e, m=m: e.scalar_tensor_tensor(out=RM[:, m, 0:n], in0=RM[:, m, 0:n], scalar=G2(m)[:, 0:1], in1=axt(m), op0=ALU.mult, op1=ALU.add), [RM.c[m], modT, axT], [RM.c[m]])
                            else:
                                A_("dve", lambda e, m=m: e.tensor_tensor(out=v3d(RM[:, m, 0:64]), in0=v3d(RM[:, m, 0:64]), in1=bc(G2(m)), op=ALU.mult), [RM.c[m], modT], [RM.c[m]])
                                A_("dve", lambda e, m=m: e.tensor_tensor(out=RM[:, m, 0:64], in0=RM[:, m, 0:64], in1=axt(m), op=ALU.add), [RM.c[m], axT], [RM.c[m]])
            if "E" in stages:
                layer_norm_stats(RM, *LNE)
                for ft in range(8):
                    A_("act", lambda e, ft=ft: e.activation(out=RM[:, ft, 0:n], in_=RM[:, ft, 0:n], func=AF.Identity, scale=PT[:, l2g + ft:l2g + ft + 1], bias=PT[:, l2b + ft:l2b + ft + 1]), [RM.c[ft], PT], [RM.c[ft]])
                for sb in range(nsub):
                    for ft in range(8):
                        bk = nb()
                        A_("pe", lambda e, ft=ft, sb=sb, bk=bk: e.transpose(PB[bk][0:sw, 0:128], RM[:, ft, sb * 128:sb * 128 + sw], ident[:]), [RM.c[ft], ident], [PB[bk]])
                        if ft % 2 == 0:
                            A_("act", lambda e, ft=ft, bk=bk: e.copy(out=yout[0:sw, ft * 128:(ft + 1) * 128], in_=PB[bk][0:sw, 0:128]), [PB[bk]], [yout])
                        else:
                            A_("dve", lambda e, ft=ft, bk=bk: e.tensor_copy(out=yout[0:sw, ft * 128:(ft + 1) * 128], in_=PB[bk][0:sw, 0:128]), [PB[bk]], [yout])
                    S.dma("sp", ydst[tok0 + sb * 128: tok0 + sb * 128 + sw, :], yout[0:sw, :], reads=[yout.b])

        def sample_tile():
            S.dma("sp", sm[0:16, :], D["h0"], writes=[sm.b])
            bk = nb()
            for ct in range(4):
                A_("pe", lambda e, ct=ct, bk=bk: e.transpose(PB[bk][:, ct * 16:(ct + 1) * 16], sm[0:16, ct * 128:(ct + 1) * 128], ident[0:16, 0:16]), [sm, ident], [PB[bk]])
            A_("dve", lambda e, bk=bk: e.tensor_copy(out=h0T[:].rearrange("p c s -> p (c s)"), in_=PB[bk][:, 0:64]), [PB[bk]], [h0T])
            S.barrier()
            s5_phase(0, True)
            S.barrier()
            S.dma("sp", sm[:, :], D["conv0"].rearrange("s k c -> (s k) c"), writes=[sm.b])
            bk = nb()
            for ct in range(4):
                A_("pe", lambda e, ct=ct, bk=bk: e.transpose(PB[bk][:, ct * 48:(ct + 1) * 48], sm[0:48, ct * 128:(ct + 1) * 128], ident[0:48, 0:48]), [sm, ident], [PB[bk]])
            A_("dve", lambda e, bk=bk: e.tensor_copy(out=xaS[:, :, :, 0:3], in_=PB[bk][:, 0:192].rearrange("p (c s k) -> p c s k", c=4, k=3)), [PB[bk]], [xaS])
            run_tile(0, True, 0)
            S.barrier()

        nsup = CFG["nsuper"]
        for st_i in range(nsup):
            S.barrier()
            s5_phase(st_i, False)
            S.barrier()
            tA, tB = st_i * 2, st_i * 2 + 1
            run_tile(tA, False, 0, stages=("F1", "F2"), role="A")
            cur_pool[0] = 0
            capA = S.capture(lambda: run_tile(tA, False, 0, stages=("M",), role="A"))
            cur_pool[0] = 1
            capB = S.capture(lambda: run_tile(tB, False, NT, stages=("F1", "F2a"), role="B"))
            cur_pool[0] = None
            S.interleave(capA, capB, frac=CFG.get("frac1", 1.0))
            run_tile(tB, False, NT, stages=("F2b",), role="B")
            cur_pool[0] = 0
            capA = S.capture(lambda: run_tile(tB, False, NT, stages=("M",), role="B"))
            cur_pool[0] = 1
            capB = S.capture(lambda: run_tile(tA, False, 0, stages=("E",), role="A"))
            cur_pool[0] = None
            S.interleave(capA, capB)
            cur_pool[0] = 0
            capE = S.capture(lambda: run_tile(tB, False, NT, stages=("E",), role="B"))
            cur_pool[0] = 1
            if st_i + 1 < nsup:
                capX = S.capture(lambda: load_x((st_i + 1) * ST, ST, False))
            elif CFG["sample"]:
                capX = S.capture(lambda: load_x(0, 64, True))
            else:
                capX = []
            cur_pool[0] = None
            S.interleave(capE, capX)
        if CFG["sample"]:
            sample_tile()
        S.barrier()
        A_("act", lambda e: e.copy(out=t1[:, 0, 0:32], in_=carryS[:]), [carryS], [t1])
        bk = nb()
        A_("pe", lambda e, bk=bk: e.transpose(PB[bk][0:32, 0:128], t1[:, 0, 0:32], ident[:]), [t1, ident], [PB[bk]])
        A_("dve", lambda e, bk=bk: e.tensor_copy(out=sm[0:32, 0:128], in_=PB[bk][0:32, 0:128]), [PB[bk]], [sm])
        S.dma("sp", D["s5rp"], sm[0:32, 0:64], reads=[sm.b])
        S.dma("sp", D["s5ip"], sm[0:32, 64:128], reads=[sm.b])

        blk_ = st.enter_context(nc.Block())
        S.emit(blk_)
    return nc


_NC_CACHE = {}


def kernel(**inp):
    f = lambda a: np.ascontiguousarray(np.asarray(a, dtype=np.float32))
    n_cores = 8
    ident = np.eye(128, dtype=np.float32)
    ii = np.arange(128)
    trii = (ii[:, None] <= ii[None, :]).astype(np.float32)
    kk, kin = ii // 16, ii % 16
    maskt = ((kk[None, :] + kk[:, None]) >= 7).astype(np.float32)
    diagsel = (((kk[None, :] + kk[:, None]) == 7) & (kin[:, None] == kin[None, :])).astype(np.float32)
    shared = {
        "ada_w": f(inp["ada_w"][0]), "ada_b": f(inp["ada_b"][0]).reshape(48, 128), "in_proj": f(inp["in_proj"][0]),
        "conv_w": f(inp["conv_w"][0]).reshape(16, 128), "conv_b": f(inp["conv_b"][0]).reshape(4, 128),
        "rg_wr": f(inp["rg_wr"][0]), "rg_br": f(inp["rg_br"][0]).reshape(4, 128), "rg_wi": f(inp["rg_wi"][0]),
        "rg_bi": f(inp["rg_bi"][0]).reshape(4, 128), "rg_lam": f(inp["rg_lam"][0]).reshape(4, 128),
        "s5_a_re": f(inp["s5_a_re"][0]), "s5_a_im": f(inp["s5_a_im"][0]), "s5_log_dt": f(inp["s5_log_dt"][0]).reshape(1, 32),
        "s5_b_re": f(inp["s5_b_re"][0]), "s5_b_im": f(inp["s5_b_im"][0]),
        "s5_c_re": f(inp["s5_c_re"][0]).reshape(512, 64), "s5_c_im": f(inp["s5_c_im"][0]).reshape(512, 64),
        "s5_d": f(inp["s5_d"][0]).reshape(32, 16), "glu_w": f(inp["glu_w"][0]), "glu_b": f(inp["glu_b"][0]).reshape(4, 128),
        "out_proj": f(inp["out_proj"][0]), "ln1_g": f(inp["ln1_g"][0]).reshape(8, 128), "ln1_b": f(inp["ln1_b"][0]).reshape(8, 128),
        "mlp_w1": f(inp["mlp_w1"][0]), "mlp_b1": f(inp["mlp_b1"][0]).reshape(32, 128), "mlp_w2": f(inp["mlp_w2"][0]),
        "mlp_b2": f(inp["mlp_b2"][0]).reshape(8, 128), "ln2_g": f(inp["ln2_g"][0]).reshape(8, 128), "ln2_b": f(inp["ln2_b"][0]).reshape(8, 128),
        "ident": ident, "trii": trii, "maskt": maskt, "diagsel": diagsel,
    }
    xp = f(inp["x_prompt"]); xs = f(inp["x_sample"])
    in_maps = []
    for c in range(n_cores):
        sl = slice(16 * c, 16 * c + 16)
        m = dict(shared)
        m["xp"] = xp[c]
        m["xs"] = xs[sl].reshape(64, 1024)
        m["cvec"] = np.ascontiguousarray(np.concatenate([f(inp["c_prompt"])[c:c + 1], f(inp["c_sample"])[sl]], axis=0))
        m["conv0"] = f(inp["state_conv"][0][sl])
        m["h0"] = f(inp["state_rglru_h"][0][sl])
        m["s5r0"] = f(inp["state_s5_re"][0][sl]).reshape(16, 2048)
        m["s5i0"] = f(inp["state_s5_im"][0][sl]).reshape(16, 2048)
        in_maps.append(m)
    if "nc" not in _NC_CACHE:
        _NC_CACHE["nc"] = build_nc()
    nc = _NC_CACHE["nc"]
    res = run_bass_kernel_spmd(nc, in_maps, core_ids=list(range(n_cores)))
    R = res.results
    cat = lambda k: np.stack([np.asarray(R[c][k], dtype=np.float32) for c in range(n_cores)])
    yp = cat("yp")
    ys = cat("ys").reshape(128, 4, 1024)
    convp = cat("convp").reshape(8, 3, 4, 128).reshape(1, 8, 3, 512)
    hp = cat("hp").reshape(1, 8, 512)
    s5rp = cat("s5rp").reshape(1, 8, 32, 64)
    s5ip = cat("s5ip").reshape(1, 8, 32, 64)
    convs = cat("convs").reshape(1, 128, 3, 512)
    hs = cat("hs").reshape(1, 128, 512)
    s5rs = cat("s5rs").reshape(1, 128, 32, 64)
    s5is = cat("s5is").reshape(1, 128, 32, 64)
    return (yp, ys, convp, hp, s5rp, s5ip, convs, hs, s5rs, s5is)
```

```python
import numpy as np
from contextlib import ExitStack
import concourse.bass as bass
import concourse.mybir as mybir
from concourse.bass_utils import run_bass_kernel_spmd

F32 = mybir.dt.float32
BF16 = mybir.dt.bfloat16
AF = mybir.ActivationFunctionType
ALU = mybir.AluOpType

ALPHA = 2.0 ** 0.25
LN_EPS = 1e-5
NT = 512
NPT = 2048 // NT
ST = 1024


class Buf:
    __slots__ = ("name", "w", "r", "sem", "semv", "excl")

    def __init__(self, name):
        self.name = name
        self.excl = False
        self.w = None
        self.r = []
        self.sem = None
        self.semv = 0


class Sched:
    ENG = ("pe", "act", "dve", "pool", "sp")

    def __init__(self, nc, stack):
        self.nc = nc
        self.stack = stack
        self.q = {e: [] for e in self.ENG}
        self.cnt = {e: 0 for e in self.ENG}
        self.esem = {e: stack.enter_context(nc.semaphore("s_" + e)) for e in self.ENG}
        self.waited = {e: {} for e in self.ENG}
        self.dmabufs = []
        self.cap = None

    def capture(self, fn):
        self.cap = []
        fn()
        c = self.cap
        self.cap = None
        return c

    def replay(self, items):
        for it in items:
            if it[0] == "op":
                self.op(*it[1:])
            else:
                self.dma(it[1], it[2], it[3], reads=it[4], writes=it[5], slow=it[6])

    def interleave(self, A, B, frac=1.0):
        def rw(items):
            r, w = set(), set()
            for it in items:
                rd, wr = (it[3], it[4]) if it[0] == "op" else (it[4], it[5])
                r.update(id(x) for x in rd)
                w.update(id(x) for x in wr)
            return r, w
        rA, wA = rw(A)
        rB, wB = rw(B)
        names = {}
        for it in list(A) + list(B):
            rd, wr = (it[3], it[4]) if it[0] == "op" else (it[4], it[5])
            for x in list(rd) + list(wr):
                names[id(x)] = x.name
        bad = (wA & (rB | wB)) | (wB & rA)
        assert not bad, "interleaved streams share written buffers: %s" % sorted(names[i] for i in bad)
        out = []
        nA, nB = len(A), len(B)
        nAe = max(1, int(nA * frac))
        j = 0
        for i, a in enumerate(A):
            out.append(a)
            while j < nB and (j + 1) * nAe <= (i + 1) * nB:
                out.append(B[j])
                j += 1
        out.extend(B[j:])
        self.replay(out)

    def _deps(self, eng, reads, writes):
        deps = {}

        def add(d):
            key, val = d
            if deps.get(key, 0) < val:
                deps[key] = val

        for b in reads:
            if b.w is not None:
                add(b.w)
            if b.excl:
                for d in b.r:
                    if d[0] != eng:
                        add(d)
        for b in writes:
            if b.w is not None:
                add(b.w)
            for d in b.r:
                add(d)
        if eng == "pe":
            deps.pop("pe", None)
        out = []
        for key, val in deps.items():
            if self.waited[eng].get(key, 0) >= val:
                continue
            self.waited[eng][key] = val
            out.append((key, val))
        return out

    def op(self, eng, fn, reads=(), writes=()):
        if self.cap is not None:
            self.cap.append(("op", eng, fn, tuple(reads), tuple(writes)))
            return
        waits = self._deps(eng, reads, writes)
        self.cnt[eng] += 1
        c = self.cnt[eng]
        self.q[eng].append((waits, fn, None))
        for b in writes:
            b.w = (eng, c)
            b.r = []
        for b in reads:
            b.r.append((eng, c))

    def dma(self, q, out_ap, in_ap, reads=(), writes=(), slow=False):
        if self.cap is not None:
            self.cap.append(("dma", q, out_ap, in_ap, tuple(reads), tuple(writes), slow))
            return
        waits = self._deps(q, reads, writes)
        tb = writes[0] if writes else reads[0]
        if tb.sem is None:
            tb.sem = self.stack.enter_context(self.nc.semaphore("d_" + tb.name))
            self.dmabufs.append(tb)
        tb.semv += 16

        def fn(eng, out_ap=out_ap, in_ap=in_ap, slow=slow):
            if slow:
                return eng.dma_start(out=out_ap, in_=in_ap, allow_slow_non_contiguous=True)
            return eng.dma_start(out=out_ap, in_=in_ap)

        self.q[q].append((waits, fn, tb))
        for b in writes:
            b.w = (tb, tb.semv)
            b.r = []
        for b in reads:
            b.r.append((tb, tb.semv))

    def barrier(self):
        snap = {e: self.cnt[e] for e in self.ENG}
        dsnap = [(b, b.semv) for b in self.dmabufs]
        for e in self.ENG:
            waits = []
            for f in self.ENG:
                if f != e and snap[f] > self.waited[e].get(f, 0):
                    waits.append((f, snap[f]))
                    self.waited[e][f] = snap[f]
            for b, v in dsnap:
                if v > self.waited[e].get(b, 0):
                    waits.append((b, v))
                    self.waited[e][b] = v
            self.q[e].append((waits, None, None))

    def emit(self, block):
        sched = self

        def run(engname, eng):
            for waits, fn, tb in sched.q[engname]:
                for key, val in waits:
                    sem = sched.esem[key] if isinstance(key, str) else key.sem
                    eng.wait_ge(sem, val)
                if fn is None:
                    continue
                ins = fn(eng)
                if tb is None:
                    ins.then_inc(sched.esem[engname], 1)
                else:
                    ins.then_inc(tb.sem, 16)
            if engname == "sp":
                for b in sched.dmabufs:
                    eng.wait_ge(b.sem, b.semv)
                for e in ("pe", "act", "dve", "pool"):
                    if sched.cnt[e]:
                        eng.wait_ge(sched.esem[e], sched.cnt[e])

        @block.tensor
        def _(e):
            run("pe", e)

        @block.scalar
        def _(e):
            run("act", e)

        @block.vector
        def _(e):
            run("dve", e)

        @block.gpsimd
        def _(e):
            run("pool", e)

        @block.sync
        def _(e):
            run("sp", e)


IN_SPECS = [
    ("xp", [2048, 1024]), ("xs", [64, 1024]), ("cvec", [17, 1024]),
    ("conv0", [16, 3, 512]), ("h0", [16, 512]), ("s5r0", [16, 2048]), ("s5i0", [16, 2048]),
    ("ada_w", [1024, 6144]), ("ada_b", [48, 128]), ("in_proj", [1024, 1536]),
    ("conv_w", [16, 128]), ("conv_b", [4, 128]), ("rg_wr", [8, 64, 64]), ("rg_br", [4, 128]),
    ("rg_wi", [8, 64, 64]), ("rg_bi", [4, 128]), ("rg_lam", [4, 128]),
    ("s5_a_re", [32, 64]), ("s5_a_im", [32, 64]), ("s5_log_dt", [1, 32]),
    ("s5_b_re", [32, 64, 16]), ("s5_b_im", [32, 64, 16]), ("s5_c_re", [512, 64]), ("s5_c_im", [512, 64]),
    ("s5_d", [32, 16]), ("glu_w", [512, 512]), ("glu_b", [4, 128]), ("out_proj", [1024, 1024]),
    ("ln1_g", [8, 128]), ("ln1_b", [8, 128]), ("mlp_w1", [1024, 4096]), ("mlp_b1", [32, 128]),
    ("mlp_w2", [4096, 1024]), ("mlp_b2", [8, 128]), ("ln2_g", [8, 128]), ("ln2_b", [8, 128]),
    ("ident", [128, 128]), ("trii", [128, 128]), ("maskt", [128, 128]), ("diagsel", [128, 128]),
]
OUT_SPECS = [
    ("yp", [2048, 1024]), ("ys", [64, 1024]), ("convp", [12, 128]), ("hp", [4, 128]),
    ("s5rp", [32, 64]), ("s5ip", [32, 64]), ("convs", [48, 512]), ("hs", [16, 512]),
    ("s5rs", [16, 2048]), ("s5is", [16, 2048]),
]

PCOLS = {}
_c = 0
for _n, _k in [("ada_b", 48), ("ln1_g", 8), ("ln1_b", 8), ("ln2_g", 8), ("ln2_b", 8), ("mlp_b2", 8),
               ("mlp_b1", 32), ("conv_w", 16), ("conv_b", 4), ("rg_br", 4), ("rg_bi", 4), ("rg_lam", 4),
               ("glu_b", 4)]:
    PCOLS[_n] = (_c, _k)
    _c += _k
NPCOL = 160

CFG = {"sample": True, "nsuper": 2}


def build_nc():
    nc = bass.Bass("TRN2", target_bir_lowering=False)
    D = {}
    for n, s in IN_SPECS:
        D[n] = nc.dram_tensor(n, list(s), F32, kind="ExternalInput").ap()
    for n, s in OUT_SPECS:
        D[n] = nc.dram_tensor(n, list(s), F32, kind="ExternalOutput").ap()

    with ExitStack() as st:
        S = Sched(nc, st)

        class TT:
            def __init__(self, name, shape, dt=F32, psum=False):
                if psum:
                    self.t = st.enter_context(nc.psum_tensor(name, shape, dt))
                else:
                    self.t = st.enter_context(nc.sbuf_tensor(name, shape, dt))
                self.b = Buf(name)

            def __getitem__(self, k):
                return self.t[k]

        class VW:
            def __init__(self, name, ap):
                self.b = Buf(name)
                self.ap_ = ap

            def __getitem__(self, k):
                return self.ap_[k]

        def carve(arena, name, off, shape, dt=F32, parts=128):
            n = 1
            for d in shape[1:]:
                n *= d
            nf = n if dt == F32 else (n + 1) // 2
            ap = arena.t[0:parts, off:off + nf]
            if dt != F32:
                ap = ap.bitcast(dt)[:, 0:n]
            if len(shape) > 2:
                names = " ".join("d%d" % i for i in range(len(shape) - 1))
                kw = {"d%d" % i: shape[i + 1] for i in range(len(shape) - 1)}
                ap = ap.rearrange("p (%s) -> p %s" % (names, names), **kw)
            return VW(name, ap)

        PB = [TT("pb%d" % i, [128, 512], F32, psum=True) for i in range(8)]
        for i in range(8):
            PB[i].b.excl = True
        PBb = [PB[i].t[:].bitcast(BF16) for i in range(8)]
        bank_ctr = [0, 0]
        cur_pool = [None]

        def nb():
            if cur_pool[0] is None:
                i = bank_ctr[0] % 8
                bank_ctr[0] += 1
                return i
            if cur_pool[0] == 0:
                i = bank_ctr[0] % 5
                bank_ctr[0] += 1
                return i
            i = 5 + bank_ctr[1] % 3
            bank_ctr[1] += 1
            return i

        ident = TT("identt", [128, 128]); identb = TT("identb", [128, 128], BF16)
        triIb = TT("triIb", [128, 128], BF16); maskT = TT("maskTt", [128, 128]); diagsel = TT("diagselt", [128, 128])
        onesb = TT("onesb", [128, 128], BF16)
        PT = TT("PT", [128, NPCOL])
        modT = TT("modT", [128, 48, 17])
        GG2 = TT("GG2", [128, 8, 17]); BB2 = TT("BB2", [128, 8, 17])
        AG1 = TT("AG1", [128, 8]); AB1 = TT("AB1", [128, 8, 17])
        gluW = TT("gluW", [128, 4, 512], BF16)
        rgW = TT("rgW", [128, 4, 2, 128], BF16)
        cp1 = TT("cp1", [128, 4]); cp2 = TT("cp2", [128, 4])
        hstate = TT("hstate", [128, 4]); xhist = TT("xhist", [128, 4, 3])
        carryS = TT("carryS", [128, 32], BF16)
        P4 = [TT("P4_%d" % i, [128, 32]) for i in range(4)]
        P8 = [TT("P8_%d" % i, [128, 32]) for i in range(4)]
        hT = TT("hT", [128, 8, ST], BF16); axT = TT("axT", [128, 8, ST]); gyT = TT("gyT", [128, 4, ST], BF16)
        f1a = TT("f1a", [128, 2, NT], BF16)
        NWB = 4
        wblk = [TT("wblk%d" % i, [128, 8, 256], BF16) for i in range(NWB)]
        scT = TT("scT", [128, 8, 17], BF16)
        h0T = TT("h0T", [128, 4, 16])

        FA = TT("FA", [128, 20432])
        BA = TT("BA", [128, 6144])

        def P(name, j=0):
            c0, k = PCOLS[name]
            return PT[:, c0 + j:c0 + j + 1]

        t1 = carve(FA, "t1", 0, [128, 4, NT]); t2 = carve(FA, "t2", 2048, [128, 4, NT])
        t3 = carve(FA, "t3", 4096, [128, 4, NT]); t4 = carve(FA, "t4", 6144, [128, 4, NT])
        r1 = carve(FA, "r1", 8192, [128, 8, NT]); gT = carve(FA, "gT", 12288, [128, 4, NT])
        xaT = carve(FA, "xaT", 14336, [128, 4, 3 + NT]); xaS = carve(FA, "xaS", 16400, [128, 4, 16, 7])
        mean = carve(FA, "mean", 16848, [128, NT]); rstd = carve(FA, "rstd", 17360, [128, NT])
        yout = carve(FA, "yout", 18896, [128, 1024])
        tmpx = carve(FA, "tmpx", 16400, [128, 64]); tmpx.b = xaS.b
        deferred_alias = []

        class _One:
            def __init__(self, b):
                self.b = b
        for v_ in (t1, t2, t3, t4, gT, xaT):
            v_.bs = getattr(v_, "bs", None) or [Buf(v_.b.name + "_%d" % i) for i in range(4)]
            v_.c = [_One(b) for b in v_.bs]
        r1a = carve(FA, "r1a", 4096, [128, 8, NT]); r1a.bs = t3.bs + t4.bs; r1a.c = t3.c + t4.c
        r1.bs = [Buf("r1_%d" % i) for i in range(8)]; r1.c = [_One(b) for b in r1.bs]
        rb2 = carve(FA, "rb2", 0, [128, 8, NT], BF16); rb2.bs = t1.bs; rb2.c = [t1.c[i // 2] for i in range(8)]
        rsq2 = carve(FA, "rsq2", 2048, [128, 8, NT], BF16); rsq2.bs = t2.bs; rsq2.c = [t2.c[i // 2] for i in range(8)]
        mean2 = carve(FA, "mean2", 12288, [128, NT]); mean2.bs = gT.bs
        deferred_alias.append(lambda: (setattr(ycat, "bs", gT.bs), setattr(ycat, "c", [gT.c[i // 2] for i in range(8)])))
        rstd2 = carve(FA, "rstd2", 12800, [128, NT]); rstd2.bs = gT.bs
        xin = TT("xin", [128, 1024])
        csb = carve(FA, "csb", 10240, [17, 1024], parts=17)
        msb = carve(FA, "msb", 8704, [17, 256], parts=17)
        prow = [carve(FA, "prow%d" % i, 8960 + 128 * i, [80, 128], parts=80) for i in range(2)]
        sm = carve(FA, "sm", 19920, [48, 512], parts=48)
        scr = nc.dram_tensor("scr_tab", [4, 128, 2048], BF16).ap()
        scrB = [Buf("scr%d" % i) for i in range(4)]
        TSt = [carve(BA, "TSt%d" % i, 1024 * i, [128, 32, 64], BF16) for i in range(4)]
        TFr, TFi, TIr, TIi = [carve(FA, "TAB%d" % i, 7184 + 1024 * i, [128, 32, 64], BF16) for i in range(4)]
        Wmov = carve(FA, "Wmov", 13328, [128, 32, 256], BF16)
        WA = carve(FA, "WA", 17424, [128, 32, 128], BF16)
        scrW = nc.dram_tensor("scr_w", [128, 6144], F32).ap()
        scrWB = Buf("scrW")
        f1b = carve(BA, "f1b", 0, [128, 16, NT], BF16); rb = carve(BA, "rb", 0, [128, 8, NT], BF16)
        rsq = carve(BA, "rsq", 2048, [128, 8, NT], BF16); rb.b = f1b.b; rsq.b = f1b.b
        h2T = carve(BA, "h2T", 4096, [128, 8, NT], BF16)
        ycat = carve(FA, "ycat", 12288, [128, 8, NT], BF16)
        xcb = carve(FA, "xcb", 17872, [128, 4, NT], BF16)
        xcb.bs = [Buf("xcb_%d" % i) for i in range(4)]
        xcb.c = [_One(b) for b in xcb.bs]
        f1b.bs = [Buf("f1b_%d" % i) for i in range(16)]; f1b.c = [_One(b) for b in f1b.bs]
        rb.bs = f1b.bs[0:8]; rb.c = f1b.c[0:8]
        rsq.bs = f1b.bs[8:16]; rsq.c = f1b.c[8:16]
        h2T.bs = [Buf("h2T_%d" % i) for i in range(8)]; h2T.c = [_One(b) for b in h2T.bs]
        for fn_ in deferred_alias:
            fn_()
        M1 = carve(BA, "M1", 0, [128, 32, 2, 64], BF16); M2 = carve(BA, "M2", 2048, [128, 32, 2, 64], BF16)
        Zc = carve(BA, "Zc", 4096, [128, 32, 8, 16], BF16); U = carve(FA, "U", 11280, [128, 32, 128], BF16)
        Hcm = carve(FA, "Hcm", 0, [128, 32, 2, 64], BF16); Ysb = carve(FA, "Ysb", 2048, [128, 4, 8, 8, 16], BF16)
        HTx = carve(FA, "HTx", 4096, [128, 32, 129], BF16)
        tmpA = carve(FA, "tmpA", 6160, [128, 4, 2, 64]); tmpB = carve(FA, "tmpB", 6672, [128, 4, 2, 64])
        s0 = carve(FA, "s0", 7184, [16, 32, 2, 64], parts=16)
        SL = [carve(BA, "SL%d" % i, 0 + 1024 * i, [16, 32, 64], BF16, parts=16) for i in range(4)]
        Fs_r = carve(FA, "Fs_r", 0, [128, 32, 128]); Fs_i = carve(FA, "Fs_i", 4096, [128, 32, 128])
        ctA = carve(FA, "ctA", 8192, [128, 32, 64])
        MBN = carve(BA, "MBN", 0, [128, 32, 8, 16], BF16); ENAT = carve(BA, "ENAT", 2048, [128, 32, 8, 16], BF16)
        cT1 = carve(FA, "cT1", 3072, [128, 32, 16]); cT2 = carve(FA, "cT2", 3584, [128, 32, 16])
        pu1 = carve(FA, "pu1", 4096, [128, 32, 16]); pu2 = carve(FA, "pu2", 4608, [128, 32, 16])
        pu3 = carve(FA, "pu3", 5120, [128, 32, 16]); pu4 = carve(FA, "pu4", 5632, [128, 32, 16])
        tmpT = carve(FA, "tmpT", 6144, [128, 4, 128])
        PWr = carve(FA, "PWr", 6656, [128, 9, 32]); PWi = carve(FA, "PWi", 6944, [128, 9, 32])
        NWr = carve(FA, "NWr", 7232, [128, 9, 32]); NWi = carve(FA, "NWi", 7520, [128, 9, 32])
        tb16 = carve(FA, "tb16", 6160, [128, 32, 16])
        sc1 = carve(FA, "sc1", 6160, [16, 4, 2, 64], parts=16); sc1.b = tb16.b
        sc2 = carve(FA, "sc2", 6672, [16, 4, 2, 64], parts=16)

        wctr = [0, 0]

        wblkB = [TT("wblkB%d" % i, [128, 8, 256], BF16) for i in range(2)]

        def next_wblk():
            if cur_pool[0] == 1:
                w = wblkB[wctr[1] % 2]
                wctr[1] += 1
                return w
            w = wblk[wctr[0] % NWB]
            wctr[0] += 1
            return w

        def load_wblock(src2d, w):
            S.dma("pool", w[:], src2d.rearrange("(k p) c -> p k c", p=128), writes=[w.b])

        def _bufs(lst):
            out = []
            for x in lst:
                if hasattr(x, "bs"):
                    out.extend(x.bs)
                else:
                    out.append(x.b)
            return out

        A_ = lambda eng, fn, r, w: S.op(eng, fn, reads=_bufs(r), writes=_bufs(w))

        S.dma("sp", ident[:], D["ident"], writes=[ident.b])
        S.dma("sp", maskT[:], D["maskt"], writes=[maskT.b])
        S.dma("sp", diagsel[:], D["diagsel"], writes=[diagsel.b])
        S.dma("pool", triIb[:], D["trii"], writes=[triIb.b])
        A_("dve", lambda e: e.tensor_copy(out=identb[:], in_=ident[:]), [ident], [identb])
        A_("dve", lambda e: e.memset(onesb[:], 1.0 / 1024.0), [], [onesb])
        for half in range(2):
            lst = []
            for n in PCOLS:
                c0, k = PCOLS[n]
                if (c0 < 80) == (half == 0):
                    lst.append((n, c0 - 80 * half, k))
            if half == 1:
                A_("dve", lambda e: e.memset(prow[1][:], 0.0), [], [prow[1]])
            for n, cc, k in lst:
                S.dma("sp", prow[half][cc:cc + k, :], D[n], writes=[prow[half].b])
            A_("pe", lambda e, half=half: e.transpose(PB[0][:, 0:80], prow[half][:], ident[0:80, 0:80]), [prow[half], ident], [PB[0]])
            A_("dve", lambda e, half=half: e.tensor_copy(out=PT[:, 80 * half:80 * half + 80], in_=PB[0][:, 0:80]), [PB[0]], [PT])
        S.dma("sp", csb[:], D["cvec"], writes=[csb.b])
        for k in range(8):
            A_("pe", lambda e, k=k: e.transpose(PB[1][:, k * 17:(k + 1) * 17], csb[0:17, k * 128:(k + 1) * 128], ident[0:17, 0:17]), [csb, ident], [PB[1]])
        A_("act", lambda e: e.activation(out=scT[:].rearrange("p k r -> p (k r)"), in_=PB[1][:, 0:136], func=AF.Silu), [PB[1]], [scT])
        arS, aiS, dtS, v1, v2, v3, v4, abr, abi, qr, qi, w5, w6 = [carve(FA, n, 8192 + 32 * i_, [128, 32]) for i_, n in enumerate("arS aiS dtS v1 v2 v3 v4 abr abi qr qi w5 w6".split())]
        bSr = carve(FA, "bSr", 0, [128, 32, 16]); bSi = carve(FA, "bSi", 512, [128, 32, 16])
        bbr = carve(FA, "bbr", 1024, [128, 32, 16]); bbi = carve(FA, "bbi", 1536, [128, 32, 16])
        cN = carve(FA, "cN", 2048, [128, 4, 2, 64]); cN2 = carve(FA, "cN2", 2560, [128, 4, 2, 64])
        dK = TT("dK", [128, 32]); hp_ = TT("hp_", [128, 1])
        for hf in range(2):
            ps_ = slice(hf * 64, hf * 64 + 64)
            for gq in range(4):
                S.dma("sp", arS[ps_, gq * 8:(gq + 1) * 8], D["s5_a_re"][gq * 8:(gq + 1) * 8].rearrange("g n -> n g"), writes=[arS.b], slow=True)
                S.dma("sp", aiS[ps_, gq * 8:(gq + 1) * 8], D["s5_a_im"][gq * 8:(gq + 1) * 8].rearrange("g n -> n g"), writes=[aiS.b], slow=True)
            for gq in range(4):
                S.dma("sp", bSr[ps_, gq * 8:(gq + 1) * 8, :], D["s5_b_re"][gq * 8:(gq + 1) * 8].rearrange("g n k -> n g k"), writes=[bSr.b])
                S.dma("sp", bSi[ps_, gq * 8:(gq + 1) * 8, :], D["s5_b_im"][gq * 8:(gq + 1) * 8].rearrange("g n k -> n g k"), writes=[bSi.b])
        for k in range(8):
            S.dma("sp", dK[k * 16:(k + 1) * 16, :], D["s5_d"].rearrange("g i -> i g"), writes=[dK.b], slow=True)
        S.dma("sp", dtS[:], D["s5_log_dt"].broadcast_to([128, 32]), writes=[dtS.b])
        S.dma("sp", cN[:, :, 0, :], D["s5_c_re"].rearrange("(t q) n -> q t n", q=128), writes=[cN.b])
        S.dma("sp", cN[:, :, 1, :], D["s5_c_im"].rearrange("(t q) n -> q t n", q=128), writes=[cN.b])
        S.dma("sp", cN2[:, :, 0, :], D["s5_c_im"].rearrange("(t q) n -> q t n", q=128), writes=[cN2.b])
        S.dma("sp", cN2[:, :, 1, :], D["s5_c_re"].rearrange("(t q) n -> q t n", q=128), writes=[cN2.b])
        TT_ = lambda o, a, b, op: (lambda e: e.tensor_tensor(out=o, in0=a, in1=b, op=op))
        A_("act", lambda e: e.activation(out=dtS[:], in_=dtS[:], func=AF.Exp), [dtS], [dtS])
        A_("dve", TT_(v1[:], dtS[:], arS[:], ALU.mult), [dtS, arS], [v1])
        A_("dve", TT_(v2[:], dtS[:], aiS[:], ALU.mult), [dtS, aiS], [v2])
        A_("act", lambda e: e.activation(out=v1[:], in_=v1[:], func=AF.Exp), [v1], [v1])
        A_("dve", lambda e: e.memset(hp_[:], float(np.pi / 2)), [], [hp_])
        A_("act", lambda e: e.activation(out=v3[:], in_=v2[:], func=AF.Sin, scale=1.0 / 16.0), [v2], [v3])
        A_("act", lambda e: e.activation(out=v4[:], in_=v2[:], func=AF.Sin, scale=-1.0 / 16.0, bias=hp_[:, 0:1]), [v2, hp_], [v4])
        for _ in range(4):
            A_("dve", TT_(w5[:], v3[:], v4[:], ALU.mult), [v3, v4], [w5])
            A_("dve", TT_(w6[:], v3[:], v3[:], ALU.mult), [v3], [w6])
            A_("dve", TT_(v4[:], v4[:], v4[:], ALU.mult), [v4], [v4])
            A_("dve", TT_(v4[:], v4[:], w6[:], ALU.subtract), [v4, w6], [v4])
            A_("dve", lambda e: e.tensor_scalar(out=v3[:], in0=w5[:], scalar1=2.0, scalar2=None, op0=ALU.mult), [w5], [v3])
        A_("dve", TT_(abr[:], v1[:], v4[:], ALU.mult), [v1, v4], [abr])
        A_("dve", TT_(abi[:], v1[:], v3[:], ALU.mult), [v1, v3], [abi])
        A_("dve", TT_(v1[:], arS[:], arS[:], ALU.mult), [arS], [v1])
        A_("dve", TT_(v2[:], aiS[:], aiS[:], ALU.mult), [aiS], [v2])
        A_("dve", TT_(v1[:], v1[:], v2[:], ALU.add), [v1, v2], [v1])
        A_("dve", lambda e: e.reciprocal(out=v1[:], in_=v1[:]), [v1], [v1])
        A_("dve", lambda e: e.tensor_scalar(out=v2[:], in0=abr[:], scalar1=-1.0, scalar2=None, op0=ALU.add), [abr], [v2])
        A_("dve", TT_(v3[:], v2[:], arS[:], ALU.mult), [v2, arS], [v3])
        A_("dve", TT_(v4[:], abi[:], aiS[:], ALU.mult), [abi, aiS], [v4])
        A_("dve", TT_(v3[:], v3[:], v4[:], ALU.add), [v3, v4], [v3])
        A_("dve", TT_(qr[:], v3[:], v1[:], ALU.mult), [v3, v1], [qr])
        A_("dve", TT_(v3[:], abi[:], arS[:], ALU.mult), [abi, arS], [v3])
        A_("dve", TT_(v4[:], v2[:], aiS[:], ALU.mult), [v2, aiS], [v4])
        A_("dve", TT_(v3[:], v3[:], v4[:], ALU.subtract), [v3, v4], [v3])
        A_("dve", TT_(qi[:], v3[:], v1[:], ALU.mult), [v3, v1], [qi])

        def cmul(or_, oi_, ar, ai, br, bi, tmp, rd, wr_r, wr_i, tmpbuf, eng="dve"):
            S.op(eng, lambda e: e.tensor_tensor(out=or_(), in0=ar(), in1=br(), op=ALU.mult), reads=rd, writes=[wr_r])
            S.op(eng, lambda e: e.tensor_tensor(out=tmp(), in0=ai(), in1=bi(), op=ALU.mult), reads=rd, writes=[tmpbuf])
            S.op(eng, lambda e: e.tensor_tensor(out=or_(), in0=or_(), in1=tmp(), op=ALU.subtract), reads=[wr_r, tmpbuf], writes=[wr_r])
            S.op(eng, lambda e: e.tensor_tensor(out=oi_(), in0=ar(), in1=bi(), op=ALU.mult), reads=rd, writes=[wr_i])
            S.op(eng, lambda e: e.tensor_tensor(out=tmp(), in0=ai(), in1=br(), op=ALU.mult), reads=rd + [wr_r], writes=[tmpbuf])
            S.op(eng, lambda e: e.tensor_tensor(out=oi_(), in0=oi_(), in1=tmp(), op=ALU.add), reads=[wr_i, tmpbuf], writes=[wr_i])

        b3 = lambda ap: ap.unsqueeze(2).broadcast_to([128, 32, 16])
        cmul(lambda: bbr[:], lambda: bbi[:], lambda: b3(qr[:]), lambda: b3(qi[:]), lambda: bSr[:], lambda: bSi[:], lambda: pu1[:],
             [qr.b, qi.b, bSr.b, bSi.b], bbr.b, bbi.b, pu1.b)
        A_("dve", lambda e: e.tensor_tensor(out=v1[:], in0=abr[:], in1=abr[:], op=ALU.mult), [abr], [v1])
        A_("dve", lambda e: e.tensor_tensor(out=v2[:], in0=abi[:], in1=abi[:], op=ALU.mult), [abi], [v2])
        A_("dve", lambda e: e.tensor_tensor(out=v1[:], in0=v1[:], in1=v2[:], op=ALU.add), [v1, v2], [v1])
        A_("dve", lambda e: e.reciprocal(out=v1[:], in_=v1[:]), [v1], [v1])
        tmpw = lambda m: pu1[:].rearrange("p g k -> p (g k)")[:, 0:m * 32].rearrange("p (a b) -> p a b", b=32)
        A_("dve", lambda e: e.memset(PWr[:, 0, :], 1.0), [], [PWr])
        A_("dve", lambda e: e.memset(PWi[:, 0, :], 0.0), [], [PWi])
        A_("dve", lambda e: e.tensor_copy(out=PWr[:, 1, :], in_=abr[:]), [abr], [PWr])
        A_("dve", lambda e: e.tensor_copy(out=PWi[:, 1, :], in_=abi[:]), [abi], [PWi])
        for m in (1, 2, 4):
            bcm = lambda ap, m=m: ap.unsqueeze(1).broadcast_to([128, m, 32])
            cmul(lambda m=m: PWr[:, 1 + m:1 + 2 * m, :], lambda m=m: PWi[:, 1 + m:1 + 2 * m, :],
                 lambda m=m: PWr[:, 1:1 + m, :], lambda m=m: PWi[:, 1:1 + m, :],
                 lambda m=m, bcm=bcm: bcm(PWr[:, m, :]), lambda m=m, bcm=bcm: bcm(PWi[:, m, :]),
                 lambda m=m: tmpw(m), [PWr.b, PWi.b], PWr.b, PWi.b, pu1.b)
        A_("dve", lambda e: e.memset(NWr[:, 8, :], 1.0), [], [NWr])
        A_("dve", lambda e: e.memset(NWi[:, 8, :], 0.0), [], [NWi])
        A_("dve", lambda e: e.tensor_tensor(out=NWr[:, 7, :], in0=abr[:], in1=v1[:], op=ALU.mult), [abr, v1], [NWr])
        A_("dve", lambda e: e.scalar_tensor_tensor(out=NWi[:, 7, :], in0=abi[:], scalar=-1.0, in1=v1[:], op0=ALU.mult, op1=ALU.mult), [abi, v1], [NWi])
        for m in (1, 2, 4):
            bcm = lambda ap, m=m: ap.unsqueeze(1).broadcast_to([128, m, 32])
            cmul(lambda m=m: NWr[:, 8 - 2 * m:8 - m, :], lambda m=m: NWi[:, 8 - 2 * m:8 - m, :],
                 lambda m=m: NWr[:, 8 - m:8, :], lambda m=m: NWi[:, 8 - m:8, :],
                 lambda m=m, bcm=bcm: bcm(NWr[:, 8 - m, :]), lambda m=m, bcm=bcm: bcm(NWi[:, 8 - m, :]),
                 lambda m=m: tmpw(m), [NWr.b, NWi.b], NWr.b, NWi.b, pu1.b)
        for (src, dstc) in ((cN, cT1), (cN2, cT2)):
            bk = nb()
            for t in range(4):
                A_("pe", lambda e, src=src, t=t, bk=bk: e.transpose(PB[bk][:, t * 128:(t + 1) * 128], src[:, t, :, :].rearrange("p r n -> p (r n)"), ident[:]), [src, ident], [PB[bk]])
            A_("dve", lambda e, dstc=dstc, bk=bk: e.tensor_copy(out=dstc[:].rearrange("p g k -> p (g k)"), in_=PB[bk][:, :]), [PB[bk]], [dstc])
        for nbk in range(24):
            w = next_wblk()
            load_wblock(D["ada_w"][:, nbk * 256:(nbk + 1) * 256], w)
            bk = nb()
            for k in range(8):
                A_("pe", lambda e, k=k, w=w, bk=bk: e.matmul(PB[bk][0:17, 0:256], lhsT=scT[:, k, :], rhs=w[:, k, :], start=(k == 0), stop=(k == 7)), [scT, w], [PB[bk]])
            A_("act", lambda e, bk=bk: e.copy(out=msb[:], in_=PB[bk][0:17, 0:256]), [PB[bk]], [msb])
            bk2 = nb()
            for q in range(2):
                A_("pe", lambda e, q=q, bk2=bk2: e.transpose(PB[bk2][:, q * 17:(q + 1) * 17], msb[0:17, q * 128:(q + 1) * 128], ident[0:17, 0:17]), [msb, ident], [PB[bk2]])
            A_("act", lambda e, nbk=nbk, bk2=bk2: e.activation(out=modT[:, nbk * 2:(nbk + 1) * 2, :].rearrange("p k r -> p (k r)"), in_=PB[bk2][:, 0:34], func=AF.Copy), [PB[bk2]], [modT])
        vA = carve(FA, "vA", 4096, [128, 8, 8, 16]); vB = carve(FA, "vB", 5120, [128, 8, 8, 16]); vC = carve(FA, "vC", 9216, [128, 8, 8, 16])
        vA.b = pu1.b
        WA4 = WA[:].rearrange("p g (j k) -> p g j k", k=16)
        for gp in range(4):
            gs = slice(gp * 8, gp * 8 + 8)
            bk_ = lambda X, lo, gs=gs: X[:, lo:lo + 8, gs].rearrange("p k g -> p g k").unsqueeze(3).broadcast_to([128, 8, 8, 16])
            bb_ = lambda X, gs=gs: X[:, gs, :].unsqueeze(2).broadcast_to([128, 8, 8, 16])
            for (dst, Xr, Xi) in ((MBN, PWr, PWi), (ENAT, NWr, NWi)):
                A_("dve", lambda e, Xr=Xr, bk_=bk_, bb_=bb_: e.tensor_tensor(out=vA[:], in0=bb_(bbr), in1=bk_(Xr, 0), op=ALU.mult), [bbr, Xr], [vA])
                A_("dve", lambda e, Xi=Xi, bk_=bk_, bb_=bb_: e.tensor_tensor(out=vB[:], in0=bb_(bbi), in1=bk_(Xi, 0), op=ALU.mult), [bbi, Xi], [vB])
                A_("dve", lambda e: e.tensor_tensor(out=vA[:], in0=vA[:], in1=vB[:], op=ALU.subtract), [vA, vB], [vA])
                A_("dve", lambda e, Xr=Xr, bk_=bk_, bb_=bb_: e.tensor_tensor(out=vC[:], in0=bb_(bbi), in1=bk_(Xr, 0), op=ALU.mult), [bbi, Xr], [vC])
                A_("dve", lambda e, Xi=Xi, bk_=bk_, bb_=bb_: e.tensor_tensor(out=vB[:], in0=bb_(bbr), in1=bk_(Xi, 0), op=ALU.mult), [bbr, Xi], [vB])
                A_("dve", lambda e: e.tensor_tensor(out=vC[:], in0=vC[:], in1=vB[:], op=ALU.add), [vC, vB], [vC])
                A_("dve", lambda e, dst=dst, gs=gs: e.tensor_copy(out=dst[0:64, gs], in_=vA[0:64]), [vA], [dst])
                A_("dve", lambda e, dst=dst, gs=gs: e.tensor_copy(out=dst[64:128, gs], in_=vC[64:128]), [vC], [dst])
            A_("dve", lambda e, bk_=bk_, bb_=bb_: e.tensor_tensor(out=vA[:], in0=bb_(cT1), in1=bk_(PWr, 1), op=ALU.mult), [cT1, PWr], [vA])
            A_("dve", lambda e, bk_=bk_, bb_=bb_: e.tensor_tensor(out=vB[:], in0=bb_(cT2), in1=bk_(PWi, 1), op=ALU.mult), [cT2, PWi], [vB])
            A_("dve", lambda e, gs=gs: e.tensor_tensor(out=WA4[0:64, gs], in0=vA[0:64], in1=vB[0:64], op=ALU.subtract), [vA, vB], [WA])
            A_("dve", lambda e, gs=gs: e.scalar_tensor_tensor(out=WA4[64:128, gs], in0=vA[64:128], scalar=-1.0, in1=vB[64:128], op0=ALU.mult, op1=ALU.subtract), [vA, vB], [WA])
        for g0 in range(0, 32, 4):
            bk = nb()
            for gl in range(4):
                g = g0 + gl
                A_("pe", lambda e, g=g, gl=gl, bk=bk: e.matmul(PB[bk][:, gl * 128:(gl + 1) * 128], lhsT=ENAT[:, g, :, :].rearrange("p k i -> p (k i)"), rhs=WA[:, g, :], start=True, stop=True), [ENAT, WA], [PB[bk]])
            A_("dve", lambda e, bk=bk: e.tensor_tensor(out=tmpT[:], in0=PB[bk][:, :].rearrange("p (g c) -> p g c", g=4), in1=maskT[:].unsqueeze(1).broadcast_to([128, 4, 128]), op=ALU.mult), [PB[bk], maskT], [tmpT])
            for gl in range(4):
                g = g0 + gl
                A_("dve", lambda e, g=g, gl=gl: e.scalar_tensor_tensor(out=Wmov[:, g, 0:128], in0=diagsel[:], scalar=dK[:, g:g + 1], in1=tmpT[:, gl, :], op0=ALU.mult, op1=ALU.add), [diagsel, dK, tmpT], [Wmov])
        for g0 in range(0, 32, 8):
            bk = nb()
            for gl in range(8):
                g = g0 + gl
                A_("pe", lambda e, g=g, gl=gl, bk=bk: e.transpose(PBb[bk][:, gl * 128:(gl + 1) * 128], MBN[:, g, :, :].rearrange("p k i -> p (k i)"), identb[:]), [MBN, identb], [PB[bk]])
            A_("act", lambda e, g0=g0, bk=bk: e.copy(out=Wmov[:, g0:g0 + 8, 128:256], in_=PBb[bk][:, :].rearrange("p (g c) -> p g c", c=128)), [PB[bk]], [Wmov])
        S.dma("sp", scrW, FA[:, 13328:19472], reads=[Wmov.b, WA.b], writes=[scrWB])
        for i_, (X, e_) in enumerate(((PWr, 4), (PWi, 4), (NWr, 4), (NWi, 4))):
            A_("dve", lambda e, X=X, e_=e_, i_=i_: e.tensor_copy(out=P4[i_][:], in_=X[:, e_, :]), [X], [P4[i_]])
        for i_, (X, e_) in enumerate(((PWr, 8), (PWi, 8), (NWr, 0), (NWi, 0))):
            A_("dve", lambda e, X=X, e_=e_, i_=i_: e.tensor_copy(out=P8[i_][:], in_=X[:, e_, :]), [X], [P8[i_]])
        c0, _ = PCOLS["ada_b"]
        A_("dve", lambda e: e.tensor_tensor(out=modT[:], in0=modT[:], in1=PT[:, c0:c0 + 48].unsqueeze(2).broadcast_to([128, 48, 17]), op=ALU.add), [modT, PT], [modT])
        for j in (1, 4):
            A_("dve", lambda e, j=j: e.tensor_scalar(out=modT[:, j * 8:(j + 1) * 8, :], in0=modT[:, j * 8:(j + 1) * 8, :], scalar1=1.0, scalar2=None, op0=ALU.add), [modT], [modT])
        SH1, A1, G1, SH2, A2, G2 = [lambda ft, j=j: modT[:, j * 8 + ft, :] for j in range(6)]
        g1c, _ = PCOLS["ln1_g"]; b1c, _ = PCOLS["ln1_b"]; mb2c, _ = PCOLS["mlp_b2"]
        bc8 = lambda c: PT[:, c:c + 8].unsqueeze(2).broadcast_to([128, 8, 17])
        A_("dve", lambda e: e.tensor_tensor(out=GG2[:], in0=modT[:, 32:40, :], in1=bc8(g1c), op=ALU.mult), [modT, PT], [GG2])
        A_("dve", lambda e: e.tensor_tensor(out=BB2[:], in0=modT[:, 32:40, :], in1=bc8(b1c), op=ALU.mult), [modT, PT], [BB2])
        A_("dve", lambda e: e.tensor_tensor(out=BB2[:], in0=BB2[:], in1=modT[:, 24:32, :], op=ALU.add), [BB2, modT], [BB2])
        A_("dve", lambda e: e.tensor_scalar(out=AG1[:], in0=PT[:, g1c:g1c + 8], scalar1=ALPHA, scalar2=None, op0=ALU.mult), [PT], [AG1])
        A_("dve", lambda e: e.tensor_tensor(out=AB1[:], in0=modT[:, 40:48, :], in1=bc8(mb2c), op=ALU.mult), [modT, PT], [AB1])
        A_("dve", lambda e: e.scalar_tensor_tensor(out=AB1[:], in0=bc8(b1c), scalar=ALPHA, in1=AB1[:], op0=ALU.mult, op1=ALU.add), [AB1, PT], [AB1])
        S.dma("pool", gluW[:], D["glu_w"].rearrange("(k p) c -> p k c", p=128), writes=[gluW.b])
        A_("dve", lambda e: e.memset(rgW[:], 0.0), [], [rgW])
        for ct in range(4):
            for wi, nm in enumerate(("rg_wr", "rg_wi")):
                S.dma("pool", rgW[0:64, ct, wi, 0:64], D[nm][2 * ct], writes=[rgW.b])
                S.dma("pool", rgW[64:128, ct, wi, 64:128], D[nm][2 * ct + 1], writes=[rgW.b])
        lc, _ = PCOLS["rg_lam"]
        A_("act", lambda e: e.activation(out=cp1[:], in_=PT[:, lc:lc + 4], func=AF.Exp, scale=-1.0), [PT], [cp1])
        A_("act", lambda e: e.activation(out=cp1[:], in_=cp1[:], func=AF.Ln, bias=1.0, scale=1.0), [cp1], [cp1])
        A_("dve", lambda e: e.tensor_scalar(out=cp2[:], in0=cp1[:], scalar1=-16.0, scalar2=None, op0=ALU.mult), [cp1], [cp2])
        A_("dve", lambda e: e.tensor_scalar(out=cp1[:], in0=cp1[:], scalar1=-8.0, scalar2=None, op0=ALU.mult), [cp1], [cp1])
        A_("dve", lambda e: e.memset(hstate[:], 0.0), [], [hstate])
        A_("dve", lambda e: e.memset(xhist[:], 0.0), [], [xhist])
        A_("dve", lambda e: e.memset(carryS[:], 0.0), [], [carryS])

        cwc, _ = PCOLS["conv_w"]; cbc, _ = PCOLS["conv_b"]
        brc, _ = PCOLS["rg_br"]; bic, _ = PCOLS["rg_bi"]; gbc, _ = PCOLS["glu_b"]; mb1c, _ = PCOLS["mlp_b1"]
        l2g, _ = PCOLS["ln2_g"]; l2b, _ = PCOLS["ln2_b"]

        def bc(ap17):
            return ap17[:, 1:17].unsqueeze(2).broadcast_to([128, 16, 4])

        def v3d(ap):
            return ap.rearrange("p (s t) -> p s t", t=4)

        def load_x(tok0, ntok, is_s):
            xsrc = D["xs"] if is_s else D["xp"]
            nsub = (ntok + 127) // 128
            for sb in range(nsub):
                sw = min(128, ntok - sb * 128)
                S.dma("sp", xin[0:sw, :], xsrc[tok0 + sb * 128: tok0 + sb * 128 + sw, :], writes=[xin.b])
                for ft in range(8):
                    bk = nb()
                    A_("pe", lambda e, ft=ft, bk=bk, sw=sw: e.transpose(PB[bk][:, 0:sw], xin[0:sw, ft * 128:(ft + 1) * 128], ident[0:sw, 0:sw]), [xin, ident], [PB[bk]])
                    cs = slice(sb * 128, sb * 128 + sw)
                    if is_s:
                        A_("dve", lambda e, ft=ft, cs=cs, bk=bk, sw=sw: e.tensor_scalar(out=axT[:, ft, cs], in0=PB[bk][:, 0:sw], scalar1=ALPHA, scalar2=None, op0=ALU.mult), [PB[bk]], [axT])
                    else:
                        A_("act", lambda e, ft=ft, cs=cs, bk=bk, sw=sw: e.mul(out=axT[:, ft, cs], in_=PB[bk][:, 0:sw], mul=ALPHA), [PB[bk]], [axT])
                    if not is_s:
                        A_("act", lambda e, ft=ft, cs=cs, bk=bk, sw=sw: e.activation(out=hT[:, ft, cs], in_=PB[bk][:, 0:sw], func=AF.Identity, scale=A1(ft)[:, 0:1], bias=SH1(ft)[:, 0:1]), [PB[bk], modT], [hT])
                    else:
                        A_("dve", lambda e, ft=ft, bk=bk: e.tensor_tensor(out=v3d(tmpx[:, 0:64]), in0=v3d(PB[bk][:, 0:64]), in1=bc(A1(ft)), op=ALU.mult), [PB[bk], modT], [tmpx])
                        A_("dve", lambda e, ft=ft: e.tensor_tensor(out=v3d(hT[:, ft, 0:64]), in0=v3d(tmpx[:, 0:64]), in1=bc(SH1(ft)), op=ALU.add), [tmpx, modT], [hT])

        if CFG["nsuper"] > 0:
            load_x(0, ST, False)
        S.barrier()
        A_("dve", lambda e: e.tensor_copy(out=Fs_r[0:64, :, 0], in_=P8[0][0:64]), [P8[0]], [Fs_r])
        A_("dve", lambda e: e.tensor_copy(out=Fs_i[0:64, :, 0], in_=P8[1][0:64]), [P8[1]], [Fs_i])
        A_("dve", lambda e: e.tensor_copy(out=Fs_r[64:128, :, 0], in_=P8[2][64:128]), [P8[2]], [Fs_r])
        A_("dve", lambda e: e.tensor_copy(out=Fs_i[64:128, :, 0], in_=P8[3][64:128]), [P8[3]], [Fs_i])
        m = 1
        while m < 128:
            bcm = lambda ap, m=m: ap.unsqueeze(2).broadcast_to([128, 32, m])
            cmul(lambda m=m: Fs_r[:, :, m:2 * m], lambda m=m: Fs_i[:, :, m:2 * m],
                 lambda m=m: Fs_r[:, :, 0:m], lambda m=m: Fs_i[:, :, 0:m],
                 lambda m=m, bcm=bcm: bcm(Fs_r[:, :, m - 1]), lambda m=m, bcm=bcm: bcm(Fs_i[:, :, m - 1]),
                 lambda m=m: ctA[:, :, 0:m], [Fs_r.b, Fs_i.b], Fs_r.b, Fs_i.b, ctA.b)
            m *= 2
        for (src, dst, p0) in ((Fs_r, TSt[0], 0), (Fs_i, TSt[1], 0), (Fs_r, TSt[2], 64), (Fs_i, TSt[3], 64)):
            for g0 in range(0, 32, 8):
                bk = nb()
                for gl in range(8):
                    g = g0 + gl
                    A_("pe", lambda e, src=src, g=g, gl=gl, bk=bk, p0=p0: e.transpose(PB[bk][:, gl * 64:(gl + 1) * 64], src[p0:p0 + 64, g, :], ident[p0:p0 + 64, p0:p0 + 64]), [src, ident], [PB[bk]])
                if (g0 // 8) % 2 == 0:
                    A_("act", lambda e, dst=dst, g0=g0, bk=bk: e.copy(out=dst[:, g0:g0 + 8, :], in_=PB[bk][:, :].rearrange("p (g n) -> p g n", n=64)), [PB[bk]], [dst])
                else:
                    A_("dve", lambda e, dst=dst, g0=g0, bk=bk: e.tensor_copy(out=dst[:, g0:g0 + 8, :], in_=PB[bk][:, :].rearrange("p (g n) -> p g n", n=64)), [PB[bk]], [dst])
        for i_ in range(4):
            S.dma("sp", scr[i_], TSt[i_][:].rearrange("p g n -> p (g n)"), reads=[TSt[i_].b], writes=[scrB[i_]])
        S.barrier()

        class BB:
            def __init__(self, name):
                self.b = Buf(name)

        ZcH = [BB("ZcH%d" % i) for i in range(2)]
        UQ = [BB("UQ%d" % i) for i in range(4)]
        M1Q = [BB("M1Q%d" % i) for i in range(8)]
        M2Q = [BB("M2Q%d" % i) for i in range(8)]
        HcQ = [BB("HcQ%d" % i) for i in range(8)]
        HTQ = [BB("HTQ%d" % i) for i in range(4)]
        YsC = [BB("YsC%d" % i) for i in range(4)]

        def s5_phase(st_i, is_s):
            ncn = 16 if is_s else 128
            S.dma("sp", FA[:, 13328:19472], scrW, reads=[scrWB], writes=[Wmov.b, WA.b])
            if is_s:
                A_("dve", lambda e: e.memset(Zc[0:16, :, 4:8, :], 0.0), [], ZcH)
                hs = hT.t[:, :, 0:64].rearrange("p k (q t) -> p k t q", t=4)
                ns = 4
            else:
                hs = hT.t[:].rearrange("p k (c s) -> p k s c", s=8)
                ns = 8
            for hf in range(2):
                wb_ = next_wblk()
                load_wblock(D["in_proj"][:, 1024 + hf * 256:1024 + (hf + 1) * 256], wb_)
                for s in range(ns):
                    bk = nb()
                    for K in range(8):
                        A_("pe", lambda e, s=s, K=K, bk=bk, wb_=wb_: e.matmul(PB[bk][0:ncn, 0:256], lhsT=hs[:, K, s, :], rhs=wb_[:, K, :], start=(K == 0), stop=(K == 7)), [hT, wb_], [PB[bk]])
                    slot = (3 - s) if is_s else (7 - s)
                    gs = slice(hf * 16, hf * 16 + 16)
                    if s % 2 == 0:
                        A_("act", lambda e, bk=bk, slot=slot, gs=gs: e.copy(out=Zc[0:ncn, gs, slot, :], in_=PB[bk][0:ncn, 0:256].rearrange("p (g k) -> p g k", k=16)), [PB[bk]], [ZcH[hf]])
                    else:
                        A_("dve", lambda e, bk=bk, slot=slot, gs=gs: e.tensor_copy(out=Zc[0:ncn, gs, slot, :], in_=PB[bk][0:ncn, 0:256].rearrange("p (g k) -> p g k", k=16)), [PB[bk]], [ZcH[hf]])
            for g0 in range(0, 32, 8):
                bk = nb()
                for gl in range(8):
                    g = g0 + gl
                    A_("pe", lambda e, g=g, gl=gl, bk=bk: e.transpose(PBb[bk][:, gl * 128:gl * 128 + ncn], Zc[0:ncn, g, :, :].rearrange("p k i -> p (k i)"), identb[0:ncn, 0:ncn]), [ZcH[g // 16], identb], [PB[bk]])
                A_("act", lambda e, g0=g0, bk=bk: e.copy(out=U[:, g0:g0 + 8, 0:ncn], in_=PBb[bk][:, :].rearrange("p (g c) -> p g c", c=128)[:, :, 0:ncn]), [PB[bk]], [UQ[g0 // 8]])
            if not is_s:
                for i_, tv in enumerate((TFr, TFi, TIr, TIi)):
                    S.dma("sp", tv[:].rearrange("p g n -> p (g n)"), scr[i_], reads=[scrB[i_]], writes=[tv.b])
                for g0 in range(0, 32, 4):
                    bk = nb()
                    for gl in range(4):
                        g = g0 + gl
                        A_("pe", lambda e, g=g, gl=gl, bk=bk: e.matmul(PB[bk][:, gl * 128:(gl + 1) * 128], lhsT=U[:, g, :], rhs=Wmov[:, g, 128:256], start=True, stop=True), [UQ[g // 8], Wmov], [PB[bk]])
                    Pv = lambda bk=bk: PB[bk][:, :].rearrange("p (g r n) -> p g r n", g=4, r=2)
                    sl = slice(g0, g0 + 4)
                    A_("dve", lambda e, Pv=Pv, sl=sl: e.tensor_tensor(out=M1[:, sl], in0=Pv(), in1=TIr[:, sl, :].unsqueeze(2).broadcast_to([128, 4, 2, 64]), op=ALU.mult), [PB[bk], TIr], [M1Q[g0 // 4]])
                    A_("dve", lambda e, Pv=Pv, sl=sl: e.scalar_tensor_tensor(out=M2[:, sl, 0, :], in0=Pv()[:, :, 1, :], scalar=-1.0, in1=TIi[:, sl, :], op0=ALU.mult, op1=ALU.mult), [PB[bk], TIi], [M2Q[g0 // 4]])
                    A_("dve", lambda e, Pv=Pv, sl=sl: e.tensor_tensor(out=M2[:, sl, 1, :], in0=Pv()[:, :, 0, :], in1=TIi[:, sl, :], op=ALU.mult), [PB[bk], TIi], [M2Q[g0 // 4]])
                for q in range(8):
                    bk = nb()
                    sl = slice(4 * q, 4 * q + 4)
                    fl = lambda X, sl=sl: X[:, sl].rearrange("p g r n -> p (g r n)")
                    A_("pe", lambda e, bk=bk, fl=fl: e.matmul(PB[bk][:, :], lhsT=triIb[:], rhs=fl(M1), start=True, stop=False), [triIb, M1Q[q]], [PB[bk]])
                    A_("pe", lambda e, bk=bk, fl=fl: e.matmul(PB[bk][:, :], lhsT=triIb[:], rhs=fl(M2), start=False, stop=(st_i == 0)), [triIb, M2Q[q]], [PB[bk]])
                    if st_i > 0:
                        for gl in range(4):
                            g = 4 * q + gl
                            A_("pe", lambda e, bk=bk, g=g, gl=gl: e.matmul(PB[bk][:, gl * 128:(gl + 1) * 128], lhsT=carryS[:, g:g + 1].broadcast_to([128, 128]), rhs=identb[:], start=False, stop=(gl == 3)), [carryS, identb], [PB[bk]])
                    Gv = lambda bk=bk: PB[bk][:, :].rearrange("p (g r n) -> p g r n", g=4, r=2)
                    A_("dve", lambda e, Gv=Gv, sl=sl: e.tensor_tensor(out=tmpA[:], in0=Gv(), in1=TFr[:, sl, :].unsqueeze(2).broadcast_to([128, 4, 2, 64]), op=ALU.mult), [PB[bk], TFr], [tmpA])
                    A_("dve", lambda e, Gv=Gv, sl=sl: e.scalar_tensor_tensor(out=tmpB[:, :, 0, :], in0=Gv()[:, :, 1, :], scalar=-1.0, in1=TFi[:, sl, :], op0=ALU.mult, op1=ALU.mult), [PB[bk], TFi], [tmpB])
                    A_("dve", lambda e, Gv=Gv, sl=sl: e.tensor_tensor(out=tmpB[:, :, 1, :], in0=Gv()[:, :, 0, :], in1=TFi[:, sl, :], op=ALU.mult), [PB[bk], TFi], [tmpB])
                    A_("dve", lambda e, sl=sl: e.tensor_tensor(out=Hcm[:, sl], in0=tmpA[:], in1=tmpB[:], op=ALU.add), [tmpA, tmpB], [HcQ[q]])
                A_("dve", lambda e: e.tensor_copy(out=HTx[:, :, 0], in_=carryS[:]), [carryS], HTQ)
                for g0 in range(0, 32, 8):
                    bk = nb()
                    for gl in range(8):
                        g = g0 + gl
                        A_("pe", lambda e, g=g, gl=gl, bk=bk: e.transpose(PBb[bk][:, gl * 128:(gl + 1) * 128], Hcm[:, g, :, :].rearrange("p r n -> p (r n)"), identb[:]), [HcQ[g // 4], identb], [PB[bk]])
                    A_("act", lambda e, g0=g0, bk=bk: e.copy(out=HTx[:, g0:g0 + 8, 1:129], in_=PBb[bk][:, :].rearrange("p (g c) -> p g c", c=128)), [PB[bk]], [HTQ[g0 // 8]])
                A_("dve", lambda e: e.tensor_copy(out=carryS[:], in_=HTx[:, :, 128]), HTQ, [carryS])
                hprev = lambda g: HTx[:, g, 0:128]
            else:
                S.dma("sp", s0[:, :, 0, :], D["s5r0"].rearrange("s (g n) -> s g n", n=64), writes=[s0.b])
                S.dma("sp", s0[:, :, 1, :], D["s5i0"].rearrange("s (g n) -> s g n", n=64), writes=[s0.b])
                for ti_, X in enumerate(P4):
                    A_("dve", lambda e, X=X: e.tensor_copy(out=tb16[:], in_=X[:].unsqueeze(2).broadcast_to([128, 32, 16])), [X], [tb16])
                    for g0 in range(0, 32, 8):
                        bk = nb()
                        for gl in range(8):
                            g = g0 + gl
                            A_("pe", lambda e, g=g, gl=gl, bk=bk: e.transpose(PB[bk][0:16, gl * 64:(gl + 1) * 64], tb16[0:64, g, :], ident[0:64, 0:64]), [tb16, ident], [PB[bk]])
                        A_("act", lambda e, ti_=ti_, g0=g0, bk=bk: e.copy(out=SL[ti_][:, g0:g0 + 8, :], in_=PB[bk][0:16, :].rearrange("p (g n) -> p g n", n=64)), [PB[bk]], [SL[ti_]])
                for g0 in range(0, 32, 4):
                    bk = nb()
                    sl = slice(g0, g0 + 4)
                    for gl in range(4):
                        g = g0 + gl
                        A_("pe", lambda e, g=g, gl=gl, bk=bk: e.matmul(PB[bk][0:16, gl * 128:(gl + 1) * 128], lhsT=U[:, g, 0:16], rhs=Wmov[:, g, 128:256], start=True, stop=True), [UQ[g // 8], Wmov], [PB[bk]])
                    Pv = lambda bk=bk: PB[bk][0:16, :].rearrange("p (g r n) -> p g r n", g=4, r=2)
                    mul_ = lambda o, a, b: (lambda e: e.tensor_tensor(out=o(), in0=a(), in1=b(), op=ALU.mult))
                    h0r = lambda sl=sl: s0[:, sl, 0, :]; h0i = lambda sl=sl: s0[:, sl, 1, :]
                    for (dr, di, Lr, Li, addP) in ((lambda sl=sl: Hcm[0:16, sl, 0, :], lambda sl=sl: Hcm[0:16, sl, 1, :], SL[2], SL[3], False),
                                                   (lambda sl=sl: s0[:, sl, 0, :], lambda sl=sl: s0[:, sl, 1, :], SL[0], SL[1], True)):
                        lr = lambda Lr=Lr, sl=sl: Lr[:, sl, :]; li = lambda Li=Li, sl=sl: Li[:, sl, :]
                        A_("dve", mul_(lambda: sc1[:, :, 0, :], h0r, lr), [s0, Lr], [sc1])
                        A_("dve", mul_(lambda: sc2[:, :, 0, :], h0i, li), [s0, Li], [sc2])
                        A_("dve", mul_(lambda: sc1[:, :, 1, :], h0i, lr), [s0, Lr], [sc1])
                        A_("dve", mul_(lambda: sc2[:, :, 1, :], h0r, li), [s0, Li], [sc2])
                        if not addP:
                            A_("dve", lambda e, dr=dr: e.tensor_tensor(out=dr(), in0=sc1[:, :, 0, :], in1=sc2[:, :, 0, :], op=ALU.subtract), [sc1, sc2], [HcQ[g0 // 4]])
                            A_("dve", lambda e, di=di: e.tensor_tensor(out=di(), in0=sc1[:, :, 1, :], in1=sc2[:, :, 1, :], op=ALU.add), [sc1, sc2], [HcQ[g0 // 4]])
                        else:
                            A_("dve", lambda e: e.tensor_tensor(out=sc1[:, :, 0, :], in0=sc1[:, :, 0, :], in1=sc2[:, :, 0, :], op=ALU.subtract), [sc1, sc2], [sc1])
                            A_("dve", lambda e: e.tensor_tensor(out=sc1[:, :, 1, :], in0=sc1[:, :, 1, :], in1=sc2[:, :, 1, :], op=ALU.add), [sc1, sc2], [sc1])
                            A_("dve", lambda e, sl=sl, Pv=Pv: e.tensor_tensor(out=s0[:, sl], in0=sc1[:], in1=Pv(), op=ALU.add), [sc1, PB[bk]], [s0])
                S.dma("sp", D["s5rs"].rearrange("s (g n) -> s g n", n=64), s0[:, :, 0, :], reads=[s0.b])
                S.dma("sp", D["s5is"].rearrange("s (g n) -> s g n", n=64), s0[:, :, 1, :], reads=[s0.b])
                for g0 in range(0, 32, 8):
                    bk = nb()
                    for gl in range(8):
                        g = g0 + gl
                        A_("pe", lambda e, g=g, gl=gl, bk=bk: e.transpose(PBb[bk][:, gl * 128:gl * 128 + 16], Hcm[0:16, g, :, :].rearrange("p r n -> p (r n)"), identb[0:16, 0:16]), [HcQ[g // 4], identb], [PB[bk]])
                    A_("act", lambda e, g0=g0, bk=bk: e.copy(out=HTx[:, g0:g0 + 8, 0:16], in_=PBb[bk][:, :].rearrange("p (g c) -> p g c", c=128)[:, :, 0:16]), [PB[bk]], [HTQ[g0 // 8]])
                hprev = lambda g: HTx[:, g, 0:16]
            for g0 in range(0, 32, 4):
                bk = nb()
                for gl in range(4):
                    g = g0 + gl
                    A_("pe", lambda e, g=g, gl=gl, bk=bk: e.matmul(PB[bk][0:ncn, gl * 128:(gl + 1) * 128], lhsT=U[:, g, 0:ncn], rhs=Wmov[:, g, 0:128], start=True, stop=False), [UQ[g // 8], Wmov], [PB[bk]])
                    A_("pe", lambda e, g=g, gl=gl, bk=bk: e.matmul(PB[bk][0:ncn, gl * 128:(gl + 1) * 128], lhsT=hprev(g), rhs=WA[:, g, :], start=False, stop=True), [HTQ[g // 8], WA], [PB[bk]])
                ct, gl0 = g0 // 8, g0 % 8
                A_("act", lambda e, bk=bk, ct=ct, gl0=gl0: e.activation(out=Ysb[0:ncn, ct, :, gl0:gl0 + 4, :], in_=PB[bk][0:ncn, :].rearrange("p (g j k) -> p j g k", g=4, j=8), func=AF.Gelu_apprx_tanh), [PB[bk]], [YsC[ct]])
            for ct in range(4):
                bk = nb()
                js = range(4, 8) if is_s else range(8)
                for j in js:
                    A_("pe", lambda e, ct=ct, j=j, bk=bk: e.transpose(PBb[bk][:, j * 128:j * 128 + ncn], Ysb[0:ncn, ct, j, :, :].rearrange("p g k -> p (g k)"), identb[0:ncn, 0:ncn]), [YsC[ct], identb], [PB[bk]])
                if not is_s:
                    A_("act", lambda e, ct=ct, bk=bk: e.copy(out=gyT[:, ct, :].rearrange("p (c j) -> p j c", j=8), in_=PBb[bk][:, :].rearrange("p (j c) -> p j c", c=128)), [PB[bk]], [gyT])
                else:
                    A_("act", lambda e, ct=ct, bk=bk: e.copy(out=gyT[:, ct, 0:64].rearrange("p (s t) -> p t s", t=4), in_=PBb[bk][:, 512:1024].rearrange("p (j c) -> p j c", c=128)[:, :, 0:16]), [PB[bk]], [gyT])

        def run_tile(ti, is_s, off, stages=("F1", "F2", "M", "E"), role="S"):
            if role == "B":
                RM, LNE = r1a, (rb2, rsq2, mean2, rstd2)
            elif role == "A":
                RM, LNE = r1, (rb2, rsq2, mean, rstd)
            else:
                RM, LNE = r1, (rb, rsq, mean, rstd)
            n = 64 if is_s else NT
            tok0 = 0 if is_s else ti * NT
            ydst = D["ys"] if is_s else D["yp"]
            nsub = 1 if is_s else NT // 128
            sw = 64 if is_s else 128
            hTt = lambda k: hT[:, k, off:off + n]
            axt = lambda k: axT[:, k, off:off + n]
            def layer_norm_stats(r1, rb, rsq, mean, rstd):
                for ft in range(8):
                    A_("act", lambda e, ft=ft: e.copy(out=rb[:, ft, 0:n], in_=r1[:, ft, 0:n]), [r1.c[ft]], [rb.c[ft]])
                    A_("act", lambda e, ft=ft: e.activation(out=rsq[:, ft, 0:n], in_=r1[:, ft, 0:n], func=AF.Square), [r1.c[ft]], [rsq.c[ft]])
                b1_, b2_ = nb(), nb()
                for ft in range(8):
                    A_("pe", lambda e, ft=ft: e.matmul(PB[b1_][:, 0:n], lhsT=onesb[:], rhs=rb[:, ft, 0:n], start=(ft == 0), stop=(ft == 7)), [onesb, rb.c[ft]], [PB[b1_]])
                for ft in range(8):
                    A_("pe", lambda e, ft=ft: e.matmul(PB[b2_][:, 0:n], lhsT=onesb[:], rhs=rsq[:, ft, 0:n], start=(ft == 0), stop=(ft == 7)), [onesb, rsq.c[ft]], [PB[b2_]])
                A_("act", lambda e: e.copy(out=mean[:, 0:n], in_=PB[b1_][:, 0:n]), [PB[b1_]], [mean])
                A_("dve", lambda e: e.tensor_tensor(out=rstd[:, 0:n], in0=mean[:, 0:n], in1=mean[:, 0:n], op=ALU.mult), [mean], [rstd])
                A_("dve", lambda e: e.tensor_tensor(out=rstd[:, 0:n], in0=PB[b2_][:, 0:n], in1=rstd[:, 0:n], op=ALU.subtract), [PB[b2_], rstd], [rstd])
                A_("dve", lambda e: e.tensor_scalar(out=rstd[:, 0:n], in0=rstd[:, 0:n], scalar1=0.0, scalar2=LN_EPS, op0=ALU.max, op1=ALU.add), [rstd], [rstd])
                A_("act", lambda e: e.activation(out=rstd[:, 0:n], in_=rstd[:, 0:n], func=AF.Sqrt), [rstd], [rstd])
                A_("dve", lambda e: e.reciprocal(out=rstd[:, 0:n], in_=rstd[:, 0:n]), [rstd], [rstd])
                for ft in range(8):
                    A_("dve", lambda e, ft=ft: e.tensor_tensor(out=r1[:, ft, 0:n], in0=r1[:, ft, 0:n], in1=mean[:, 0:n], op=ALU.subtract), [r1.c[ft], mean], [r1.c[ft]])
                    A_("dve", lambda e, ft=ft: e.tensor_tensor(out=r1[:, ft, 0:n], in0=r1[:, ft, 0:n], in1=rstd[:, 0:n], op=ALU.mult), [r1.c[ft], rstd], [r1.c[ft]])

            if "F1" in stages:
                for mt in range(8):
                    if mt % 2 == 0:
                        win = next_wblk()
                        load_wblock(D["in_proj"][:, (mt // 2) * 256:(mt // 2 + 1) * 256], win)
                    bk = nb()
                    for k in range(8):
                        A_("pe", lambda e, mt=mt, k=k, bk=bk, win=win: e.matmul(PB[bk][:, 0:n], lhsT=win[:, k, (mt % 2) * 128:(mt % 2 + 1) * 128], rhs=hTt(k), start=(k == 0), stop=(k == 7)), [win, hT], [PB[bk]])
                    ct = mt % 4
                    if mt < 4:
                        A_("act", lambda e, ct=ct, bk=bk: e.copy(out=gT[:, ct, 0:n], in_=PB[bk][:, 0:n]), [PB[bk]], [gT.c[ct]])
                    elif not is_s:
                        A_("act", lambda e, ct=ct, bk=bk: e.copy(out=xaT[:, ct, 3:3 + n], in_=PB[bk][:, 0:n]), [PB[bk]], [xaT.c[ct]])
                    else:
                        A_("act", lambda e, ct=ct, bk=bk: e.copy(out=xaS[:, ct, :, 3:7], in_=v3d(PB[bk][:, 0:64])), [PB[bk]], [xaS])
                if not is_s:
                    A_("dve", lambda e: e.tensor_copy(out=xaT[:, :, 0:3], in_=xhist[:]), [xhist], [xaT])
                for ct in range(4):
                    if not is_s:
                        xx = lambda k, ct=ct: xaT[:, ct, k:k + n]
                        o3 = lambda tt, ct=ct: tt[:, ct, 0:n]
                        xab = xaT.c[ct]
                    else:
                        xx = lambda k, ct=ct: xaS[:, ct, :, k:k + 4]
                        o3 = lambda tt, ct=ct: v3d(tt[:, ct, 0:64])
                        xab = xaS
                    cw = lambda k, ct=ct: PT[:, cwc + k * 4 + ct: cwc + k * 4 + ct + 1]
                    A_("dve", lambda e, ct=ct, xx=xx, o3=o3, cw=cw: e.tensor_scalar(out=o3(t1), in0=xx(0), scalar1=cw(0), scalar2=PT[:, cbc + ct:cbc + ct + 1], op0=ALU.mult, op1=ALU.add), [xab, PT], [t1.c[ct]])
                    for k in (1, 2, 3):
                        A_("dve", lambda e, k=k, xx=xx, o3=o3, cw=cw: e.scalar_tensor_tensor(out=o3(t1), in0=xx(k), scalar=cw(k), in1=o3(t1), op0=ALU.mult, op1=ALU.add), [xab, PT, t1.c[ct]], [t1.c[ct]])
                    A_("act", lambda e, ct=ct: e.copy(out=xcb[:, ct, 0:n], in_=t1[:, ct, 0:n]), [t1.c[ct]], [xcb.c[ct]])
                for ct in range(4):
                    for wi in range(2):
                        bk = nb()
                        A_("pe", lambda e, ct=ct, wi=wi, bk=bk: e.matmul(PB[bk][:, 0:n], lhsT=rgW[:, ct, wi, :], rhs=xcb[:, ct, 0:n], start=True, stop=True), [rgW, xcb.c[ct]], [PB[bk]])
                        dst = t2 if wi == 0 else t3
                        dstb = dst.c[ct]
                        bcol = (brc if wi == 0 else bic) + ct
                        A_("act", lambda e, ct=ct, bk=bk, dst=dst, bcol=bcol: e.activation(out=dst[:, ct, 0:n], in_=PB[bk][:, 0:n], func=AF.Sigmoid, bias=PT[:, bcol:bcol + 1], scale=1.0), [PB[bk], PT], [dstb])
                for ct in range(4):
                    A_("act", lambda e, ct=ct: e.activation(out=t4[:, ct, 0:n], in_=t2[:, ct, 0:n], func=AF.Exp, scale=cp1[:, ct:ct + 1]), [t2.c[ct], cp1], [t4.c[ct]])
                    A_("act", lambda e, ct=ct: e.activation(out=t2[:, ct, 0:n], in_=t2[:, ct, 0:n], func=AF.Exp, scale=cp2[:, ct:ct + 1]), [t2.c[ct], cp2], [t2.c[ct]])
                for ct in range(4):
                    A_("act", lambda e, ct=ct: e.activation(out=t2[:, ct, 0:n], in_=t2[:, ct, 0:n], func=AF.Sqrt, scale=-1.0, bias=1.0), [t2.c[ct]], [t2.c[ct]])
                if (not is_s) and ti == 0:
                    A_("dve", lambda e: e.memset(t2[:, :, 0:1], 1.0), [t2], [t2])
                for ct in range(4):
                    A_("dve", lambda e, ct=ct: e.tensor_tensor(out=t3[:, ct, 0:n], in0=t3[:, ct, 0:n], in1=t1[:, ct, 0:n], op=ALU.mult), [t3.c[ct], t1.c[ct]], [t3.c[ct]])
                    A_("dve", lambda e, ct=ct: e.tensor_tensor(out=t3[:, ct, 0:n], in0=t3[:, ct, 0:n], in1=t2[:, ct, 0:n], op=ALU.mult), [t3.c[ct], t2.c[ct]], [t3.c[ct]])
                if is_s:
                    for ct in range(4):
                        a0 = lambda ct=ct: t4[:, ct, 0:64].rearrange("p (s t) -> p s t", t=4)[:, :, 0]
                        b0 = lambda ct=ct: t3[:, ct, 0:64].rearrange("p (s t) -> p s t", t=4)[:, :, 0]
                        A_("dve", lambda e, ct=ct, a0=a0: e.tensor_tensor(out=a0(), in0=a0(), in1=h0T[:, ct, :], op=ALU.mult), [t4.c[ct], h0T], [t4.c[ct]])
                        A_("dve", lambda e, ct=ct, a0=a0, b0=b0: e.tensor_tensor(out=b0(), in0=b0(), in1=a0(), op=ALU.add), [t4.c[ct], t3.c[ct]], [t3.c[ct]])
                        A_("dve", lambda e, ct=ct, a0=a0: e.memset(a0(), 0.0), [t4.c[ct]], [t4.c[ct]])
                for ct in range(4):
                    init = 0.0 if is_s else hstate[:, ct:ct + 1]
                    A_("dve", lambda e, ct=ct, init=init: e.tensor_tensor_scan(out=t1[:, ct, 0:n], data0=t4[:, ct, 0:n], data1=t3[:, ct, 0:n], initial=init, op0=ALU.mult, op1=ALU.add), [t4.c[ct], t3.c[ct], hstate], [t1.c[ct]])
                    if not is_s:
                        A_("dve", lambda e, ct=ct: e.tensor_copy(out=hstate[:, ct:ct + 1], in_=t1[:, ct, n - 1:n]), [t1.c[ct]], [hstate])
                        A_("dve", lambda e, ct=ct: e.tensor_copy(out=xhist[:, ct, :], in_=xaT[:, ct, n:n + 3]), [xaT.c[ct]], [xhist])
                    A_("act", lambda e, ct=ct: e.activation(out=t2[:, ct, 0:n], in_=gT[:, ct, 0:n], func=AF.Gelu_apprx_tanh), [gT.c[ct]], [t2.c[ct]])
            if "F2" in stages or "F2a" in stages:
                for ct in range(4):
                    A_("dve", lambda e, ct=ct: e.tensor_tensor(out=ycat[:, ct, 0:n], in0=t2[:, ct, 0:n], in1=t1[:, ct, 0:n], op=ALU.mult), [t2.c[ct], t1.c[ct]], [ycat.c[ct]])
                if is_s:
                    bk = nb()
                    for ct in range(4):
                        A_("pe", lambda e, ct=ct, bk=bk: e.transpose(PB[bk][0:16, ct * 128:(ct + 1) * 128], t1[:, ct, 0:64].rearrange("p (s t) -> p s t", t=4)[:, :, 3], ident[:]), [t1, ident], [PB[bk]])
                    A_("dve", lambda e, bk=bk: e.tensor_copy(out=sm[0:16, :], in_=PB[bk][0:16, :]), [PB[bk]], [sm])
                    S.dma("sp", D["hs"], sm[0:16, :], reads=[sm.b])
                    bk = nb()
                    for ct in range(4):
                        A_("act", lambda e, ct=ct: e.copy(out=t2[:, ct, 0:48].rearrange("p (s k) -> p s k", k=3), in_=xaS[:, ct, :, 4:7]), [xaS], [t2])
                        A_("pe", lambda e, ct=ct, bk=bk: e.transpose(PB[bk][0:48, ct * 128:(ct + 1) * 128], t2[:, ct, 0:48], ident[:]), [t2, ident], [PB[bk]])
                    A_("dve", lambda e, bk=bk: e.tensor_copy(out=sm[:, :], in_=PB[bk][0:48, :]), [PB[bk]], [sm])
                    S.dma("sp", D["convs"], sm[:, :], reads=[sm.b])
                elif ti == NPT - 1:
                    bk = nb()
                    A_("pe", lambda e, bk=bk: e.transpose(PB[bk][0:4, 0:128], hstate[:, 0:4], ident[:]), [hstate, ident], [PB[bk]])
                    A_("dve", lambda e, bk=bk: e.tensor_copy(out=sm[0:4, 0:128], in_=PB[bk][0:4, 0:128]), [PB[bk]], [sm])
                    S.dma("sp", D["hp"], sm[0:4, 0:128], reads=[sm.b])
                    A_("act", lambda e: e.copy(out=t2[:, 0, 0:12].rearrange("p (k c) -> p k c", c=4), in_=xhist[:].rearrange("p c k -> p k c")), [xhist], [t2])
                    bk = nb()
                    A_("pe", lambda e, bk=bk: e.transpose(PB[bk][0:12, 0:128], t2[:, 0, 0:12], ident[:]), [t2, ident], [PB[bk]])
                    A_("dve", lambda e, bk=bk: e.tensor_copy(out=sm[32:44, 0:128], in_=PB[bk][0:12, 0:128]), [PB[bk]], [sm])
                    S.dma("sp", D["convp"], sm[32:44, 0:128], reads=[sm.b])
                for mt in range(4):
                    bk = nb()
                    for k in range(4):
                        A_("pe", lambda e, mt=mt, k=k, bk=bk: e.matmul(PB[bk][:, 0:n], lhsT=gluW[:, k, mt * 128:(mt + 1) * 128], rhs=gyT[:, k, off:off + n], start=(k == 0), stop=(k == 3)), [gluW, gyT], [PB[bk]])
                    A_("act", lambda e, mt=mt, bk=bk: e.activation(out=t3[:, mt, 0:n], in_=PB[bk][:, 0:n], func=AF.Sigmoid, bias=PT[:, gbc + mt:gbc + mt + 1], scale=1.0), [PB[bk], PT], [t3.c[mt]])
                    A_("dve", lambda e, mt=mt: e.tensor_tensor(out=ycat[:, 4 + mt, 0:n], in0=t3[:, mt, 0:n], in1=gyT[:, mt, off:off + n], op=ALU.mult), [t3.c[mt], gyT], [ycat.c[4 + mt]])
                for cbk in range(4):
                    w = next_wblk()
                    load_wblock(D["out_proj"][:, cbk * 256:(cbk + 1) * 256], w)
                    for q in range(2):
                        ft = cbk * 2 + q
                        bk = nb()
                        for k in range(8):
                            A_("pe", lambda e, w=w, q=q, k=k, bk=bk: e.matmul(PB[bk][:, 0:n], lhsT=w[:, k, q * 128:(q + 1) * 128], rhs=ycat[:, k, 0:n], start=(k == 0), stop=(k == 7)), [w, ycat.c[k]], [PB[bk]])
                        if not is_s:
                            A_("dve", lambda e, ft=ft, bk=bk: e.scalar_tensor_tensor(out=r1a[:, ft, 0:n], in0=PB[bk][:, 0:n], scalar=G1(ft)[:, 0:1], in1=axt(ft), op0=ALU.mult, op1=ALU.add), [PB[bk], modT, axT], [r1a.c[ft]])
                        else:
                            A_("dve", lambda e, ft=ft, bk=bk: e.tensor_tensor(out=v3d(r1a[:, ft, 0:64]), in0=v3d(PB[bk][:, 0:64]), in1=bc(G1(ft)), op=ALU.mult), [PB[bk], modT], [r1a.c[ft]])
                            A_("dve", lambda e, ft=ft: e.tensor_tensor(out=r1a[:, ft, 0:64], in0=r1a[:, ft, 0:64], in1=axt(ft), op=ALU.add), [r1a.c[ft], axT], [r1a.c[ft]])

                layer_norm_stats(r1a, rb2, rsq2, mean2, rstd2)
            if "F2" in stages or "F2b" in stages:
                for ft in range(8):
                    if not is_s:
                        A_("act", lambda e, ft=ft: e.activation(out=h2T[:, ft, 0:n], in_=r1a[:, ft, 0:n], func=AF.Identity, scale=GG2[:, ft, 0:1], bias=BB2[:, ft, 0:1]), [r1a.c[ft], GG2, BB2], [h2T.c[ft]])
                        A_("act", lambda e, ft=ft: e.activation(out=axt(ft), in_=r1a[:, ft, 0:n], func=AF.Identity, scale=AG1[:, ft:ft + 1], bias=AB1[:, ft, 0:1]), [r1a.c[ft], AG1, AB1], [axT])
                    else:
                        A_("dve", lambda e, ft=ft: e.tensor_tensor(out=v3d(t1[:, 0, 0:64]), in0=v3d(r1a[:, ft, 0:64]), in1=bc(GG2[:, ft, :]), op=ALU.mult), [r1a.c[ft], GG2], [t1])
                        A_("dve", lambda e, ft=ft: e.tensor_tensor(out=v3d(h2T[:, ft, 0:64]), in0=v3d(t1[:, 0, 0:64]), in1=bc(BB2[:, ft, :]), op=ALU.add), [t1, BB2], [h2T.c[ft]])
                        A_("dve", lambda e, ft=ft: e.tensor_scalar(out=t1[:, 1, 0:64], in0=r1a[:, ft, 0:64], scalar1=AG1[:, ft:ft + 1], scalar2=None, op0=ALU.mult), [r1a.c[ft], AG1], [t1])
                        A_("dve", lambda e, ft=ft: e.tensor_tensor(out=v3d(axt(ft)), in0=v3d(t1[:, 1, 0:64]), in1=bc(AB1[:, ft, :]), op=ALU.add), [t1, AB1], [axT])
            if "M" in stages:
                for hk in range(2):
                    for blk in range(8):
                        w1 = next_wblk()
                        load_wblock(D["mlp_w1"][:, (hk * 8 + blk) * 256:(hk * 8 + blk + 1) * 256], w1)
                        for q in range(2):
                            bk = nb()
                            for k in range(8):
                                A_("pe", lambda e, w1=w1, q=q, k=k, bk=bk: e.matmul(PB[bk][:, 0:n], lhsT=w1[:, k, q * 128:(q + 1) * 128], rhs=h2T[:, k, 0:n], start=(k == 0), stop=(k == 7)), [w1, h2T.c[k]], [PB[bk]])
                            fcol = mb1c + (hk * 8 + blk) * 2 + q
                            A_("act", lambda e, q=q, bk=bk, fcol=fcol: e.activation(out=f1a[:, q, 0:n], in_=PB[bk][:, 0:n], func=AF.Relu, bias=PT[:, fcol:fcol + 1], scale=1.0), [PB[bk], PT], [f1a])
                            A_("dve", lambda e, q=q, blk=blk: e.tensor_tensor(out=f1b[:, blk * 2 + q, 0:n], in0=f1a[:, q, 0:n], in1=f1a[:, q, 0:n], op=ALU.mult), [f1a], [f1b.c[blk * 2 + q]])
                    for m in range(8):
                        bk = nb()
                        w2 = next_wblk()
                        w2v = w2[:].rearrange("p a b -> p (a b)").rearrange("p (k c) -> p k c", c=128)
                        S.dma("pool", w2v, D["mlp_w2"][hk * 2048:(hk + 1) * 2048, m * 128:(m + 1) * 128].rearrange("(k p) c -> p k c", p=128), writes=[w2.b])
                        for k in range(16):
                            A_("pe", lambda e, w2v=w2v, k=k, bk=bk, w2=w2: e.matmul(PB[bk][:, 0:n], lhsT=w2v[:, k, :], rhs=f1b[:, k, 0:n], start=(k == 0), stop=(k == 15)), [w2, f1b.c[k]], [PB[bk]])
                        if hk == 0:
                            A_("act", lambda e, m=m, bk=bk: e.copy(out=RM[:, m, 0:n], in_=PB[bk][:, 0:n]), [PB[bk]], [RM.c[m]])
                        else:
                            A_("dve", lambda e, m=m, bk=bk: e.tensor_tensor(out=RM[:, m, 0:n], in0=PB[bk][:, 0:n], in1=RM[:, m, 0:n], op=ALU.add), [PB[bk], RM.c[m]], [RM.c[m]])
                            if not is_s:
                                A_("dve", lambda e, m=m: e.scalar_tensor_tensor(out=RM[:, m, 0:n], in0=RM[:, m, 0:n], scalar=G2(m)[:, 0:1], in1=axt(m), op0=ALU.mult, op1=ALU.add), [RM.c[m], modT, axT], [RM.c[m]])
                            else:
                                A_("dve", lambda e, m=m: e.tensor_tensor(out=v3d(RM[:, m, 0:64]), in0=v3d(RM[:, m, 0:64]), in1=bc(G2(m)), op=ALU.mult), [RM.c[m], modT], [RM.c[m]])
                                A_("dve", lambda e, m=m: e.tensor_tensor(out=RM[:, m, 0:64], in0=RM[:, m, 0:64], in1=axt(m), op=ALU.add), [RM.c[m], axT], [RM.c[m]])
            if "E" in stages:
                layer_norm_stats(RM, *LNE)
                for ft in range(8):
                    A_("act", lambda e, ft=ft: e.activation(out=RM[:, ft, 0:n], in_=RM[:, ft, 0:n], func=AF.Identity, scale=PT[:, l2g + ft:l2g + ft + 1], bias=PT[:, l2b + ft:l2b + ft + 1]), [RM.c[ft], PT], [RM.c[ft]])
                for sb in range(nsub):
                    for ft in range(8):
                        bk = nb()
                        A_("pe", lambda e, ft=ft, sb=sb, bk=bk: e.transpose(PB[bk][0:sw, 0:128], RM[:, ft, sb * 128:sb * 128 + sw], ident[:]), [RM.c[ft], ident], [PB[bk]])
                        if ft % 2 == 0:
                            A_("act", lambda e, ft=ft, bk=bk: e.copy(out=yout[0:sw, ft * 128:(ft + 1) * 128], in_=PB[bk][0:sw, 0:128]), [PB[bk]], [yout])
                        else:
                            A_("dve", lambda e, ft=ft, bk=bk: e.tensor_copy(out=yout[0:sw, ft * 128:(ft + 1) * 128], in_=PB[bk][0:sw, 0:128]), [PB[bk]], [yout])
                    S.dma("sp", ydst[tok0 + sb * 128: tok0 + sb * 128 + sw, :], yout[0:sw, :], reads=[yout.b])

        def sample_tile():
            S.dma("sp", sm[0:16, :], D["h0"], writes=[sm.b])
            bk = nb()
            for ct in range(4):
                A_("pe", lambda e, ct=ct, bk=bk: e.transpose(PB[bk][:, ct * 16:(ct + 1) * 16], sm[0:16, ct * 128:(ct + 1) * 128], ident[0:16, 0:16]), [sm, ident], [PB[bk]])
            A_("dve", lambda e, bk=bk: e.tensor_copy(out=h0T[:].rearrange("p c s -> p (c s)"), in_=PB[bk][:, 0:64]), [PB[bk]], [h0T])
            S.barrier()
            s5_phase(0, True)
            S.barrier()
            S.dma("sp", sm[:, :], D["conv0"].rearrange("s k c -> (s k) c"), writes=[sm.b])
            bk = nb()
            for ct in range(4):
                A_("pe", lambda e, ct=ct, bk=bk: e.transpose(PB[bk][:, ct * 48:(ct + 1) * 48], sm[0:48, ct * 128:(ct + 1) * 128], ident[0:48, 0:48]), [sm, ident], [PB[bk]])
            A_("dve", lambda e, bk=bk: e.tensor_copy(out=xaS[:, :, :, 0:3], in_=PB[bk][:, 0:192].rearrange("p (c s k) -> p c s k", c=4, k=3)), [PB[bk]], [xaS])
            run_tile(0, True, 0)
            S.barrier()

        nsup = CFG["nsuper"]
        for st_i in range(nsup):
            S.barrier()
            s5_phase(st_i, False)
            S.barrier()
            tA, tB = st_i * 2, st_i * 2 + 1
            run_tile(tA, False, 0, stages=("F1", "F2"), role="A")
            cur_pool[0] = 0
            capA = S.capture(lambda: run_tile(tA, False, 0, stages=("M",), role="A"))
            cur_pool[0] = 1
            capB = S.capture(lambda: run_tile(tB, False, NT, stages=("F1", "F2a"), role="B"))
            cur_pool[0] = None
            S.interleave(capA, capB, frac=CFG.get("frac1", 1.0))
            run_tile(tB, False, NT, stages=("F2b",), role="B")
            cur_pool[0] = 0
            capA = S.capture(lambda: run_tile(tB, False, NT, stages=("M",), role="B"))
            cur_pool[0] = 1
            capB = S.capture(lambda: run_tile(tA, False, 0, stages=("E",), role="A"))
            cur_pool[0] = None
            S.interleave(capA, capB)
            cur_pool[0] = 0
            capE = S.capture(lambda: run_tile(tB, False, NT, stages=("E",), role="B"))
            cur_pool[0] = 1
            if st_i + 1 < nsup:
                capX = S.capture(lambda: load_x((st_i + 1) * ST, ST, False))
            elif CFG["sample"]:
                capX = S.capture(lambda: load_x(0, 64, True))
            else:
                capX = []
            cur_pool[0] = None
            S.interleave(capE, capX)
        if CFG["sample"]:
            sample_tile()
        S.barrier()
        A_("act", lambda e: e.copy(out=t1[:, 0, 0:32], in_=carryS[:]), [carryS], [t1])
        bk = nb()
        A_("pe", lambda e, bk=bk: e.transpose(PB[bk][0:32, 0:128], t1[:, 0, 0:32], ident[:]), [t1, ident], [PB[bk]])
        A_("dve", lambda e, bk=bk: e.tensor_copy(out=sm[0:32, 0:128], in_=PB[bk][0:32, 0:128]), [PB[bk]], [sm])
        S.dma("sp", D["s5rp"], sm[0:32, 0:64], reads=[sm.b])
        S.dma("sp", D["s5ip"], sm[0:32, 64:128], reads=[sm.b])

        blk_ = st.enter_context(nc.Block())
        S.emit(blk_)
    return nc


_NC_CACHE = {}


def kernel(**inp):
    f = lambda a: np.ascontiguousarray(np.asarray(a, dtype=np.float32))
    n_cores = 8
    ident = np.eye(128, dtype=np.float32)
    ii = np.arange(128)
    trii = (ii[:, None] <= ii[None, :]).astype(np.float32)
    kk, kin = ii // 16, ii % 16
    maskt = ((kk[None, :] + kk[:, None]) >= 7).astype(np.float32)
    diagsel = (((kk[None, :] + kk[:, None]) == 7) & (kin[:, None] == kin[None, :])).astype(np.float32)
    shared = {
        "ada_w": f(inp["ada_w"][0]), "ada_b": f(inp["ada_b"][0]).reshape(48, 128), "in_proj": f(inp["in_proj"][0]),
        "conv_w": f(inp["conv_w"][0]).reshape(16, 128), "conv_b": f(inp["conv_b"][0]).reshape(4, 128),
        "rg_wr": f(inp["rg_wr"][0]), "rg_br": f(inp["rg_br"][0]).reshape(4, 128), "rg_wi": f(inp["rg_wi"][0]),
        "rg_bi": f(inp["rg_bi"][0]).reshape(4, 128), "rg_lam": f(inp["rg_lam"][0]).reshape(4, 128),
        "s5_a_re": f(inp["s5_a_re"][0]), "s5_a_im": f(inp["s5_a_im"][0]), "s5_log_dt": f(inp["s5_log_dt"][0]).reshape(1, 32),
        "s5_b_re": f(inp["s5_b_re"][0]), "s5_b_im": f(inp["s5_b_im"][0]),
        "s5_c_re": f(inp["s5_c_re"][0]).reshape(512, 64), "s5_c_im": f(inp["s5_c_im"][0]).reshape(512, 64),
        "s5_d": f(inp["s5_d"][0]).reshape(32, 16), "glu_w": f(inp["glu_w"][0]), "glu_b": f(inp["glu_b"][0]).reshape(4, 128),
        "out_proj": f(inp["out_proj"][0]), "ln1_g": f(inp["ln1_g"][0]).reshape(8, 128), "ln1_b": f(inp["ln1_b"][0]).reshape(8, 128),
        "mlp_w1": f(inp["mlp_w1"][0]), "mlp_b1": f(inp["mlp_b1"][0]).reshape(32, 128), "mlp_w2": f(inp["mlp_w2"][0]),
        "mlp_b2": f(inp["mlp_b2"][0]).reshape(8, 128), "ln2_g": f(inp["ln2_g"][0]).reshape(8, 128), "ln2_b": f(inp["ln2_b"][0]).reshape(8, 128),
        "ident": ident, "trii": trii, "maskt": maskt, "diagsel": diagsel,
    }
    xp = f(inp["x_prompt"]); xs = f(inp["x_sample"])
    in_maps = []
    for c in range(n_cores):
        sl = slice(16 * c, 16 * c + 16)
        m = dict(shared)
        m["xp"] = xp[c]
        m["xs"] = xs[sl].reshape(64, 1024)
        m["cvec"] = np.ascontiguousarray(np.concatenate([f(inp["c_prompt"])[c:c + 1], f(inp["c_sample"])[sl]], axis=0))
        m["conv0"] = f(inp["state_conv"][0][sl])
        m["h0"] = f(inp["state_rglru_h"][0][sl])
        m["s5r0"] = f(inp["state_s5_re"][0][sl]).reshape(16, 2048)
        m["s5i0"] = f(inp["state_s5_im"][0][sl]).reshape(16, 2048)
        in_maps.append(m)
    if "nc" not in _NC_CACHE:
        _NC_CACHE["nc"] = build_nc()
    nc = _NC_CACHE["nc"]
    res = run_bass_kernel_spmd(nc, in_maps, core_ids=list(range(n_cores)))
    R = res.results
    cat = lambda k: np.stack([np.asarray(R[c][k], dtype=np.float32) for c in range(n_cores)])
    yp = cat("yp")
    ys = cat("ys").reshape(128, 4, 1024)
    convp = cat("convp").reshape(8, 3, 4, 128).reshape(1, 8, 3, 512)
    hp = cat("hp").reshape(1, 8, 512)
    s5rp = cat("s5rp").reshape(1, 8, 32, 64)
    s5ip = cat("s5ip").reshape(1, 8, 32, 64)
    convs = cat("convs").reshape(1, 128, 3, 512)
    hs = cat("hs").reshape(1, 128, 512)
    s5rs = cat("s5rs").reshape(1, 128, 32, 64)
    s5is = cat("s5is").reshape(1, 128, 32, 64)
    return (yp, ys, convp, hp, s5rp, s5ip, convs, hs, s5rs, s5is)
```
